# Optimizing a Trainium2 kernel written in Bass

```python
import math
import jax, jax.numpy as jnp
from jax import lax
import numpy as np

D_MODEL = 1024
BATCH = 8
SEQ = 2048
DEPTH = 4

N_A = DEPTH // 2
N_B = DEPTH - N_A
PLE_DIM = 256
N_HEADS = 16
HEAD_DIM = 64
ROPE_DIM = 32
QK_DIM = HEAD_DIM + ROPE_DIM
Q_LORA = 384
KV_LORA = 256
MIX_WIDTH = N_HEADS * HEAD_DIM
MLA_IN = Q_LORA + KV_LORA + ROPE_DIM + MIX_WIDTH
SB_IN = 2 * MIX_WIDTH
ROPE_THETA = 10000.0
Q_BLOCK = 128
EPS = 1e-6

kernel_name = "yoco_mla_stickbreaking_hybrid"


def rmsnorm(x, g):
    xf = x.astype(jnp.float32)
    y = xf * lax.rsqrt(jnp.mean(xf * xf, axis=-1, keepdims=True) + EPS)
    return (y * g.astype(jnp.float32)).astype(x.dtype)


def rope(x, pos):
    half = ROPE_DIM // 2
    inv = 1.0 / (ROPE_THETA ** (jnp.arange(half, dtype=jnp.float32) / half))
    ang = pos.astype(jnp.float32)[..., None] * inv
    cos = jnp.cos(ang)[:, :, None, :]
    sin = jnp.sin(ang)[:, :, None, :]
    x1 = x[..., :half].astype(jnp.float32)
    x2 = x[..., half:].astype(jnp.float32)
    out = jnp.concatenate([x1 * cos - x2 * sin, x2 * cos + x1 * sin], axis=-1)
    return out.astype(x.dtype)


def _blocks(q):
    B, H, S, d = q.shape
    nb = S // Q_BLOCK
    return q.reshape(B, H, nb, Q_BLOCK, d).transpose(2, 0, 1, 3, 4), nb


def _unblocks(o):
    nb, B, H, qb, d = o.shape
    return o.transpose(1, 2, 0, 3, 4).reshape(B, H, nb * qb, d)


def causal_softmax_attention(q, k, v):
    S = q.shape[2]
    qb, nb = _blocks(q)
    s_idx = jnp.arange(S)
    scale = QK_DIM ** -0.5

    def one(args):
        qi, bi = args
        t_idx = bi * Q_BLOCK + jnp.arange(Q_BLOCK)
        logits = jnp.einsum('bhqd,bhkd->bhqk', qi, k,
                            preferred_element_type=jnp.float32) * scale
        mask = s_idx[None, :] <= t_idx[:, None]
        w = jax.nn.softmax(jnp.where(mask, logits, -jnp.inf), axis=-1)
        return jnp.einsum('bhqk,bhkd->bhqd', w.astype(v.dtype), v)

    return _unblocks(lax.map(one, (qb, jnp.arange(nb))))


def stick_breaking_attention(q, k, v):
    S = q.shape[2]
    qb, nb = _blocks(q)
    s_idx = jnp.arange(S)
    scale = HEAD_DIM ** -0.5

    def one(args):
        qi, bi = args
        t_idx = bi * Q_BLOCK + jnp.arange(Q_BLOCK)
        z = jnp.einsum('bhqd,bhkd->bhqk', qi, k,
                       preferred_element_type=jnp.float32) * scale
        mask = s_idx[None, :] < t_idx[:, None]
        log_beta = jax.nn.log_sigmoid(z)
        log_one_minus = jnp.where(mask, jax.nn.log_sigmoid(-z), 0.0)
        tail = lax.cumsum(log_one_minus, axis=log_one_minus.ndim - 1,
                          reverse=True) - log_one_minus
        a = jnp.where(mask, jnp.exp(log_beta + tail), 0.0)
        return jnp.einsum('bhqk,bhkd->bhqd', a.astype(v.dtype), v)

    return _unblocks(lax.map(one, (qb, jnp.arange(nb))))


def mla_layer(x, positions, ln_g, w_in, q_norm_g, kv_norm_g, w_q_up, w_kv_up,
              q_head_g, k_head_g, w_out):
    B, S, _ = x.shape
    h = rmsnorm(x, ln_g)
    proj = h @ w_in
    c_q, c_kv, k_rope, gate = jnp.split(
        proj, [Q_LORA, Q_LORA + KV_LORA, Q_LORA + KV_LORA + ROPE_DIM], axis=-1)
    q = (rmsnorm(c_q, q_norm_g) @ w_q_up).reshape(B, S, N_HEADS, QK_DIM)
    kv = (rmsnorm(c_kv, kv_norm_g) @ w_kv_up).reshape(B, S, N_HEADS, 2 * HEAD_DIM)
    k_nope, v = kv[..., :HEAD_DIM], kv[..., HEAD_DIM:]
    k = jnp.concatenate(
        [k_nope, jnp.broadcast_to(k_rope[:, :, None, :], (B, S, N_HEADS, ROPE_DIM))], axis=-1)
    q = rmsnorm(q, q_head_g)
    k = rmsnorm(k, k_head_g)
    q = jnp.concatenate([q[..., :HEAD_DIM], rope(q[..., HEAD_DIM:], positions)], axis=-1)
    k = jnp.concatenate([k[..., :HEAD_DIM], rope(k[..., HEAD_DIM:], positions)], axis=-1)
    o = causal_softmax_attention(q.transpose(0, 2, 1, 3), k.transpose(0, 2, 1, 3),
                                 v.transpose(0, 2, 1, 3))
    o = o.transpose(0, 2, 1, 3).reshape(B, S, MIX_WIDTH) * jax.nn.silu(gate)
    return x + o @ w_out


def sb_layer(x, k_sh, v_sh, ln_g, w_in, w_out):
    B, S, _ = x.shape
    h = rmsnorm(x, ln_g)
    q, gate = jnp.split(h @ w_in, [MIX_WIDTH], axis=-1)
    q = q.reshape(B, S, N_HEADS, HEAD_DIM).transpose(0, 2, 1, 3)
    o = stick_breaking_attention(q, k_sh, v_sh)
    o = o.transpose(0, 2, 1, 3).reshape(B, S, MIX_WIDTH) * jax.nn.silu(gate)
    return x + o @ w_out


def setup_inputs(seed: int = 0) -> dict:
    key = jax.random.key(seed)
    ks = jax.random.split(key, 24)

    def w(k, shape, fan_in):
        return jax.random.normal(k, shape, jnp.float32) * (fan_in ** -0.5)

    def gain(k, shape):
        return 1.0 + 0.02 * jax.random.normal(k, shape, jnp.float32)

    x = jax.random.normal(ks[0], (BATCH, SEQ, D_MODEL), jnp.float32)
    p = jax.random.normal(ks[1], (DEPTH, BATCH, SEQ, PLE_DIM), jnp.float32)
    offs = jax.random.randint(ks[2], (BATCH, 1), 0, 1024, dtype=jnp.int32)
    positions = offs + jnp.arange(SEQ, dtype=jnp.int32)[None, :]
    return {
        "x": x,
        "p": p,
        "positions": positions,
        "mla_ln_g": gain(ks[3], (N_A, D_MODEL)),
        "mla_w_in": w(ks[4], (N_A, D_MODEL, MLA_IN), D_MODEL),
        "mla_q_norm_g": gain(ks[5], (N_A, Q_LORA)),
        "mla_kv_norm_g": gain(ks[6], (N_A, KV_LORA)),
        "mla_w_q_up": w(ks[7], (N_A, Q_LORA, N_HEADS * QK_DIM), Q_LORA),
        "mla_w_kv_up": w(ks[8], (N_A, KV_LORA, N_HEADS * 2 * HEAD_DIM), KV_LORA),
        "mla_q_head_g": gain(ks[9], (N_A, QK_DIM)),
        "mla_k_head_g": gain(ks[10], (N_A, QK_DIM)),
        "mla_w_out": w(ks[11], (N_A, MIX_WIDTH, D_MODEL), MIX_WIDTH),
        "kv_ln_g": gain(ks[12], (D_MODEL,)),
        "w_kv_shared": w(ks[13], (D_MODEL, 2 * MIX_WIDTH), D_MODEL),
        "sb_ln_g": gain(ks[14], (N_B, D_MODEL)),
        "sb_w_in": w(ks[15], (N_B, D_MODEL, SB_IN), D_MODEL),
        "sb_w_out": w(ks[16], (N_B, MIX_WIDTH, D_MODEL), MIX_WIDTH),
        "ple_w_proj": w(ks[17], (DEPTH, PLE_DIM, D_MODEL), PLE_DIM),
        "ple_w_gate": w(ks[18], (DEPTH, D_MODEL, D_MODEL), D_MODEL),
    }


def reference(x, p, positions, mla_ln_g, mla_w_in, mla_q_norm_g, mla_kv_norm_g,
              mla_w_q_up, mla_w_kv_up, mla_q_head_g, mla_k_head_g, mla_w_out,
              kv_ln_g, w_kv_shared, sb_ln_g, sb_w_in, sb_w_out,
              ple_w_proj, ple_w_gate):
    B, S, _ = x.shape
    k_sh = v_sh = None
    for i in range(DEPTH):
        if i < N_A:
            x = mla_layer(x, positions, mla_ln_g[i], mla_w_in[i], mla_q_norm_g[i],
                          mla_kv_norm_g[i], mla_w_q_up[i], mla_w_kv_up[i],
                          mla_q_head_g[i], mla_k_head_g[i], mla_w_out[i])
        else:
            j = i - N_A
            x = sb_layer(x, k_sh, v_sh, sb_ln_g[j], sb_w_in[j], sb_w_out[j])
        x = x + jax.nn.sigmoid(x @ ple_w_gate[i]) * (p[i] @ ple_w_proj[i])
        if i == N_A - 1:
            kv = rmsnorm(x, kv_ln_g) @ w_kv_shared
            k_sh = kv[..., :MIX_WIDTH].reshape(B, S, N_HEADS, HEAD_DIM).transpose(0, 2, 1, 3)
            v_sh = kv[..., MIX_WIDTH:].reshape(B, S, N_HEADS, HEAD_DIM).transpose(0, 2, 1, 3)
    return x
```

```python
import math
from contextlib import ExitStack

import numpy as np
import concourse.bass as bass
import concourse.mybir as mybir
from concourse.bass_utils import run_bass_kernel_spmd

F32 = mybir.dt.float32
BF16 = mybir.dt.bfloat16
I32 = mybir.dt.int32
AF = mybir.ActivationFunctionType
ALU = mybir.AluOpType
AX = mybir.AxisListType

S = 2048
D = 1024
NT = 16
NJ = 4
H = 16
EPS = 1e-6
NEG = -30000.0

_ESZ = {}


def esize(dt):
    if dt not in _ESZ:
        _ESZ[dt] = mybir.dt.size(dt)
    return _ESZ[dt]


def region(ap):
    t = ap.tensor
    es = esize(ap.dtype)
    dims = ap.ap
    off = ap.offset
    if type(t).__name__.startswith('DRam'):
        lo = hi = off
        for st, cnt in dims:
            d = st * (cnt - 1)
            if d < 0:
                lo += d
            else:
                hi += d
        return (t.name, 0, 1, lo * es, (hi + 1) * es)
    if t.name.startswith('ps'):
        return (t.name, 0, 128, 0, 2048)
    pstep, pcnt = dims[0]
    p0 = off // pstep
    c0 = off - p0 * pstep
    lo = hi = c0
    for st, cnt in dims[1:]:
        d = st * (cnt - 1)
        if d < 0:
            lo += d
        else:
            hi += d
    return (t.name, p0, p0 + pcnt, lo * es, (hi + 1) * es)


class Op:
    __slots__ = ('eng', 'fn', 'seq', 'waits', 'signal', 'dma', 'snap', 'sigcount')

    def __init__(self, eng, fn):
        self.eng = eng
        self.fn = fn
        self.waits = []
        self.signal = False
        self.dma = None
        self.snap = None


class Prog:
    ENGS = ('pe', 'act', 'dve', 'pool', 'sp')

    def __init__(self, nc, n_dma_sems=32):
        self.nc = nc
        self.ops = {e: [] for e in self.ENGS}
        self.track = {}
        self.seen = {e: {f: -1 for f in self.ENGS} for e in self.ENGS}
        self.seen_dma = {e: {} for e in self.ENGS}
        self.n_dma_sems = n_dma_sems
        self.dma_sem_val = [0] * n_dma_sems
        h = n_dma_sems // 2
        self.dma_pool = {'sp': list(range(0, h)), 'act': list(range(0, h)), 'pool': list(range(h, n_dma_sems))}
        self.dma_rr = {'sp': 0, 'act': 0, 'pool': 0}
        self.nops = 0

    def _deps(self, regs_r, regs_w, eng=None):
        deps = []
        for (key, p0, p1, b0, b1) in regs_r:
            lst = self.track.get(key)
            if lst:
                psum = key.startswith('ps')
                for ent in lst:
                    if ent[5] and ent[0] < p1 and p0 < ent[1] and ent[2] < b1 and b0 < ent[3]:
                        deps.append(ent[4])
                    elif psum and (not ent[5]) and ent[4].eng != eng:
                        deps.append(ent[4])
        for (key, p0, p1, b0, b1) in regs_w:
            lst = self.track.get(key)
            if lst:
                for ent in lst:
                    if ent[0] < p1 and p0 < ent[1] and ent[2] < b1 and b0 < ent[3]:
                        deps.append(ent[4])
        return deps

    def _record(self, op, regs_r, regs_w):
        for (key, p0, p1, b0, b1) in regs_w:
            lst = self.track.setdefault(key, [])
            lst[:] = [e for e in lst if not (p0 <= e[0] and e[1] <= p1 and b0 <= e[2] and e[3] <= b1)]
            lst.append([p0, p1, b0, b1, op, True])
        for (key, p0, p1, b0, b1) in regs_r:
            lst = self.track.setdefault(key, [])
            lst[:] = [e for e in lst if not ((not e[5]) and e[4].eng == op.eng and e[4].dma is None
                                             and op.dma is None
                                             and p0 <= e[0] and e[1] <= p1 and b0 <= e[2] and e[3] <= b1)]
            lst.append([p0, p1, b0, b1, op, False])

    def add(self, eng, fn, reads=(), writes=(), dma=False):
        op = Op(eng, fn)
        op.seq = len(self.ops[eng])
        regs_r = [region(a) for a in reads]
        regs_w = [region(a) for a in writes]
        deps = self._deps(regs_r, regs_w, eng)
        seen = self.seen[eng]
        seen_d = self.seen_dma[eng]
        need_e = {}
        need_d = {}
        for d in deps:
            if d.dma is not None:
                s, v = d.dma
                if seen_d.get(s, 0) < v and need_d.get(s, 0) < v:
                    need_d[s] = v
            else:
                if d.eng == 'pe' and eng == 'pe':
                    continue
                if d.seq > seen[d.eng]:
                    if d.eng not in need_e or need_e[d.eng].seq < d.seq:
                        need_e[d.eng] = d
        if dma:
            pool = self.dma_pool[eng]
            s = pool[self.dma_rr[eng] % len(pool)]
            self.dma_rr[eng] += 1
            prev = self.dma_sem_val[s]
            if prev > 0 and seen_d.get(s, 0) < prev and need_d.get(s, 0) < prev:
                need_d[s] = prev
            self.dma_sem_val[s] = prev + 16
            op.dma = (s, prev + 16)
        for f, d in need_e.items():
            d.signal = True
            op.waits.append(('e', f, d))
            seen[f] = max(seen[f], d.seq)
            if d.snap is not None:
                se, sd = d.snap
                for g, v in se.items():
                    if v > seen[g]:
                        seen[g] = v
                for g, v in sd.items():
                    if v > seen_d.get(g, 0):
                        seen_d[g] = v
        for s, v in need_d.items():
            op.waits.append(('d', s, v))
            seen_d[s] = max(seen_d.get(s, 0), v)
        if not dma:
            op.snap = (dict(seen), dict(seen_d))
        self.ops[eng].append(op)
        self._record(op, regs_r, regs_w)
        self.nops += 1
        return op

    def wait_all_dma(self, eng='sp'):
        op = Op(eng, None)
        op.seq = len(self.ops[eng])
        for s in range(self.n_dma_sems):
            if self.dma_sem_val[s] > 0:
                op.waits.append(('d', s, self.dma_sem_val[s]))
        self.ops[eng].append(op)

    def emit(self, stack):
        nc = self.nc
        esem = {e: stack.enter_context(nc.semaphore("s_" + e)) for e in self.ENGS}
        dsem = [stack.enter_context(nc.semaphore("d_%d" % i)) for i in range(self.n_dma_sems)]
        for e in self.ENGS:
            c = 0
            for op in self.ops[e]:
                if op.signal:
                    c += 1
                op.sigcount = c
        block = stack.enter_context(nc.Block())
        engmap = {'pe': block.tensor, 'act': block.scalar, 'dve': block.vector,
                  'pool': block.gpsimd, 'sp': block.sync}

        def make(e):
            ops = self.ops[e]

            def body(eng):
                for op in ops:
                    for w in op.waits:
                        if w[0] == 'e':
                            eng.wait_ge(esem[w[1]], w[2].sigcount)
                        else:
                            eng.wait_ge(dsem[w[1]], w[2])
                    if op.fn is None:
                        continue
                    ins = op.fn(eng)
                    if op.dma is not None:
                        ins.then_inc(dsem[op.dma[0]], 16)
                    elif op.signal:
                        ins.then_inc(esem[e], 1)
            return body

        for e in self.ENGS:
            if self.ops[e]:
                engmap[e](make(e))


R0 = 0
R1 = 32768
R2 = 65536
R3 = 98304
R4 = 122880
O_QT = [R0 + 0, R0 + 4096]
O_KT = [R0 + 8192, R0 + 12288]
O_VA = R0 + 16384
O_QN = R0 + 24576
O_CQB = R2
O_CKVB = R2 + 12288
O_WA = R2 + 20480
O_WQ = R2 + 20480
O_PPT = R3
O_VSH = R2
O_WKV = R3
O_XSQ = R3
O_KROPE = R3 + 8192
O_QR = R3 + 10240
O_GAM = R3 + 14336
O_DEL = R3 + 15360
O_DELN = R3 + 16384
O_PT = [R3 + 17408, R3 + 18432, R3 + 19456]
O_SQT = O_PT
O_KRRAW = R3 + 20480
O_REC = R3 + 22528
O_QS = [R3 + 8192, R3 + 9216]
O_SGC = [R3 + 10240, R3 + 11264]
O_OG = [R3 + 12288, R3 + 13312]
O_SP = [R3 + 14336, R3 + 15360]
O_SPACC = [R3 + 16384, R3 + 17408]
O_A = [R3 + 18432, R3 + 19456]
O_E = [R3 + 20480, R3 + 22528]
O_RING = [R4, R4 + 2048, R4 + 4096]
O_TMP = [R4 + 6144, R4 + 8192]
O_XIN = R4 + 6144
O_RB = [R4 + 10240, R4 + 10240]
O_CONSTB = R4 + 12288
O_IDENTF = O_CONSTB + 1792
O_COS = O_IDENTF + 512
O_SIN = O_COS + 1024
O_GQ2 = O_SIN + 1024
O_GKN2 = O_GQ2 + 768
O_GKR = O_GKN2 + 512
O_STAT = O_GKR + 128
O_RECH = O_STAT + 384
O_RECL = O_RECH + 1024
O_GCOL = O_RECL + 1024
O_PST = O_GCOL + 64
O_POSF = O_PST + 512
O_INV = O_POSF + 128
O_ANG = O_INV + 64
ARENA_BYTES = O_ANG + 3072
N_CONSTB = 7
C_IDENT, C_NEGU, C_NEGONES, C_ONES, C_MASKM, C_MASKS, C_BSEL = range(7)


def build_program(n_layers=4, dbg=None, stage=99, layers=None):
    nc = bass.Bass("TRN2", target_bir_lowering=False)
    dt_in = lambda n, s, d=F32: nc.dram_tensor(n, s, d, kind="ExternalInput").ap()
    x_d = dt_in("x", [S, D])
    p_d = dt_in("p", [4, S, 256])
    pos_d = dt_in("pos", [128, NT], I32)
    cb_d = dt_in("constb", [128, N_CONSTB * 128])
    cf_d = dt_in("constf", [128, 128 + 16])
    mla_g_d = dt_in("mla_gcol", [2, 128, 13])
    mla_wa_d = dt_in("mla_wa", [2, 128, 8, 672])
    mla_wg_d = dt_in("mla_wgate", [2, 8, 128, 8, 128])
    mla_wq_d = dt_in("mla_wq", [2, 128, 3, 1536])
    mla_wkv_d = dt_in("mla_wkv", [2, 128, 2, 2048])
    mla_hg_d = dt_in("mla_hg", [2, 128, 352])
    mla_wo_d = dt_in("mla_wo", [2, 8, 128, 8, 128])
    kv_g_d = dt_in("kv_gcol", [128, 8])
    kv_wk_d = dt_in("kv_wk", [8, 128, 8, 128])
    kv_wv_d = dt_in("kv_wv", [8, 128, 8, 128])
    sb_g_d = dt_in("sb_gcol", [2, 128, 8])
    sb_wq_d = dt_in("sb_wq", [2, 8, 128, 8, 128])
    sb_wgt_d = dt_in("sb_wgate", [2, 8, 128, 8, 128])
    sb_wo_d = dt_in("sb_wo", [2, 8, 128, 1024])
    ple_wg_d = dt_in("ple_wg", [4, 8, 128, 8, 128])
    ple_wp_d = dt_in("ple_wp", [4, 8, 128, 2, 128])
    y_d = nc.dram_tensor("y", [S, D], F32, kind="ExternalOutput").ap()
    dbg_d = None
    if dbg is not None:
        dbg_d = nc.dram_tensor("dbg", list(dbg[1]), F32, kind="ExternalOutput").ap()

    st = ExitStack()
    XTt = st.enter_context(nc.sbuf_tensor("XT", [128, 8 * S], F32))
    AR = st.enter_context(nc.sbuf_tensor("AR", [128, ARENA_BYTES // 2], BF16))
    PS = [st.enter_context(nc.psum_tensor("ps%d" % i, [128, 512], F32)) for i in range(8)]
    PSB = [t.bitcast(BF16) for t in PS]
    P = Prog(nc)

    def av(off, n, dt=BF16):
        v = AR[:, off // 2: off // 2 + (n * esize(dt)) // 2]
        return v if dt == BF16 else v.bitcast(dt)

    XT = XTt[:].rearrange("p (k t) -> p k t", k=8)

    def mm(out, lhsT, rhs, start=True, stop=True, skip=False):
        kw = dict(start=start, stop=stop)
        if skip:
            kw['skip_group_check'] = True
        P.add('pe', lambda q: q.matmul(out, lhsT=lhsT, rhs=rhs, **kw),
              reads=[lhsT, rhs] + ([] if start else [out]), writes=[out])

    def tr(out, in_, ident):
        P.add('pe', lambda q: q.transpose(out, in_, ident), reads=[in_, ident], writes=[out])

    def act(out, in_, func, scale=1.0, bias=0.0, eng='act'):
        rd = [in_]
        if not isinstance(scale, (int, float)):
            rd.append(scale)
        if not isinstance(bias, (int, float)):
            rd.append(bias)
        P.add('act', lambda q: q.activation(out=out, in_=in_, func=func, scale=scale, bias=bias),
              reads=rd, writes=[out])

    def tt(out, in0, in1, op, eng='dve'):
        P.add(eng, lambda q: q.tensor_tensor(out=out, in0=in0, in1=in1, op=op), reads=[in0, in1], writes=[out])

    def ts(out, in0, s1, op0, s2=None, op1=None, eng='dve'):
        rd = [in0]
        if not isinstance(s1, (int, float)):
            rd.append(s1)
        if s2 is not None and not isinstance(s2, (int, float)):
            rd.append(s2)
        if op1 is None:
            P.add(eng, lambda q: q.tensor_scalar(out=out, in0=in0, scalar1=s1, scalar2=None, op0=op0),
                  reads=rd, writes=[out])
        else:
            P.add(eng, lambda q: q.tensor_scalar(out=out, in0=in0, scalar1=s1, scalar2=s2, op0=op0, op1=op1),
                  reads=rd, writes=[out])

    def stt(out, in0, scalar, in1, op0, op1):
        rd = [in0, in1]
        if not isinstance(scalar, (int, float)):
            rd.append(scalar)
        P.add('dve', lambda q: q.scalar_tensor_tensor(out=out, in0=in0, scalar=scalar, in1=in1, op0=op0, op1=op1),
              reads=rd, writes=[out])

    def cp(out, in_, eng='dve'):
        if eng == 'act':
            P.add('act', lambda q: q.copy(out=out, in_=in_), reads=[in_], writes=[out])
        else:
            P.add(eng, lambda q: q.tensor_copy(out=out, in_=in_), reads=[in_], writes=[out])

    def red(out, in_, eng='dve'):
        P.add(eng, lambda q: q.tensor_reduce(out=out, in_=in_, op=ALU.add, axis=AX.X), reads=[in_], writes=[out])

    def recip(out, in_):
        P.add('dve', lambda q: q.reciprocal(out=out, in_=in_), reads=[in_], writes=[out])

    def memset(ap, val, eng='dve'):
        P.add(eng, lambda q: q.memset(ap, val), reads=[], writes=[ap])

    def dma(out, in_, eng='sp'):
        P.add(eng, lambda q: q.dma_start(out=out, in_=in_), reads=[in_], writes=[out], dma=True)

    def rsqrt_to(out, in_, scale, tmp):
        act(tmp, in_, AF.Ln, scale=scale, bias=EPSC)
        act(out, tmp, AF.Exp, scale=-0.5)

    CB = av(O_CONSTB, N_CONSTB * 128).rearrange("p (c n) -> p c n", c=N_CONSTB)
    dma(av(O_CONSTB, N_CONSTB * 128), cb_d, eng='pool')
    IDENT = CB[:, C_IDENT, :]
    NEGU = CB[:, C_NEGU, :]
    NEGONES = CB[:, C_NEGONES, :]
    ONES = CB[:, C_ONES, :]
    MASKM = CB[:, C_MASKM, :]
    MASKS = CB[:, C_MASKS, :]
    BSEL = CB[:, C_BSEL, :]
    IDENTF = av(O_IDENTF, 128, F32)
    dma(IDENTF, cf_d[:, 0:128])
    INV = av(O_INV, 16, F32)
    dma(INV, cf_d[:, 128:144])
    STAT = av(O_STAT, 96, F32)
    EPSC = STAT[:, 95:96]
    memset(EPSC, EPS)
    GCOL = av(O_GCOL, 16, F32)
    psrot = [0]

    def bank(i=None):
        if i is None:
            psrot[0] = (psrot[0] + 1) % 2
            return 6 + psrot[0]
        return i

    TMP = [av(o, 512, F32) for o in O_TMP]
    RBt = [av(o, 512, F32) for o in O_RB]
    RING = [av(o, 1024).rearrange("p (k n) -> p k n", k=8) for o in O_RING]
    ring_i = [0]

    def ring_load(src, shape=None):
        i = ring_i[0] % 3
        ring_i[0] += 1
        k, n = src.shape[1], src.shape[2]
        dst = av(O_RING[i], k * n).rearrange("p (k n) -> p k n", k=k)
        dma(dst, src, eng='pool')
        return dst

    XB = av(R0, 8 * S).rearrange("p (k t) -> p k t", k=8)
    XSQ = av(O_XSQ, 8 * 512).rearrange("p (k t) -> p k t", k=8)

    XIN = av(O_XIN, 1024, F32)
    for t in range(NT):
        dma(XIN, x_d[t * 128:(t + 1) * 128, :])
        for half in range(2):
            b = 4 + half
            for q4 in range(4):
                kc = half * 4 + q4
                tr(PS[b][:, q4 * 128:(q4 + 1) * 128], XIN[:, kc * 128:(kc + 1) * 128], IDENTF)
            dst = XT[:, half * 4:(half + 1) * 4, t * 128:(t + 1) * 128]
            src = PS[b][:, :].rearrange("p (k t) -> p k t", k=4)
            if half == 0:
                cp(dst, src, eng='dve')
            else:
                cp(dst, src, eng='act')

    COS = av(O_COS, 256, F32).rearrange("p (t i) -> p t i", t=NT)
    SIN = av(O_SIN, 256, F32).rearrange("p (t i) -> p t i", t=NT)
    if True:
        POSI = av(O_POSF + 64, 16, I32)
        POSF = av(O_POSF, 16, F32)
        dma(POSI, pos_d)
        cp(POSF, POSI)
        ANG = av(O_ANG, 256, F32)
        A2 = av(O_ANG + 1024, 256, F32)
        A3 = av(O_ANG + 2048, 256, F32)
        KI = av(O_ANG + 2048, 256, I32)
        ANG3 = ANG.rearrange("p (t i) -> p t i", t=NT)
        tt(ANG3, POSF.unsqueeze(2).broadcast_to([128, NT, 16]), INV.unsqueeze(1).broadcast_to([128, NT, 16]), ALU.mult)
        TWO_PI = 2.0 * math.pi

        def reduce_to_pi(dst, src):
            ts(A2, src, 1.0 / TWO_PI, ALU.mult)
            cp(KI, A2)
            cp(A2, KI)
            stt(dst, A2, -TWO_PI, src, ALU.mult, ALU.add)
            ts(A2, dst, math.pi, ALU.is_gt, -TWO_PI, ALU.mult)
            tt(dst, dst, A2, ALU.add)
            ts(A2, dst, -math.pi, ALU.is_lt, TWO_PI, ALU.mult)
            tt(dst, dst, A2, ALU.add)

        SINr = av(O_SIN, 256, F32)
        COSr = av(O_COS, 256, F32)
        reduce_to_pi(SINr, ANG)
        ts(ANG, ANG, math.pi / 2.0, ALU.add)
        reduce_to_pi(COSr, ANG)
        act(SINr, SINr, AF.Sin)
        act(COSr, COSr, AF.Sin)

    def prep_norm(gcols):
        for j in range(NJ):
            cols = slice(j * 512, (j + 1) * 512)
            for kc in range(8):
                act(XSQ[:, kc, :], XT[:, kc, cols], AF.Square)
            b = bank()
            for kc in range(8):
                mm(PS[b][:, :], ONES, XSQ[:, kc, :], start=(kc == 0), stop=(kc == 7))
            rb = RBt[j % 2]
            rsqrt_to(rb, PS[b][:, :], 1.0 / D, TMP[j % 2])
            for kc in range(8):
                stt(XB[:, kc, cols], XT[:, kc, cols], gcols[:, kc:kc + 1], rb, ALU.mult, ALU.mult)

    PPT = av(O_PPT, 2 * S).rearrange("p (k t) -> p k t", k=2)

    def ple(i):
        PST = [av(O_PST, 256), av(O_PST, 256)]
        for t in range(NT):
            pst = PST[t % 2]
            dma(pst, p_d[i, t * 128:(t + 1) * 128, :], eng='pool')
            b = bank()
            for kc in range(2):
                tr(PSB[b][:, kc * 128:(kc + 1) * 128], pst[:, kc * 128:(kc + 1) * 128], IDENT)
            cp(PPT[:, :, t * 128:(t + 1) * 128], PSB[b][:, 0:256].rearrange("p (k t) -> p k t", k=2),
               eng='act' if t % 2 else 'dve')
        for n in range(8):
            wg = ring_load(ple_wg_d[i, n])
            wp = ring_load(ple_wp_d[i, n])
            for j in range(NJ):
                cols = slice(j * 512, (j + 1) * 512)
                bg = bank()
                for kc in range(8):
                    mm(PS[bg][:, :], wg[:, kc, :], XB[:, kc, cols], start=(kc == 0), stop=(kc == 7))
                tmp = TMP[j % 2]
                act(tmp, PS[bg][:, :], AF.Sigmoid)
                bp = bank()
                for kc in range(2):
                    mm(PS[bp][:, :], wp[:, kc, :], PPT[:, kc, cols], start=(kc == 0), stop=(kc == 1))
                tt(tmp, tmp, PS[bp][:, :], ALU.mult)
                tt(XT[:, n, cols], XT[:, n, cols], tmp, ALU.add)

    SG = av(R1, 8 * S).rearrange("p (k t) -> p k t", k=8)
    CQB = av(O_CQB, 3 * S).rearrange("p (k t) -> p k t", k=3)
    CKVB = av(O_CKVB, 2 * S).rearrange("p (k t) -> p k t", k=2)
    KRRAW = av(O_KRRAW, NT * 32, F32).rearrange("p (t i) -> p t i", t=NT)
    KROPE = av(O_KROPE, NT * 32, F32).rearrange("p (t i) -> p t i", t=NT)
    GAM = av(O_GAM, 256, F32)
    DEL = av(O_DEL, 256, F32)
    DELN = av(O_DELN, 256, F32)
    GQ2 = av(O_GQ2, 192, F32)
    GKN2 = av(O_GKN2, 128, F32)
    GKR = av(O_GKR, 32, F32)
    BQ = STAT[:, 0:16]
    BKV = STAT[:, 16:32]
    SSKR = STAT[:, 32:48]
    ST1 = STAT[:, 48:64]
    ST2 = STAT[:, 64:80]

    def mla_layer(li):
        dma(GCOL[:, 0:13], mla_g_d[li])
        if stage == 1:
            return
        prep_norm(GCOL[:, 0:8])
        if stage == 1.5:
            return
        dma(av(O_GQ2, 352, F32), mla_hg_d[li])
        ts(GQ2, GQ2, 96.0 ** -0.5, ALU.mult)
        if stage < 2:
            return
        WA = av(O_WA, 8 * 672).rearrange("p (k n) -> p k n", k=8)
        for kc in range(8):
            dma(WA[:, kc, :], mla_wa_d[li, :, kc, :], eng='pool')
        SQT = [av(o, 512) for o in O_SQT]
        for j in range(NJ):
            cols = slice(j * 512, (j + 1) * 512)
            for (nblk, c0, dstT, gofs, statcol) in ((3, 0, CQB, 8, 0), (2, 384, CKVB, 11, 1)):
                for cb in range(nblk):
                    b = bank()
                    for kc in range(8):
                        mm(PS[b][:, :], WA[:, kc, c0 + cb * 128:c0 + (cb + 1) * 128], XB[:, kc, cols],
                           start=(kc == 0), stop=(kc == 7))
                    ts(dstT[:, cb, cols], PS[b][:, :], GCOL[:, gofs + cb:gofs + cb + 1], ALU.mult)
                    if stage >= 2.2:
                        act(SQT[cb], PS[b][:, :], AF.Square)
                for t4 in range(4):
                    if stage < 2.3:
                        break
                    t = j * 4 + t4
                    for cb in range(nblk):
                        mm(PS[4][:, statcol * 16 + t:statcol * 16 + t + 1], SQT[cb][:, t4 * 128:(t4 + 1) * 128],
                           ONES[:, 0:1], start=(cb == 0), stop=(cb == nblk - 1))
            for t4 in range(4):
                if stage < 2.4:
                    break
                t = j * 4 + t4
                for kc in range(8):
                    mm(PS[5][:, t * 32:(t + 1) * 32], XB[:, kc, t * 128:(t + 1) * 128], WA[:, kc, 640:672],
                       start=(kc == 0), stop=(kc == 7))
        if stage < 2.5:
            return
        cp(KRRAW, PS[5][:, :].rearrange("p (t i) -> p t i", t=NT))
        if stage < 2.6:
            return
        rsqrt_to(BQ, PS[4][:, 0:16], 1.0 / 384, ST1)
        rsqrt_to(BKV, PS[4][:, 16:32], 1.0 / 256, ST2)
        if stage < 3:
            return
        for cb in range(8):
            w = ring_load(mla_wg_d[li, cb])
            for j in range(NJ):
                cols = slice(j * 512, (j + 1) * 512)
                b = bank()
                for kc in range(8):
                    mm(PS[b][:, :], w[:, kc, :], XB[:, kc, cols], start=(kc == 0), stop=(kc == 7))
                act(SG[:, cb, cols], PS[b][:, :], AF.Silu)
        if stage < 4:
            return
        KRSQ = av(O_QR, NT * 32, F32).rearrange("p (t i) -> p t i", t=NT)
        tt(KRSQ, KRRAW, KRRAW, ALU.mult)
        red(SSKR, KRSQ)
        KRG = av(O_QR + 2048, NT * 32, F32).rearrange("p (t i) -> p t i", t=NT)
        tt(KRG, KRRAW, GKR.unsqueeze(1).broadcast_to([128, NT, 32]), ALU.mult)

        def rope(dst1, dst2, x1, x2, cosb, sinb, t1, t2):
            tt(t1, x1, cosb, ALU.mult)
            tt(t2, x2, sinb, ALU.mult)
            tt(dst1, t1, t2, ALU.subtract)
            tt(t1, x2, cosb, ALU.mult)
            tt(t2, x1, sinb, ALU.mult)
            tt(dst2, t1, t2, ALU.add)

        RT1 = av(O_ANG, 256, F32).rearrange("p (t i) -> p t i", t=NT)
        RT2 = av(O_ANG + 1024, 256, F32).rearrange("p (t i) -> p t i", t=NT)
        rope(KROPE[:, :, 0:16], KROPE[:, :, 16:32], KRG[:, :, 0:16], KRG[:, :, 16:32], COS, SIN, RT1, RT2)
        if stage < 5:
            return
        WQ = av(O_WQ, 3 * 1536).rearrange("p (k n) -> p k n", k=3)
        WKV = av(O_WKV, 2 * 2048).rearrange("p (k n) -> p k n", k=2)
        for kc in range(3):
            dma(WQ[:, kc, :], mla_wq_d[li, :, kc, :], eng='pool')
        for kc in range(2):
            dma(WKV[:, kc, :], mla_wkv_d[li, :, kc, :], eng='pool')
        GAM3 = GAM.rearrange("p (t h) -> p t h", t=NT)
        DEL3 = DEL.rearrange("p (t h) -> p t h", t=NT)
        DELN3 = DELN.rearrange("p (t h) -> p t h", t=NT)
        for t in range(NT):
            tcols = slice(t * 128, (t + 1) * 128)
            for qb in range(4):
                b = bank()
                for kc in range(3):
                    mm(PS[b][:, 0:384], CQB[:, kc, tcols], WQ[:, kc, qb * 384:(qb + 1) * 384],
                       start=(kc == 0), stop=(kc == 2))
                tmp = TMP[qb % 2]
                act(tmp[:, 0:384], PS[b][:, 0:384], AF.Square)
                red(GAM3[:, t, qb * 4:(qb + 1) * 4], tmp[:, 0:384].rearrange("p (h d) -> p h d", h=4))
            for kb4 in range(4):
                b = bank()
                for kc in range(2):
                    mm(PS[b][:, :], CKVB[:, kc, tcols], WKV[:, kc, kb4 * 512:(kb4 + 1) * 512],
                       start=(kc == 0), stop=(kc == 1))
                tmp = TMP[kb4 % 2]
                act(tmp[:, 0:256].rearrange("p (h d) -> p h d", h=4),
                    PS[b][:, :].rearrange("p (h d) -> p h d", h=4)[:, :, 0:64], AF.Square)
                red(DEL3[:, t, kb4 * 4:(kb4 + 1) * 4], tmp[:, 0:256].rearrange("p (h d) -> p h d", h=4))
        BQb = BQ.unsqueeze(2).broadcast_to([128, NT, H])
        BKVb = BKV.unsqueeze(2).broadcast_to([128, NT, H])
        T256 = av(O_ANG, 256, F32)
        T256b = av(O_ANG + 1024, 256, F32)
        T3 = T256.rearrange("p (t h) -> p t h", t=NT)
        tt(T3, GAM3, BQb, ALU.mult)
        tt(T3, T3, BQb, ALU.mult)
        rsqrt_to(GAM, T256, 1.0 / 96, T256b)
        tt(GAM3, GAM3, BQb, ALU.mult)
        tt(T3, DEL3, BKVb, ALU.mult)
        tt(T3, T3, BKVb, ALU.mult)
        tt(T3, T3, SSKR.unsqueeze(2).broadcast_to([128, NT, H]), ALU.add)
        rsqrt_to(DEL, T256, 1.0 / 96, T256b)
        tt(DELN3, DEL3, BKVb, ALU.mult)
        if stage < 6:
            return
        QT = [av(o, S) for o in O_QT]
        KT = [av(o, S) for o in O_KT]
        VA = av(O_VA, NT * 256).rearrange("p (t c) -> p t c", t=NT)
        QN = av(O_QN, NT * 192).rearrange("p (t c) -> p t c", t=NT)
        QR = av(O_QR, NT * 64, F32).rearrange("p (t h i) -> p t h i", t=NT, h=2)
        memset(VA[:, :, 64:192], 0.0)
        memset(VA[:, :, 64:65], 1.0)
        memset(VA[:, :, 128:129], 1.0)
        PT = [av(o, 512) for o in O_PT]
        REC = av(O_REC, 512, F32)
        RECH = av(O_RECH, 512)
        RECL = av(O_RECL, 512)
        pti = [0]
        COSb = COS.unsqueeze(2).broadcast_to([128, NT, 2, 16])
        SINb = SIN.unsqueeze(2).broadcast_to([128, NT, 2, 16])
        for c in range(8):
            QN4 = QN.rearrange("p t (h d) -> p t h d", h=2)
            for t in range(NT):
                tcols = slice(t * 128, (t + 1) * 128)
                b = bank()
                for kc in range(3):
                    mm(PS[b][:, 0:192], CQB[:, kc, tcols], WQ[:, kc, c * 192:(c + 1) * 192],
                       start=(kc == 0), stop=(kc == 2))
                tmp = TMP[t % 2]
                t3 = tmp[:, 0:192].rearrange("p (h d) -> p h d", h=2)
                tt(t3, PS[b][:, 0:192].rearrange("p (h d) -> p h d", h=2),
                   GAM3[:, t, 2 * c:2 * c + 2].unsqueeze(2).broadcast_to([128, 2, 96]), ALU.mult)
                g3 = GQ2.rearrange("p (h d) -> p h d", h=2)
                tt(QN4[:, t, :, 0:64], t3[:, :, 0:64], g3[:, :, 0:64], ALU.mult)
                tt(QR[:, t, :, :], t3[:, :, 64:96], g3[:, :, 64:96], ALU.mult)
            RA1 = av(O_ANG, 512, F32).rearrange("p (t h i) -> p t h i", t=NT, h=2)
            RA2 = av(O_KRRAW, 512, F32).rearrange("p (t h i) -> p t h i", t=NT, h=2)
            rope(QN4[:, :, :, 64:80], QN4[:, :, :, 80:96], QR[:, :, :, 0:16], QR[:, :, :, 16:32], COSb, SINb, RA1, RA2)
            for hh in range(2):
                for half in range(2):
                    b = bank()
                    for t8 in range(8):
                        t = half * 8 + t8
                        tr(PSB[b][0:96, t8 * 128:(t8 + 1) * 128], QN[:, t, hh * 96:(hh + 1) * 96], IDENT)
                    cp(QT[hh][0:96, half * 1024:(half + 1) * 1024], PSB[b][0:96, :], eng='act' if half else 'dve')
            for t in range(NT):
                tcols = slice(t * 128, (t + 1) * 128)
                b = bank()
                for kc in range(2):
                    mm(PS[b][:, 0:256], CKVB[:, kc, tcols], WKV[:, kc, c * 256:(c + 1) * 256],
                       start=(kc == 0), stop=(kc == 1))
                p3 = PS[b][:, 0:256].rearrange("p (h d) -> p h d", h=2)
                tmp = TMP[t % 2]
                t3 = tmp[:, 0:128].rearrange("p (h d) -> p h d", h=2)
                tt(t3, p3[:, :, 0:64], DELN3[:, t, 2 * c:2 * c + 2].unsqueeze(2).broadcast_to([128, 2, 64]), ALU.mult)
                tt(QN4[:, t, :, 0:64], t3, GKN2.rearrange("p (h d) -> p h d", h=2), ALU.mult)
                vdst = VA[:, t, :].rearrange("p (a c) -> p a c", a=4)[:, 0:4:3, :]
                act(vdst, p3[:, :, 64:128], AF.Copy, scale=BKV[:, t:t + 1])
            tt(QN4[:, :, :, 64:96], KROPE.unsqueeze(2).broadcast_to([128, NT, 2, 32]),
               DEL3[:, :, 2 * c:2 * c + 2].unsqueeze(3).broadcast_to([128, NT, 2, 32]), ALU.mult)
            for hh in range(2):
                for half in range(2):
                    b = bank()
                    for t8 in range(8):
                        t = half * 8 + t8
                        tr(PSB[b][0:96, t8 * 128:(t8 + 1) * 128], QN[:, t, hh * 96:(hh + 1) * 96], IDENT)
                    cp(KT[hh][0:96, half * 1024:(half + 1) * 1024], PSB[b][0:96, :], eng='act' if half else 'dve')
            for hh in range(2):
                for j in range(NJ):
                    ob = 2 + (j % 2)
                    nkb = 4 * j + 4
                    for kb in range(nkb):
                        r = kb - 4 * j
                        c0 = max(0, r) * 128
                        zb = kb % 2
                        qs = slice(j * 512 + c0, (j + 1) * 512)
                        mm(PS[zb][:, c0:512], KT[hh][0:96, kb * 128:(kb + 1) * 128], QT[hh][0:96, qs],
                           start=True, stop=(r < 0))
                        if r >= 0:
                            mm(PS[zb][:, c0:c0 + 128], IDENT, MASKM, start=False, stop=True, skip=True)
                        pt = PT[pti[0] % 3]
                        pti[0] += 1
                        act(pt[:, c0:512], PS[zb][:, c0:512], AF.Exp)
                        if hh == 0:
                            mm(PS[ob][0:65, c0:512], VA[:, kb, 0:65], pt[:, c0:512],
                               start=(kb == 0), stop=(kb == nkb - 1), skip=True)
                        else:
                            mm(PS[ob][:, c0:512], VA[:, kb, 128:256], pt[:, c0:512],
                               start=(kb == 0), stop=(kb == nkb - 1), skip=True)
                    cols = slice(j * 512, (j + 1) * 512)
                    if hh == 0:
                        drow, rows = 64, slice(0, 64)
                        lsel = BSEL[64:65, 0:64]
                    else:
                        drow, rows = 0, slice(64, 128)
                        lsel = BSEL[0:1, :]
                    rr = slice(drow, drow + 1)
                    recip(REC[rr, :], PS[ob][rr, :])
                    cp(RECH[rr, :], REC[rr, :])
                    tt(RECL[rr, :], REC[rr, :], RECH[rr, :], ALU.subtract)
                    bb = bank()
                    orow = slice(0, 64) if hh == 0 else slice(0, 128)
                    mm(PS[bb][orow, :], lsel, RECH[rr, :], start=True, stop=False)
                    mm(PS[bb][orow, :], lsel, RECL[rr, :], start=False, stop=True)
                    tmp = TMP[j % 2]
                    tt(tmp[rows, :], PS[bb][rows, :], SG[rows, c, cols], ALU.mult)
                    tt(SG[rows, c, cols], PS[ob][rows, :], tmp[rows, :], ALU.mult)
        if stage < 7:
            return
        for n in range(8):
            w = ring_load(mla_wo_d[li, n])
            for j in range(NJ):
                cols = slice(j * 512, (j + 1) * 512)
                b = bank()
                for kc in range(8):
                    mm(PS[b][:, :], w[:, kc, :], SG[:, kc, cols], start=(kc == 0), stop=(kc == 7))
                tt(XT[:, n, cols], XT[:, n, cols], PS[b][:, :], ALU.add)
                cp(XB[:, n, cols], XT[:, n, cols], eng='act')

    KSH = av(R1, 8 * S).rearrange("p (k t) -> p k t", k=8)
    VSH = av(O_VSH, NT * 1024).rearrange("p (t c) -> p t c", t=NT)

    def shared_kv():
        dma(GCOL[:, 0:8], kv_g_d)
        prep_norm(GCOL[:, 0:8])
        for n in range(8):
            w = ring_load(kv_wk_d[n])
            for j in range(NJ):
                cols = slice(j * 512, (j + 1) * 512)
                b = bank()
                for kc in range(8):
                    mm(PS[b][:, :], w[:, kc, :], XB[:, kc, cols], start=(kc == 0), stop=(kc == 7))
                cp(KSH[:, n, cols], PS[b][:, :], eng='act' if j % 2 else 'dve')
        for n in range(8):
            w = ring_load(kv_wv_d[n])
            for t in range(NT):
                b = bank()
                for kc in range(8):
                    mm(PS[b][:, 0:128], XB[:, kc, t * 128:(t + 1) * 128], w[:, kc, :], start=(kc == 0), stop=(kc == 7))
                cp(VSH[:, t, n * 128:(n + 1) * 128], PS[b][:, 0:128], eng='act' if t % 2 else 'dve')

    def sb_layer(lj):
        dma(GCOL[:, 0:8], sb_g_d[lj])
        prep_norm(GCOL[:, 0:8])
        QS = [av(o, 512) for o in O_QS]
        SGC = [av(o, 512) for o in O_SGC]
        OG = [av(o, 512) for o in O_OG]
        SP = [av(o, 512) for o in O_SP]
        SPACC = [av(o, 512) for o in O_SPACC]
        AT = [av(o, 512) for o in O_A]
        E = [av(o, 512, F32) for o in O_E]
        cnt = [0]
        for c in range(8):
            wq = ring_load(sb_wq_d[lj, c])
            wg = ring_load(sb_wgt_d[lj, c])
            wo = av(O_RING[ring_i[0] % 3], 1024)
            ring_i[0] += 1
            dma(wo, sb_wo_d[lj, c], eng='pool')
            for j in range(NJ):
                cols = slice(j * 512, (j + 1) * 512)
                qs_t = QS[j % 2]
                sg_t = SGC[j % 2]
                og_t = OG[j % 2]
                b = bank()
                for kc in range(8):
                    mm(PS[b][:, :], wq[:, kc, :], XB[:, kc, cols], start=(kc == 0), stop=(kc == 7))
                ts(qs_t, PS[b][:, :], 0.125, ALU.mult)
                b = bank()
                for kc in range(8):
                    mm(PS[b][:, :], wg[:, kc, :], XB[:, kc, cols], start=(kc == 0), stop=(kc == 7))
                act(sg_t, PS[b][:, :], AF.Silu)
                ob = 2 + (j % 2)
                nkb = 4 * j + 4
                for hh in range(2):
                    rows = slice(hh * 64, (hh + 1) * 64)
                    spacc = SPACC[hh]
                    for kb in range(nkb - 1, -1, -1):
                        r = kb - 4 * j
                        c0 = max(0, r) * 128
                        k = cnt[0]
                        cnt[0] += 1
                        zb = k % 2
                        e_t = E[k % 2]
                        sp_t = SP[k % 2]
                        a_t = AT[k % 2]
                        first = (kb == nkb - 1)
                        mm(PS[zb][:, c0:512], KSH[rows, c, kb * 128:(kb + 1) * 128], qs_t[rows, c0:512],
                           start=True, stop=(r < 0))
                        if r >= 0:
                            mm(PS[zb][:, c0:c0 + 128], IDENT, MASKS, start=False, stop=True, skip=True)
                        act(e_t[:, c0:512], PS[zb][:, c0:512], AF.Exp)
                        act(sp_t[:, c0:512], e_t[:, c0:512], AF.Ln, bias=ONEC)
                        mm(PS[zb][:, c0:512], NEGU, sp_t[:, c0:512], start=False, stop=first, skip=True)
                        if not first:
                            a0 = c0 + 128 if r >= 0 else 0
                            mm(PS[zb][:, a0:512], NEGONES, spacc[:, a0:512], start=False, stop=True, skip=True)
                        if kb > 0:
                            if first:
                                cp(spacc[:, c0:512], sp_t[:, c0:512])
                            elif r >= 0:
                                pc0 = c0 + 128
                                cp(spacc[:, c0:pc0], sp_t[:, c0:pc0])
                                tt(spacc[:, pc0:512], spacc[:, pc0:512], sp_t[:, pc0:512], ALU.add)
                            else:
                                tt(spacc[:, :], spacc[:, :], sp_t[:, :], ALU.add)
                        act(a_t[:, c0:512], PS[zb][:, c0:512], AF.Exp)
                        mm(PS[ob][rows, c0:512], VSH[:, kb, (2 * c + hh) * 64:(2 * c + hh + 1) * 64], a_t[:, c0:512],
                           start=first, stop=(kb == 0), skip=True)
                tt(og_t, PS[ob][:, :], sg_t, ALU.mult)
                for n in range(8):
                    b = bank()
                    mm(PS[b][:, :], wo[:, n * 128:(n + 1) * 128], og_t, start=True, stop=True)
                    tt(XT[:, n, cols], XT[:, n, cols], PS[b][:, :], ALU.add)
        for n in range(8):
            for j in range(NJ):
                cols = slice(j * 512, (j + 1) * 512)
                cp(XB[:, n, cols], XT[:, n, cols], eng='act')

    ONEC = STAT[:, 94:95]
    memset(ONEC, 1.0)

    for i in (layers if layers is not None else range(n_layers)):
        if stage < 1:
            break
        if i < 2:
            mla_layer(i)
        else:
            sb_layer(i - 2)
        if stage >= 8:
            ple(i)
        if i == 1 and n_layers > 2:
            shared_kv()

    if dbg is not None:
        name = dbg[0]
        if name == 'cos':
            dma(dbg_d[:, 0:256], av(O_COS, 256, F32))
            dma(dbg_d[:, 256:512], av(O_SIN, 256, F32))

    YO = av(O_XIN, 1024, F32)
    for t in range(NT):
        for half in range(2):
            b = 4 + half
            for q4 in range(4):
                kc = half * 4 + q4
                tr(PS[b][:, q4 * 128:(q4 + 1) * 128], XT[:, kc, t * 128:(t + 1) * 128], IDENTF)
            cp(YO[:, half * 512:(half + 1) * 512], PS[b][:, :], eng='act' if half else 'dve')
        dma(y_d[t * 128:(t + 1) * 128, :], YO)
    P.wait_all_dma('sp')
    P.emit(st)
    st.close()
    return nc, P


def _consts():
    j = np.arange(128)[:, None]
    s = np.arange(128)[None, :]
    cb = np.zeros((128, N_CONSTB, 128), np.float32)
    cb[:, C_IDENT] = (j == s)
    cb[:, C_NEGU] = -(j >= s).astype(np.float32)
    cb[:, C_NEGONES] = -1.0
    cb[:, C_ONES] = 1.0
    cb[:, C_MASKM] = np.where(s >= j, 0.0, NEG)
    cb[:, C_MASKS] = np.where(s > j, 0.0, NEG)
    bs = np.zeros((128, 128), np.float32)
    bs[64, 0:64] = 1.0
    bs[0, 64:128] = 1.0
    cb[:, C_BSEL] = bs
    cf = np.zeros((128, 144), np.float32)
    cf[:, 0:128] = np.eye(128, dtype=np.float32)
    half = 16
    inv = (1.0 / (np.float32(10000.0) ** (np.arange(half, dtype=np.float32) / np.float32(half)))).astype(np.float32)
    cf[:, 128:144] = inv[None, :]
    return cb.reshape(128, -1), cf


def _kt(w):
    K, N = w.shape
    return np.ascontiguousarray(w.reshape(K // 128, 128, N).transpose(1, 0, 2))


def _blk(w, nb=128):
    K, N = w.shape
    a = w.reshape(K // 128, 128, N // nb, nb).transpose(2, 1, 0, 3)
    return np.ascontiguousarray(a)


def _gcol(g):
    return np.ascontiguousarray(g.reshape(-1, 128).T)


def make_in_maps(inputs):
    f = lambda k: np.asarray(inputs[k], dtype=np.float32)
    cb, cf = _consts()
    mla_w_in = f("mla_w_in")
    shared = {
        "constb": cb, "constf": cf,
        "mla_gcol": np.stack([np.concatenate([_gcol(f("mla_ln_g")[l]), _gcol(f("mla_q_norm_g")[l]),
                                              _gcol(f("mla_kv_norm_g")[l])], axis=1) for l in range(2)]),
        "mla_wa": np.stack([_kt(mla_w_in[l][:, 0:672]) for l in range(2)]),
        "mla_wgate": np.stack([_blk(mla_w_in[l][:, 672:1696]) for l in range(2)]),
        "mla_wq": np.stack([_kt(f("mla_w_q_up")[l]) for l in range(2)]),
        "mla_wkv": np.stack([_kt(f("mla_w_kv_up")[l]) for l in range(2)]),
        "mla_hg": np.stack([np.ascontiguousarray(np.broadcast_to(np.concatenate(
            [f("mla_q_head_g")[l], f("mla_q_head_g")[l], f("mla_k_head_g")[l][:64], f("mla_k_head_g")[l][:64],
             f("mla_k_head_g")[l][64:]])[None, :], (128, 352))) for l in range(2)]),
        "mla_wo": np.stack([_blk(f("mla_w_out")[l]) for l in range(2)]),
        "kv_gcol": _gcol(f("kv_ln_g")),
        "kv_wk": _blk(f("w_kv_shared")[:, 0:1024]),
        "kv_wv": _blk(f("w_kv_shared")[:, 1024:2048]),
        "sb_gcol": np.stack([_gcol(f("sb_ln_g")[l]) for l in range(2)]),
        "sb_wq": np.stack([_blk(f("sb_w_in")[l][:, 0:1024]) for l in range(2)]),
        "sb_wgate": np.stack([_blk(f("sb_w_in")[l][:, 1024:2048]) for l in range(2)]),
        "sb_wo": np.ascontiguousarray(f("sb_w_out").reshape(2, 8, 128, 1024)),
        "ple_wg": np.stack([_blk(f("ple_w_gate")[l]) for l in range(4)]),
        "ple_wp": np.stack([_blk(f("ple_w_proj")[l]) for l in range(4)]),
    }
    x = f("x")
    p = f("p")
    pos = np.asarray(inputs["positions"]).astype(np.int32)
    maps = []
    for b in range(8):
        m = dict(shared)
        m["x"] = np.ascontiguousarray(x[b])
        m["p"] = np.ascontiguousarray(p[:, b])
        m["pos"] = np.ascontiguousarray(pos[b].reshape(NT, 128).T)
        maps.append(m)
    return maps


_CACHE = {}


def kernel(**inputs):
    maps = make_in_maps(inputs)
    if "nc" not in _CACHE:
        _CACHE["nc"] = build_program(4)[0]
    res = run_bass_kernel_spmd(_CACHE["nc"], maps, core_ids=list(range(8)))
    return np.stack([np.asarray(r["y"], dtype=np.float32) for r in res.results], axis=0)
```

```python
import math
from contextlib import ExitStack

import numpy as np
import concourse.bass as bass
import concourse.mybir as mybir
from concourse.bass_utils import run_bass_kernel_spmd

F32 = mybir.dt.float32
BF16 = mybir.dt.bfloat16
I32 = mybir.dt.int32
AF = mybir.ActivationFunctionType
ALU = mybir.AluOpType
AX = mybir.AxisListType

S = 2048
D = 1024
NT = 16
NJ = 4
H = 16
EPS = 1e-6
NEG = -30000.0

_ESZ = {}


def esize(dt):
    if dt not in _ESZ:
        _ESZ[dt] = mybir.dt.size(dt)
    return _ESZ[dt]


def region(ap):
    t = ap.tensor
    es = esize(ap.dtype)
    dims = ap.ap
    off = ap.offset
    if type(t).__name__.startswith('DRam'):
        lo = hi = off
        for st, cnt in dims:
            d = st * (cnt - 1)
            if d < 0:
                lo += d
            else:
                hi += d
        return (t.name, 0, 1, lo * es, (hi + 1) * es)
    if t.name.startswith('ps'):
        return (t.name, 0, 128, 0, 2048)
    pstep, pcnt = dims[0]
    p0 = off // pstep
    c0 = off - p0 * pstep
    lo = hi = c0
    for st, cnt in dims[1:]:
        d = st * (cnt - 1)
        if d < 0:
            lo += d
        else:
            hi += d
    return (t.name, p0, p0 + pcnt, lo * es, (hi + 1) * es)


class Op:
    __slots__ = ('eng', 'fn', 'seq', 'waits', 'signal', 'dma', 'snap', 'sigcount')

    def __init__(self, eng, fn):
        self.eng = eng
        self.fn = fn
        self.waits = []
        self.signal = False
        self.dma = None
        self.snap = None


class Prog:
    ENGS = ('pe', 'act', 'dve', 'pool', 'sp')

    def __init__(self, nc, n_dma_sems=32):
        self.nc = nc
        self.ops = {e: [] for e in self.ENGS}
        self.track = {}
        self.seen = {e: {f: -1 for f in self.ENGS} for e in self.ENGS}
        self.seen_dma = {e: {} for e in self.ENGS}
        self.n_dma_sems = n_dma_sems
        self.dma_sem_val = [0] * n_dma_sems
        h = n_dma_sems // 2
        self.dma_pool = {'sp': list(range(0, h)), 'act': list(range(0, h)), 'pool': list(range(h, n_dma_sems))}
        self.dma_rr = {'sp': 0, 'act': 0, 'pool': 0}
        self.nops = 0

    def _deps(self, regs_r, regs_w, eng=None):
        deps = []
        for (key, p0, p1, b0, b1) in regs_r:
            lst = self.track.get(key)
            if lst:
                psum = key.startswith('ps')
                for ent in lst:
                    if ent[5] and ent[0] < p1 and p0 < ent[1] and ent[2] < b1 and b0 < ent[3]:
                        deps.append(ent[4])
                    elif psum and (not ent[5]) and ent[4].eng != eng:
                        deps.append(ent[4])
        for (key, p0, p1, b0, b1) in regs_w:
            lst = self.track.get(key)
            if lst:
                for ent in lst:
                    if ent[0] < p1 and p0 < ent[1] and ent[2] < b1 and b0 < ent[3]:
                        deps.append(ent[4])
        return deps

    def _record(self, op, regs_r, regs_w):
        for (key, p0, p1, b0, b1) in regs_w:
            lst = self.track.setdefault(key, [])
            lst[:] = [e for e in lst if not (p0 <= e[0] and e[1] <= p1 and b0 <= e[2] and e[3] <= b1)]
            lst.append([p0, p1, b0, b1, op, True])
        for (key, p0, p1, b0, b1) in regs_r:
            lst = self.track.setdefault(key, [])
            lst[:] = [e for e in lst if not ((not e[5]) and e[4].eng == op.eng and e[4].dma is None
                                             and op.dma is None
                                             and p0 <= e[0] and e[1] <= p1 and b0 <= e[2] and e[3] <= b1)]
            lst.append([p0, p1, b0, b1, op, False])

    def add(self, eng, fn, reads=(), writes=(), dma=False):
        op = Op(eng, fn)
        op.seq = len(self.ops[eng])
        regs_r = [region(a) for a in reads]
        regs_w = [region(a) for a in writes]
        deps = self._deps(regs_r, regs_w, eng)
        seen = self.seen[eng]
        seen_d = self.seen_dma[eng]
        need_e = {}
        need_d = {}
        for d in deps:
            if d.dma is not None:
                s, v = d.dma
                if seen_d.get(s, 0) < v and need_d.get(s, 0) < v:
                    need_d[s] = v
            else:
                if d.eng == 'pe' and eng == 'pe':
                    continue
                if d.seq > seen[d.eng]:
                    if d.eng not in need_e or need_e[d.eng].seq < d.seq:
                        need_e[d.eng] = d
        if dma:
            pool = self.dma_pool[eng]
            s = pool[self.dma_rr[eng] % len(pool)]
            self.dma_rr[eng] += 1
            prev = self.dma_sem_val[s]
            if prev > 0 and seen_d.get(s, 0) < prev and need_d.get(s, 0) < prev:
                need_d[s] = prev
            self.dma_sem_val[s] = prev + 16
            op.dma = (s, prev + 16)
        for f, d in need_e.items():
            d.signal = True
            op.waits.append(('e', f, d))
            seen[f] = max(seen[f], d.seq)
            if d.snap is not None:
                se, sd = d.snap
                for g, v in se.items():
                    if v > seen[g]:
                        seen[g] = v
                for g, v in sd.items():
                    if v > seen_d.get(g, 0):
                        seen_d[g] = v
        for s, v in need_d.items():
            op.waits.append(('d', s, v))
            seen_d[s] = max(seen_d.get(s, 0), v)
        if not dma:
            op.snap = (dict(seen), dict(seen_d))
        self.ops[eng].append(op)
        self._record(op, regs_r, regs_w)
        self.nops += 1
        return op

    def wait_all_dma(self, eng='sp'):
        op = Op(eng, None)
        op.seq = len(self.ops[eng])
        for s in range(self.n_dma_sems):
            if self.dma_sem_val[s] > 0:
                op.waits.append(('d', s, self.dma_sem_val[s]))
        self.ops[eng].append(op)

    def emit(self, stack):
        nc = self.nc
        esem = {e: stack.enter_context(nc.semaphore("s_" + e)) for e in self.ENGS}
        dsem = [stack.enter_context(nc.semaphore("d_%d" % i)) for i in range(self.n_dma_sems)]
        for e in self.ENGS:
            c = 0
            for op in self.ops[e]:
                if op.signal:
                    c += 1
                op.sigcount = c
        block = stack.enter_context(nc.Block())
        engmap = {'pe': block.tensor, 'act': block.scalar, 'dve': block.vector,
                  'pool': block.gpsimd, 'sp': block.sync}

        def make(e):
            ops = self.ops[e]

            def body(eng):
                for op in ops:
                    for w in op.waits:
                        if w[0] == 'e':
                            eng.wait_ge(esem[w[1]], w[2].sigcount)
                        else:
                            eng.wait_ge(dsem[w[1]], w[2])
                    if op.fn is None:
                        continue
                    ins = op.fn(eng)
                    if op.dma is not None:
                        ins.then_inc(dsem[op.dma[0]], 16)
                    elif op.signal:
                        ins.then_inc(esem[e], 1)
            return body

        for e in self.ENGS:
            if self.ops[e]:
                engmap[e](make(e))


R0 = 0
R1 = 32768
R2 = 65536
R3 = 98304
R4 = 122880
O_QT = [R0 + 0, R0 + 4096]
O_KT = [R0 + 8192, R0 + 12288]
O_VA = R0 + 16384
O_QN = R0 + 24576
O_CQB = R2
O_CKVB = R2 + 12288
O_WA = R2 + 20480
O_WQ = R2 + 20480
O_PPT = R3
O_VSH = R2
O_WKV = R3
O_XSQ = R3
O_KROPE = R3 + 8192
O_QR = R3 + 10240
O_GAM = R3 + 14336
O_DEL = R3 + 15360
O_DELN = R3 + 16384
O_PT = [R3 + 17408, R3 + 18432, R3 + 19456]
O_SQT = O_PT
O_KRRAW = R3 + 20480
O_REC = R3 + 22528
O_QS = [R3 + 8192, R3 + 9216]
O_SGC = [R3 + 10240, R3 + 11264]
O_OG = [R3 + 12288, R3 + 13312]
O_SP = [R3 + 14336, R3 + 15360]
O_SPACC = [R3 + 16384, R3 + 17408]
O_A = [R3 + 18432, R3 + 19456]
O_E = [R3 + 20480, R3 + 22528]
O_RING = [R4, R4 + 2048, R4 + 4096]
O_TMP = [R4 + 6144, R4 + 8192]
O_XIN = R4 + 6144
O_RB = [R4 + 10240, R4 + 10240]
O_CONSTB = R4 + 12288
O_IDENTF = O_CONSTB + 1792
O_COS = O_IDENTF + 512
O_SIN = O_COS + 1024
O_GQ2 = O_SIN + 1024
O_GKN2 = O_GQ2 + 768
O_GKR = O_GKN2 + 512
O_STAT = O_GKR + 128
O_RECH = O_STAT + 384
O_RECL = O_RECH + 1024
O_GCOL = O_RECL + 1024
O_PST = O_GCOL + 64
O_POSF = O_PST + 512
O_INV = O_POSF + 128
O_ANG = O_INV + 64
ARENA_BYTES = O_ANG + 3072
N_CONSTB = 7
C_IDENT, C_NEGU, C_NEGONES, C_ONES, C_MASKM, C_MASKS, C_BSEL = range(7)


def build_program(n_layers=4, dbg=None, stage=99, layers=None):
    nc = bass.Bass("TRN2", target_bir_lowering=False)
    dt_in = lambda n, s, d=F32: nc.dram_tensor(n, s, d, kind="ExternalInput").ap()
    x_d = dt_in("x", [S, D])
    p_d = dt_in("p", [4, S, 256])
    pos_d = dt_in("pos", [128, NT], I32)
    cb_d = dt_in("constb", [128, N_CONSTB * 128])
    cf_d = dt_in("constf", [128, 128 + 16])
    mla_g_d = dt_in("mla_gcol", [2, 128, 13])
    mla_wa_d = dt_in("mla_wa", [2, 128, 8, 672])
    mla_wg_d = dt_in("mla_wgate", [2, 8, 128, 8, 128])
    mla_wq_d = dt_in("mla_wq", [2, 128, 3, 1536])
    mla_wkv_d = dt_in("mla_wkv", [2, 128, 2, 2048])
    mla_hg_d = dt_in("mla_hg", [2, 128, 352])
    mla_wo_d = dt_in("mla_wo", [2, 8, 128, 8, 128])
    kv_g_d = dt_in("kv_gcol", [128, 8])
    kv_wk_d = dt_in("kv_wk", [8, 128, 8, 128])
    kv_wv_d = dt_in("kv_wv", [8, 128, 8, 128])
    sb_g_d = dt_in("sb_gcol", [2, 128, 8])
    sb_wq_d = dt_in("sb_wq", [2, 8, 128, 8, 128])
    sb_wgt_d = dt_in("sb_wgate", [2, 8, 128, 8, 128])
    sb_wo_d = dt_in("sb_wo", [2, 8, 128, 1024])
    ple_wg_d = dt_in("ple_wg", [4, 8, 128, 8, 128])
    ple_wp_d = dt_in("ple_wp", [4, 8, 128, 2, 128])
    y_d = nc.dram_tensor("y", [S, D], F32, kind="ExternalOutput").ap()
    dbg_d = None
    if dbg is not None:
        dbg_d = nc.dram_tensor("dbg", list(dbg[1]), F32, kind="ExternalOutput").ap()

    st = ExitStack()
    XTt = st.enter_context(nc.sbuf_tensor("XT", [128, 8 * S], F32))
    AR = st.enter_context(nc.sbuf_tensor("AR", [128, ARENA_BYTES // 2], BF16))
    PS = [st.enter_context(nc.psum_tensor("ps%d" % i, [128, 512], F32)) for i in range(8)]
    PSB = [t.bitcast(BF16) for t in PS]
    P = Prog(nc)

    def av(off, n, dt=BF16):
        v = AR[:, off // 2: off // 2 + (n * esize(dt)) // 2]
        return v if dt == BF16 else v.bitcast(dt)

    XT = XTt[:].rearrange("p (k t) -> p k t", k=8)

    def mm(out, lhsT, rhs, start=True, stop=True, skip=False):
        kw = dict(start=start, stop=stop)
        if skip:
            kw['skip_group_check'] = True
        P.add('pe', lambda q: q.matmul(out, lhsT=lhsT, rhs=rhs, **kw),
              reads=[lhsT, rhs] + ([] if start else [out]), writes=[out])

    def tr(out, in_, ident):
        P.add('pe', lambda q: q.transpose(out, in_, ident), reads=[in_, ident], writes=[out])

    def act(out, in_, func, scale=1.0, bias=0.0, eng='act'):
        rd = [in_]
        if not isinstance(scale, (int, float)):
            rd.append(scale)
        if not isinstance(bias, (int, float)):
            rd.append(bias)
        P.add('act', lambda q: q.activation(out=out, in_=in_, func=func, scale=scale, bias=bias),
              reads=rd, writes=[out])

    def tt(out, in0, in1, op, eng='dve'):
        P.add(eng, lambda q: q.tensor_tensor(out=out, in0=in0, in1=in1, op=op), reads=[in0, in1], writes=[out])

    def ts(out, in0, s1, op0, s2=None, op1=None, eng='dve'):
        rd = [in0]
        if not isinstance(s1, (int, float)):
            rd.append(s1)
        if s2 is not None and not isinstance(s2, (int, float)):
            rd.append(s2)
        if op1 is None:
            P.add(eng, lambda q: q.tensor_scalar(out=out, in0=in0, scalar1=s1, scalar2=None, op0=op0),
                  reads=rd, writes=[out])
        else:
            P.add(eng, lambda q: q.tensor_scalar(out=out, in0=in0, scalar1=s1, scalar2=s2, op0=op0, op1=op1),
                  reads=rd, writes=[out])

    def stt(out, in0, scalar, in1, op0, op1):
        rd = [in0, in1]
        if not isinstance(scalar, (int, float)):
            rd.append(scalar)
        P.add('dve', lambda q: q.scalar_tensor_tensor(out=out, in0=in0, scalar=scalar, in1=in1, op0=op0, op1=op1),
              reads=rd, writes=[out])

    def cp(out, in_, eng='dve'):
        if eng == 'act':
            P.add('act', lambda q: q.copy(out=out, in_=in_), reads=[in_], writes=[out])
        else:
            P.add(eng, lambda q: q.tensor_copy(out=out, in_=in_), reads=[in_], writes=[out])

    def red(out, in_, eng='dve'):
        P.add(eng, lambda q: q.tensor_reduce(out=out, in_=in_, op=ALU.add, axis=AX.X), reads=[in_], writes=[out])

    def recip(out, in_):
        P.add('dve', lambda q: q.reciprocal(out=out, in_=in_), reads=[in_], writes=[out])

    def memset(ap, val, eng='dve'):
        P.add(eng, lambda q: q.memset(ap, val), reads=[], writes=[ap])

    def dma(out, in_, eng='sp'):
        P.add(eng, lambda q: q.dma_start(out=out, in_=in_), reads=[in_], writes=[out], dma=True)

    def rsqrt_to(out, in_, scale, tmp):
        act(tmp, in_, AF.Ln, scale=scale, bias=EPSC)
        act(out, tmp, AF.Exp, scale=-0.5)

    CB = av(O_CONSTB, N_CONSTB * 128).rearrange("p (c n) -> p c n", c=N_CONSTB)
    dma(av(O_CONSTB, N_CONSTB * 128), cb_d, eng='pool')
    IDENT = CB[:, C_IDENT, :]
    NEGU = CB[:, C_NEGU, :]
    NEGONES = CB[:, C_NEGONES, :]
    ONES = CB[:, C_ONES, :]
    MASKM = CB[:, C_MASKM, :]
    MASKS = CB[:, C_MASKS, :]
    BSEL = CB[:, C_BSEL, :]
    IDENTF = av(O_IDENTF, 128, F32)
    dma(IDENTF, cf_d[:, 0:128])
    INV = av(O_INV, 16, F32)
    dma(INV, cf_d[:, 128:144])
    STAT = av(O_STAT, 96, F32)
    EPSC = STAT[:, 95:96]
    memset(EPSC, EPS)
    GCOL = av(O_GCOL, 16, F32)
    psrot = [0]

    def bank(i=None):
        if i is None:
            psrot[0] = (psrot[0] + 1) % 2
            return 6 + psrot[0]
        return i

    TMP = [av(o, 512, F32) for o in O_TMP]
    RBt = [av(o, 512, F32) for o in O_RB]
    RING = [av(o, 1024).rearrange("p (k n) -> p k n", k=8) for o in O_RING]
    ring_i = [0]

    def ring_load(src, shape=None):
        i = ring_i[0] % 3
        ring_i[0] += 1
        k, n = src.shape[1], src.shape[2]
        dst = av(O_RING[i], k * n).rearrange("p (k n) -> p k n", k=k)
        dma(dst, src, eng='pool')
        return dst

    XB = av(R0, 8 * S).rearrange("p (k t) -> p k t", k=8)
    XSQ = av(O_XSQ, 8 * 512).rearrange("p (k t) -> p k t", k=8)

    XIN = av(O_XIN, 1024, F32)
    for t in range(NT):
        dma(XIN, x_d[t * 128:(t + 1) * 128, :])
        for half in range(2):
            b = 4 + half
            for q4 in range(4):
                kc = half * 4 + q4
                tr(PS[b][:, q4 * 128:(q4 + 1) * 128], XIN[:, kc * 128:(kc + 1) * 128], IDENTF)
            dst = XT[:, half * 4:(half + 1) * 4, t * 128:(t + 1) * 128]
            src = PS[b][:, :].rearrange("p (k t) -> p k t", k=4)
            if half == 0:
                cp(dst, src, eng='dve')
            else:
                cp(dst, src, eng='act')

    COS = av(O_COS, 256, F32).rearrange("p (t i) -> p t i", t=NT)
    SIN = av(O_SIN, 256, F32).rearrange("p (t i) -> p t i", t=NT)
    if True:
        POSI = av(O_POSF + 64, 16, I32)
        POSF = av(O_POSF, 16, F32)
        dma(POSI, pos_d)
        cp(POSF, POSI)
        ANG = av(O_ANG, 256, F32)
        A2 = av(O_ANG + 1024, 256, F32)
        A3 = av(O_ANG + 2048, 256, F32)
        KI = av(O_ANG + 2048, 256, I32)
        ANG3 = ANG.rearrange("p (t i) -> p t i", t=NT)
        tt(ANG3, POSF.unsqueeze(2).broadcast_to([128, NT, 16]), INV.unsqueeze(1).broadcast_to([128, NT, 16]), ALU.mult)
        TWO_PI = 2.0 * math.pi

        def reduce_to_pi(dst, src):
            ts(A2, src, 1.0 / TWO_PI, ALU.mult)
            cp(KI, A2)
            cp(A2, KI)
            stt(dst, A2, -TWO_PI, src, ALU.mult, ALU.add)
            ts(A2, dst, math.pi, ALU.is_gt, -TWO_PI, ALU.mult)
            tt(dst, dst, A2, ALU.add)
            ts(A2, dst, -math.pi, ALU.is_lt, TWO_PI, ALU.mult)
            tt(dst, dst, A2, ALU.add)

        SINr = av(O_SIN, 256, F32)
        COSr = av(O_COS, 256, F32)
        reduce_to_pi(SINr, ANG)
        ts(ANG, ANG, math.pi / 2.0, ALU.add)
        reduce_to_pi(COSr, ANG)
        act(SINr, SINr, AF.Sin)
        act(COSr, COSr, AF.Sin)

    def prep_norm(gcols):
        for j in range(NJ):
            cols = slice(j * 512, (j + 1) * 512)
            for kc in range(8):
                act(XSQ[:, kc, :], XT[:, kc, cols], AF.Square)
            b = bank()
            for kc in range(8):
                mm(PS[b][:, :], ONES, XSQ[:, kc, :], start=(kc == 0), stop=(kc == 7))
            rb = RBt[j % 2]
            rsqrt_to(rb, PS[b][:, :], 1.0 / D, TMP[j % 2])
            for kc in range(8):
                stt(XB[:, kc, cols], XT[:, kc, cols], gcols[:, kc:kc + 1], rb, ALU.mult, ALU.mult)

    PPT = av(O_PPT, 2 * S).rearrange("p (k t) -> p k t", k=2)

    def ple(i):
        PST = [av(O_PST, 256), av(O_PST, 256)]
        for t in range(NT):
            pst = PST[t % 2]
            dma(pst, p_d[i, t * 128:(t + 1) * 128, :], eng='pool')
            b = bank()
            for kc in range(2):
                tr(PSB[b][:, kc * 128:(kc + 1) * 128], pst[:, kc * 128:(kc + 1) * 128], IDENT)
            cp(PPT[:, :, t * 128:(t + 1) * 128], PSB[b][:, 0:256].rearrange("p (k t) -> p k t", k=2),
               eng='act' if t % 2 else 'dve')
        for n in range(8):
            wg = ring_load(ple_wg_d[i, n])
            wp = ring_load(ple_wp_d[i, n])
            for j in range(NJ):
                cols = slice(j * 512, (j + 1) * 512)
                bg = bank()
                for kc in range(8):
                    mm(PS[bg][:, :], wg[:, kc, :], XB[:, kc, cols], start=(kc == 0), stop=(kc == 7))
                tmp = TMP[j % 2]
                act(tmp, PS[bg][:, :], AF.Sigmoid)
                bp = bank()
                for kc in range(2):
                    mm(PS[bp][:, :], wp[:, kc, :], PPT[:, kc, cols], start=(kc == 0), stop=(kc == 1))
                tt(tmp, tmp, PS[bp][:, :], ALU.mult)
                tt(XT[:, n, cols], XT[:, n, cols], tmp, ALU.add)

    SG = av(R1, 8 * S).rearrange("p (k t) -> p k t", k=8)
    CQB = av(O_CQB, 3 * S).rearrange("p (k t) -> p k t", k=3)
    CKVB = av(O_CKVB, 2 * S).rearrange("p (k t) -> p k t", k=2)
    KRRAW = av(O_KRRAW, NT * 32, F32).rearrange("p (t i) -> p t i", t=NT)
    KROPE = av(O_KROPE, NT * 32, F32).rearrange("p (t i) -> p t i", t=NT)
    GAM = av(O_GAM, 256, F32)
    DEL = av(O_DEL, 256, F32)
    DELN = av(O_DELN, 256, F32)
    GQ2 = av(O_GQ2, 192, F32)
    GKN2 = av(O_GKN2, 128, F32)
    GKR = av(O_GKR, 32, F32)
    BQ = STAT[:, 0:16]
    BKV = STAT[:, 16:32]
    SSKR = STAT[:, 32:48]
    ST1 = STAT[:, 48:64]
    ST2 = STAT[:, 64:80]

    def mla_layer(li):
        dma(GCOL[:, 0:13], mla_g_d[li])
        if stage == 1:
            return
        prep_norm(GCOL[:, 0:8])
        if stage == 1.5:
            return
        dma(av(O_GQ2, 352, F32), mla_hg_d[li])
        ts(GQ2, GQ2, 96.0 ** -0.5, ALU.mult)
        if stage < 2:
            return
        WA = av(O_WA, 8 * 672).rearrange("p (k n) -> p k n", k=8)
        for kc in range(8):
            dma(WA[:, kc, :], mla_wa_d[li, :, kc, :], eng='pool')
        SQT = [av(o, 512) for o in O_SQT]
        for j in range(NJ):
            cols = slice(j * 512, (j + 1) * 512)
            for (nblk, c0, dstT, gofs, statcol) in ((3, 0, CQB, 8, 0), (2, 384, CKVB, 11, 1)):
                for cb in range(nblk):
                    b = bank()
                    for kc in range(8):
                        mm(PS[b][:, :], WA[:, kc, c0 + cb * 128:c0 + (cb + 1) * 128], XB[:, kc, cols],
                           start=(kc == 0), stop=(kc == 7))
                    ts(dstT[:, cb, cols], PS[b][:, :], GCOL[:, gofs + cb:gofs + cb + 1], ALU.mult)
                    if stage >= 2.2:
                        act(SQT[cb], PS[b][:, :], AF.Square)
                for t4 in range(4):
                    if stage < 2.3:
                        break
                    t = j * 4 + t4
                    for cb in range(nblk):
                        mm(PS[4][:, statcol * 16 + t:statcol * 16 + t + 1], SQT[cb][:, t4 * 128:(t4 + 1) * 128],
                           ONES[:, 0:1], start=(cb == 0), stop=(cb == nblk - 1))
            for t4 in range(4):
                if stage < 2.4:
                    break
                t = j * 4 + t4
                for kc in range(8):
                    mm(PS[5][:, t * 32:(t + 1) * 32], XB[:, kc, t * 128:(t + 1) * 128], WA[:, kc, 640:672],
                       start=(kc == 0), stop=(kc == 7))
        if stage < 2.5:
            return
        cp(KRRAW, PS[5][:, :].rearrange("p (t i) -> p t i", t=NT))
        if stage < 2.6:
            return
        rsqrt_to(BQ, PS[4][:, 0:16], 1.0 / 384, ST1)
        rsqrt_to(BKV, PS[4][:, 16:32], 1.0 / 256, ST2)
        if stage < 3:
            return
        for cb in range(8):
            w = ring_load(mla_wg_d[li, cb])
            for j in range(NJ):
                cols = slice(j * 512, (j + 1) * 512)
                b = bank()
                for kc in range(8):
                    mm(PS[b][:, :], w[:, kc, :], XB[:, kc, cols], start=(kc == 0), stop=(kc == 7))
                act(SG[:, cb, cols], PS[b][:, :], AF.Silu)
        if stage < 4:
            return
        KRSQ = av(O_QR, NT * 32, F32).rearrange("p (t i) -> p t i", t=NT)
        tt(KRSQ, KRRAW, KRRAW, ALU.mult)
        red(SSKR, KRSQ)
        KRG = av(O_QR + 2048, NT * 32, F32).rearrange("p (t i) -> p t i", t=NT)
        tt(KRG, KRRAW, GKR.unsqueeze(1).broadcast_to([128, NT, 32]), ALU.mult)

        def rope(dst1, dst2, x1, x2, cosb, sinb, t1, t2):
            tt(t1, x1, cosb, ALU.mult)
            tt(t2, x2, sinb, ALU.mult)
            tt(dst1, t1, t2, ALU.subtract)
            tt(t1, x2, cosb, ALU.mult)
            tt(t2, x1, sinb, ALU.mult)
            tt(dst2, t1, t2, ALU.add)

        RT1 = av(O_ANG, 256, F32).rearrange("p (t i) -> p t i", t=NT)
        RT2 = av(O_ANG + 1024, 256, F32).rearrange("p (t i) -> p t i", t=NT)
        rope(KROPE[:, :, 0:16], KROPE[:, :, 16:32], KRG[:, :, 0:16], KRG[:, :, 16:32], COS, SIN, RT1, RT2)
        if stage < 5:
            return
        WQ = av(O_WQ, 3 * 1536).rearrange("p (k n) -> p k n", k=3)
        WKV = av(O_WKV, 2 * 2048).rearrange("p (k n) -> p k n", k=2)
        for kc in range(3):
            dma(WQ[:, kc, :], mla_wq_d[li, :, kc, :], eng='pool')
        for kc in range(2):
            dma(WKV[:, kc, :], mla_wkv_d[li, :, kc, :], eng='pool')
        GAM3 = GAM.rearrange("p (t h) -> p t h", t=NT)
        DEL3 = DEL.rearrange("p (t h) -> p t h", t=NT)
        DELN3 = DELN.rearrange("p (t h) -> p t h", t=NT)
        for t in range(NT):
            tcols = slice(t * 128, (t + 1) * 128)
            for qb in range(4):
                b = bank()
                for kc in range(3):
                    mm(PS[b][:, 0:384], CQB[:, kc, tcols], WQ[:, kc, qb * 384:(qb + 1) * 384],
                       start=(kc == 0), stop=(kc == 2))
                tmp = TMP[qb % 2]
                act(tmp[:, 0:384], PS[b][:, 0:384], AF.Square)
                red(GAM3[:, t, qb * 4:(qb + 1) * 4], tmp[:, 0:384].rearrange("p (h d) -> p h d", h=4))
            for kb4 in range(4):
                b = bank()
                for kc in range(2):
                    mm(PS[b][:, :], CKVB[:, kc, tcols], WKV[:, kc, kb4 * 512:(kb4 + 1) * 512],
                       start=(kc == 0), stop=(kc == 1))
                tmp = TMP[kb4 % 2]
                act(tmp[:, 0:256].rearrange("p (h d) -> p h d", h=4),
                    PS[b][:, :].rearrange("p (h d) -> p h d", h=4)[:, :, 0:64], AF.Square)
                red(DEL3[:, t, kb4 * 4:(kb4 + 1) * 4], tmp[:, 0:256].rearrange("p (h d) -> p h d", h=4))
        BQb = BQ.unsqueeze(2).broadcast_to([128, NT, H])
        BKVb = BKV.unsqueeze(2).broadcast_to([128, NT, H])
        T256 = av(O_ANG, 256, F32)
        T256b = av(O_ANG + 1024, 256, F32)
        T3 = T256.rearrange("p (t h) -> p t h", t=NT)
        tt(T3, GAM3, BQb, ALU.mult)
        tt(T3, T3, BQb, ALU.mult)
        rsqrt_to(GAM, T256, 1.0 / 96, T256b)
        tt(GAM3, GAM3, BQb, ALU.mult)
        tt(T3, DEL3, BKVb, ALU.mult)
        tt(T3, T3, BKVb, ALU.mult)
        tt(T3, T3, SSKR.unsqueeze(2).broadcast_to([128, NT, H]), ALU.add)
        rsqrt_to(DEL, T256, 1.0 / 96, T256b)
        tt(DELN3, DEL3, BKVb, ALU.mult)
        if stage < 6:
            return
        QT = [av(o, S) for o in O_QT]
        KT = [av(o, S) for o in O_KT]
        VA = av(O_VA, NT * 256).rearrange("p (t c) -> p t c", t=NT)
        QN = av(O_QN, NT * 192).rearrange("p (t c) -> p t c", t=NT)
        QR = av(O_QR, NT * 64, F32).rearrange("p (t h i) -> p t h i", t=NT, h=2)
        memset(VA[:, :, 64:192], 0.0)
        memset(VA[:, :, 64:65], 1.0)
        memset(VA[:, :, 128:129], 1.0)
        PT = [av(o, 512) for o in O_PT]
        REC = av(O_REC, 512, F32)
        RECH = av(O_RECH, 512)
        RECL = av(O_RECL, 512)
        pti = [0]
        COSb = COS.unsqueeze(2).broadcast_to([128, NT, 2, 16])
        SINb = SIN.unsqueeze(2).broadcast_to([128, NT, 2, 16])
        for c in range(8):
            QN4 = QN.rearrange("p t (h d) -> p t h d", h=2)
            for t in range(NT):
                tcols = slice(t * 128, (t + 1) * 128)
                b = bank()
                for kc in range(3):
                    mm(PS[b][:, 0:192], CQB[:, kc, tcols], WQ[:, kc, c * 192:(c + 1) * 192],
                       start=(kc == 0), stop=(kc == 2))
                tmp = TMP[t % 2]
                t3 = tmp[:, 0:192].rearrange("p (h d) -> p h d", h=2)
                tt(t3, PS[b][:, 0:192].rearrange("p (h d) -> p h d", h=2),
                   GAM3[:, t, 2 * c:2 * c + 2].unsqueeze(2).broadcast_to([128, 2, 96]), ALU.mult)
                g3 = GQ2.rearrange("p (h d) -> p h d", h=2)
                tt(QN4[:, t, :, 0:64], t3[:, :, 0:64], g3[:, :, 0:64], ALU.mult)
                tt(QR[:, t, :, :], t3[:, :, 64:96], g3[:, :, 64:96], ALU.mult)
            RA1 = av(O_ANG, 512, F32).rearrange("p (t h i) -> p t h i", t=NT, h=2)
            RA2 = av(O_KRRAW, 512, F32).rearrange("p (t h i) -> p t h i", t=NT, h=2)
            rope(QN4[:, :, :, 64:80], QN4[:, :, :, 80:96], QR[:, :, :, 0:16], QR[:, :, :, 16:32], COSb, SINb, RA1, RA2)
            for hh in range(2):
                for half in range(2):
                    b = bank()
                    for t8 in range(8):
                        t = half * 8 + t8
                        tr(PSB[b][0:96, t8 * 128:(t8 + 1) * 128], QN[:, t, hh * 96:(hh + 1) * 96], IDENT)
                    cp(QT[hh][0:96, half * 1024:(half + 1) * 1024], PSB[b][0:96, :], eng='act' if half else 'dve')
            for t in range(NT):
                tcols = slice(t * 128, (t + 1) * 128)
                b = bank()
                for kc in range(2):
                    mm(PS[b][:, 0:256], CKVB[:, kc, tcols], WKV[:, kc, c * 256:(c + 1) * 256],
                       start=(kc == 0), stop=(kc == 1))
                p3 = PS[b][:, 0:256].rearrange("p (h d) -> p h d", h=2)
                tmp = TMP[t % 2]
                t3 = tmp[:, 0:128].rearrange("p (h d) -> p h d", h=2)
                tt(t3, p3[:, :, 0:64], DELN3[:, t, 2 * c:2 * c + 2].unsqueeze(2).broadcast_to([128, 2, 64]), ALU.mult)
                tt(QN4[:, t, :, 0:64], t3, GKN2.rearrange("p (h d) -> p h d", h=2), ALU.mult)
                vdst = VA[:, t, :].rearrange("p (a c) -> p a c", a=4)[:, 0:4:3, :]
                act(vdst, p3[:, :, 64:128], AF.Copy, scale=BKV[:, t:t + 1])
            tt(QN4[:, :, :, 64:96], KROPE.unsqueeze(2).broadcast_to([128, NT, 2, 32]),
               DEL3[:, :, 2 * c:2 * c + 2].unsqueeze(3).broadcast_to([128, NT, 2, 32]), ALU.mult)
            for hh in range(2):
                for half in range(2):
                    b = bank()
                    for t8 in range(8):
                        t = half * 8 + t8
                        tr(PSB[b][0:96, t8 * 128:(t8 + 1) * 128], QN[:, t, hh * 96:(hh + 1) * 96], IDENT)
                    cp(KT[hh][0:96, half * 1024:(half + 1) * 1024], PSB[b][0:96, :], eng='act' if half else 'dve')
            PT4 = [[PT[0], PT[1]], [PT[2], av(O_RB[0], 512)]]
            for j in range(NJ):
                nkb = 4 * j + 4
                cols = slice(j * 512, (j + 1) * 512)

                def s_mm(hh, kb):
                    r = kb - 4 * j
                    c0 = max(0, r) * 128
                    zb = 2 * hh + kb % 2
                    qs = slice(j * 512 + c0, (j + 1) * 512)
                    mm(PS[zb][:, c0:512], KT[hh][0:96, kb * 128:(kb + 1) * 128], QT[hh][0:96, qs],
                       start=True, stop=(r < 0))
                    if r >= 0:
                        mm(PS[zb][:, c0:c0 + 128], IDENT, MASKM, start=False, stop=True, skip=True)

                for hh in range(2):
                    s_mm(hh, 0)
                for kb in range(nkb):
                    r = kb - 4 * j
                    c0 = max(0, r) * 128
                    if kb + 1 < nkb:
                        for hh in range(2):
                            s_mm(hh, kb + 1)
                    for hh in range(2):
                        act(PT4[hh][kb % 2][:, c0:512], PS[2 * hh + kb % 2][:, c0:512], AF.Exp)
                    for hh in range(2):
                        pt = PT4[hh][kb % 2]
                        if hh == 0:
                            mm(PS[4][0:65, c0:512], VA[:, kb, 0:65], pt[:, c0:512],
                               start=(kb == 0), stop=(kb == nkb - 1), skip=True)
                        else:
                            mm(PS[5][:, c0:512], VA[:, kb, 128:256], pt[:, c0:512],
                               start=(kb == 0), stop=(kb == nkb - 1), skip=True)
                for hh in range(2):
                    ob = 4 + hh
                    if hh == 0:
                        drow, rows = 64, slice(0, 64)
                        lsel = BSEL[64:65, 0:64]
                    else:
                        drow, rows = 0, slice(64, 128)
                        lsel = BSEL[0:1, :]
                    rr = slice(drow, drow + 1)
                    recip(REC[rr, :], PS[ob][rr, :])
                    cp(RECH[rr, :], REC[rr, :])
                    tt(RECL[rr, :], REC[rr, :], RECH[rr, :], ALU.subtract)
                    bb = bank()
                    orow = slice(0, 64) if hh == 0 else slice(0, 128)
                    mm(PS[bb][orow, :], lsel, RECH[rr, :], start=True, stop=False)
                    mm(PS[bb][orow, :], lsel, RECL[rr, :], start=False, stop=True)
                    tmp = TMP[hh]
                    tt(tmp[rows, :], PS[bb][rows, :], SG[rows, c, cols], ALU.mult)
                    tt(SG[rows, c, cols], PS[ob][rows, :], tmp[rows, :], ALU.mult)
        for n in range(8):
            w = ring_load(mla_wo_d[li, n])
            for j in range(NJ):
                cols = slice(j * 512, (j + 1) * 512)
                b = bank()
                for kc in range(8):
                    mm(PS[b][:, :], w[:, kc, :], SG[:, kc, cols], start=(kc == 0), stop=(kc == 7))
                tt(XT[:, n, cols], XT[:, n, cols], PS[b][:, :], ALU.add)
                cp(XB[:, n, cols], XT[:, n, cols], eng='act')

    KSH = av(R1, 8 * S).rearrange("p (k t) -> p k t", k=8)
    VSH = av(O_VSH, NT * 1024).rearrange("p (t c) -> p t c", t=NT)

    def shared_kv():
        dma(GCOL[:, 0:8], kv_g_d)
        prep_norm(GCOL[:, 0:8])
        for n in range(8):
            w = ring_load(kv_wk_d[n])
            for j in range(NJ):
                cols = slice(j * 512, (j + 1) * 512)
                b = bank()
                for kc in range(8):
                    mm(PS[b][:, :], w[:, kc, :], XB[:, kc, cols], start=(kc == 0), stop=(kc == 7))
                cp(KSH[:, n, cols], PS[b][:, :], eng='act' if j % 2 else 'dve')
        for n in range(8):
            w = ring_load(kv_wv_d[n])
            for t in range(NT):
                b = bank()
                for kc in range(8):
                    mm(PS[b][:, 0:128], XB[:, kc, t * 128:(t + 1) * 128], w[:, kc, :], start=(kc == 0), stop=(kc == 7))
                cp(VSH[:, t, n * 128:(n + 1) * 128], PS[b][:, 0:128], eng='act' if t % 2 else 'dve')

    def sb_layer(lj):
        dma(GCOL[:, 0:8], sb_g_d[lj])
        prep_norm(GCOL[:, 0:8])
        QS = [av(R3 + k * 1024, 512) for k in range(4)]
        SGC = [av(R3 + 4096 + k * 1024, 512) for k in range(4)]
        OG = [av(R3 + 8192 + k * 1024, 512) for k in range(4)]
        SP = [av(R3 + 12288 + k * 1024, 512) for k in range(4)]
        SPACC = [av(R3 + 16384 + k * 1024, 512) for k in range(4)]
        AT = [av(R3 + 20480 + k * 1024, 512) for k in range(4)]
        E = [av(O_TMP[0], 512, F32), av(O_TMP[1], 512, F32), av(O_RB[0], 512, F32), av(O_ANG, 512, F32)]
        for c in range(8):
            wq = ring_load(sb_wq_d[lj, c])
            wg = ring_load(sb_wgt_d[lj, c])
            wo = av(O_RING[ring_i[0] % 3], 1024)
            ring_i[0] += 1
            dma(wo, sb_wo_d[lj, c], eng='pool')
            for j in (3, 2, 1, 0):
                cols = slice(j * 512, (j + 1) * 512)
                b = bank()
                for kc in range(8):
                    mm(PS[b][:, :], wq[:, kc, :], XB[:, kc, cols], start=(kc == 0), stop=(kc == 7))
                ts(QS[j], PS[b][:, :], 0.125, ALU.mult)
                b = bank()
                for kc in range(8):
                    mm(PS[b][:, :], wg[:, kc, :], XB[:, kc, cols], start=(kc == 0), stop=(kc == 7))
                act(SGC[j], PS[b][:, :], AF.Silu)
            slots = []
            for k in range(4):
                seq = (3, 0) if k // 2 == 0 else (2, 1)
                blocks = [(j, kb) for j in seq for kb in range(4 * j + 3, -1, -1)]
                slots.append((k % 2, k, 4 + k // 2, blocks))
            todo_wo = []

            def flush_wo():
                for j in todo_wo:
                    cols = slice(j * 512, (j + 1) * 512)
                    for n in range(8):
                        b = bank()
                        mm(PS[b][:, :], wo[:, n * 128:(n + 1) * 128], OG[j], start=True, stop=True)
                        tt(XT[:, n, cols], XT[:, n, cols], PS[b][:, :], ALU.add)
                del todo_wo[:]

            for r in range(20):
                info = []
                for (hh, zb, ob, blocks) in slots:
                    j, kb = blocks[r]
                    rr = kb - 4 * j
                    info.append((hh, zb, ob, j, kb, rr, max(0, rr) * 128, kb == 4 * j + 3, slice(hh * 64, hh * 64 + 64)))
                for k, (hh, zb, ob, j, kb, rr, c0, first, rows) in enumerate(info):
                    mm(PS[zb][:, c0:512], KSH[rows, c, kb * 128:(kb + 1) * 128], QS[j][rows, c0:512],
                       start=True, stop=(rr < 0))
                    if rr >= 0:
                        mm(PS[zb][:, c0:c0 + 128], IDENT, MASKS, start=False, stop=True, skip=True)
                for k, (hh, zb, ob, j, kb, rr, c0, first, rows) in enumerate(info):
                    act(E[k][:, c0:512], PS[zb][:, c0:512], AF.Exp)
                for k, (hh, zb, ob, j, kb, rr, c0, first, rows) in enumerate(info):
                    act(SP[k][:, c0:512], E[k][:, c0:512], AF.Ln, bias=ONEC)
                for k, (hh, zb, ob, j, kb, rr, c0, first, rows) in enumerate(info):
                    mm(PS[zb][:, c0:512], NEGU, SP[k][:, c0:512], start=False, stop=first, skip=True)
                    if not first:
                        a0 = c0 + 128 if rr >= 0 else 0
                        mm(PS[zb][:, a0:512], NEGONES, SPACC[k][:, a0:512], start=False, stop=True, skip=True)
                for k, (hh, zb, ob, j, kb, rr, c0, first, rows) in enumerate(info):
                    if kb > 0:
                        if first:
                            cp(SPACC[k][:, c0:512], SP[k][:, c0:512])
                        elif rr >= 0:
                            pc0 = c0 + 128
                            cp(SPACC[k][:, c0:pc0], SP[k][:, c0:pc0])
                            tt(SPACC[k][:, pc0:512], SPACC[k][:, pc0:512], SP[k][:, pc0:512], ALU.add)
                        else:
                            tt(SPACC[k][:, :], SPACC[k][:, :], SP[k][:, :], ALU.add)
                for k, (hh, zb, ob, j, kb, rr, c0, first, rows) in enumerate(info):
                    act(AT[k][:, c0:512], PS[zb][:, c0:512], AF.Exp)
                for k, (hh, zb, ob, j, kb, rr, c0, first, rows) in enumerate(info):
                    mm(PS[ob][rows, c0:512], VSH[:, kb, (2 * c + hh) * 64:(2 * c + hh + 1) * 64], AT[k][:, c0:512],
                       start=first, stop=(kb == 0), skip=True)
                flush_wo()
                for k, (hh, zb, ob, j, kb, rr, c0, first, rows) in enumerate(info):
                    if kb == 0 and hh == 1:
                        tt(OG[j], PS[ob][:, :], SGC[j], ALU.mult)
                        todo_wo.append(j)
            flush_wo()
        for n in range(8):
            for j in range(NJ):
                cols = slice(j * 512, (j + 1) * 512)
                cp(XB[:, n, cols], XT[:, n, cols], eng='act')

    ONEC = STAT[:, 94:95]
    memset(ONEC, 1.0)

    for i in (layers if layers is not None else range(n_layers)):
        if stage < 1:
            break
        if i < 2:
            mla_layer(i)
        else:
            sb_layer(i - 2)
        if stage >= 8:
            ple(i)
        if i == 1 and n_layers > 2:
            shared_kv()

    if dbg is not None:
        name = dbg[0]
        if name == 'cos':
            dma(dbg_d[:, 0:256], av(O_COS, 256, F32))
            dma(dbg_d[:, 256:512], av(O_SIN, 256, F32))

    YO = av(O_XIN, 1024, F32)
    for t in range(NT):
        for half in range(2):
            b = 4 + half
            for q4 in range(4):
                kc = half * 4 + q4
                tr(PS[b][:, q4 * 128:(q4 + 1) * 128], XT[:, kc, t * 128:(t + 1) * 128], IDENTF)
            cp(YO[:, half * 512:(half + 1) * 512], PS[b][:, :], eng='act' if half else 'dve')
        dma(y_d[t * 128:(t + 1) * 128, :], YO)
    P.wait_all_dma('sp')
    P.emit(st)
    st.close()
    return nc, P


def _consts():
    j = np.arange(128)[:, None]
    s = np.arange(128)[None, :]
    cb = np.zeros((128, N_CONSTB, 128), np.float32)
    cb[:, C_IDENT] = (j == s)
    cb[:, C_NEGU] = -(j >= s).astype(np.float32)
    cb[:, C_NEGONES] = -1.0
    cb[:, C_ONES] = 1.0
    cb[:, C_MASKM] = np.where(s >= j, 0.0, NEG)
    cb[:, C_MASKS] = np.where(s > j, 0.0, NEG)
    bs = np.zeros((128, 128), np.float32)
    bs[64, 0:64] = 1.0
    bs[0, 64:128] = 1.0
    cb[:, C_BSEL] = bs
    cf = np.zeros((128, 144), np.float32)
    cf[:, 0:128] = np.eye(128, dtype=np.float32)
    half = 16
    inv = (1.0 / (np.float32(10000.0) ** (np.arange(half, dtype=np.float32) / np.float32(half)))).astype(np.float32)
    cf[:, 128:144] = inv[None, :]
    return cb.reshape(128, -1), cf


def _kt(w):
    K, N = w.shape
    return np.ascontiguousarray(w.reshape(K // 128, 128, N).transpose(1, 0, 2))


def _blk(w, nb=128):
    K, N = w.shape
    a = w.reshape(K // 128, 128, N // nb, nb).transpose(2, 1, 0, 3)
    return np.ascontiguousarray(a)


def _gcol(g):
    return np.ascontiguousarray(g.reshape(-1, 128).T)


def make_in_maps(inputs):
    f = lambda k: np.asarray(inputs[k], dtype=np.float32)
    cb, cf = _consts()
    mla_w_in = f("mla_w_in")
    shared = {
        "constb": cb, "constf": cf,
        "mla_gcol": np.stack([np.concatenate([_gcol(f("mla_ln_g")[l]), _gcol(f("mla_q_norm_g")[l]),
                                              _gcol(f("mla_kv_norm_g")[l])], axis=1) for l in range(2)]),
        "mla_wa": np.stack([_kt(mla_w_in[l][:, 0:672]) for l in range(2)]),
        "mla_wgate": np.stack([_blk(mla_w_in[l][:, 672:1696]) for l in range(2)]),
        "mla_wq": np.stack([_kt(f("mla_w_q_up")[l]) for l in range(2)]),
        "mla_wkv": np.stack([_kt(f("mla_w_kv_up")[l]) for l in range(2)]),
        "mla_hg": np.stack([np.ascontiguousarray(np.broadcast_to(np.concatenate(
            [f("mla_q_head_g")[l], f("mla_q_head_g")[l], f("mla_k_head_g")[l][:64], f("mla_k_head_g")[l][:64],
             f("mla_k_head_g")[l][64:]])[None, :], (128, 352))) for l in range(2)]),
        "mla_wo": np.stack([_blk(f("mla_w_out")[l]) for l in range(2)]),
        "kv_gcol": _gcol(f("kv_ln_g")),
        "kv_wk": _blk(f("w_kv_shared")[:, 0:1024]),
        "kv_wv": _blk(f("w_kv_shared")[:, 1024:2048]),
        "sb_gcol": np.stack([_gcol(f("sb_ln_g")[l]) for l in range(2)]),
        "sb_wq": np.stack([_blk(f("sb_w_in")[l][:, 0:1024]) for l in range(2)]),
        "sb_wgate": np.stack([_blk(f("sb_w_in")[l][:, 1024:2048]) for l in range(2)]),
        "sb_wo": np.ascontiguousarray(f("sb_w_out").reshape(2, 8, 128, 1024)),
        "ple_wg": np.stack([_blk(f("ple_w_gate")[l]) for l in range(4)]),
        "ple_wp": np.stack([_blk(f("ple_w_proj")[l]) for l in range(4)]),
    }
    x = f("x")
    p = f("p")
    pos = np.asarray(inputs["positions"]).astype(np.int32)
    maps = []
    for b in range(8):
        m = dict(shared)
        m["x"] = np.ascontiguousarray(x[b])
        m["p"] = np.ascontiguousarray(p[:, b])
        m["pos"] = np.ascontiguousarray(pos[b].reshape(NT, 128).T)
        maps.append(m)
    return maps


_CACHE = {}


def kernel(**inputs):
    maps = make_in_maps(inputs)
    if "nc" not in _CACHE:
        _CACHE["nc"] = build_program(4)[0]
    res = run_bass_kernel_spmd(_CACHE["nc"], maps, core_ids=list(range(8)))
    return np.stack([np.asarray(r["y"], dtype=np.float32) for r in res.results], axis=0)
```

```python
import math
from contextlib import ExitStack

import numpy as np
import concourse.bass as bass
import concourse.mybir as mybir
from concourse.bass_utils import run_bass_kernel_spmd

F32 = mybir.dt.float32
BF16 = mybir.dt.bfloat16
I32 = mybir.dt.int32
AF = mybir.ActivationFunctionType
ALU = mybir.AluOpType
AX = mybir.AxisListType

S = 2048
D = 1024
NT = 16
NJ = 4
H = 16
EPS = 1e-6
NEG = -30000.0

_ESZ = {}


def esize(dt):
    if dt not in _ESZ:
        _ESZ[dt] = mybir.dt.size(dt)
    return _ESZ[dt]


def region(ap):
    t = ap.tensor
    es = esize(ap.dtype)
    dims = ap.ap
    off = ap.offset
    if type(t).__name__.startswith('DRam'):
        lo = hi = off
        for st, cnt in dims:
            d = st * (cnt - 1)
            if d < 0:
                lo += d
            else:
                hi += d
        return (t.name, 0, 1, lo * es, (hi + 1) * es)
    if t.name.startswith('ps'):
        return (t.name, 0, 128, 0, 2048)
    pstep, pcnt = dims[0]
    p0 = off // pstep
    c0 = off - p0 * pstep
    lo = hi = c0
    for st, cnt in dims[1:]:
        d = st * (cnt - 1)
        if d < 0:
            lo += d
        else:
            hi += d
    return (t.name, p0, p0 + pcnt, lo * es, (hi + 1) * es)


class Op:
    __slots__ = ('eng', 'fn', 'seq', 'waits', 'signal', 'dma', 'snap', 'sigcount')

    def __init__(self, eng, fn):
        self.eng = eng
        self.fn = fn
        self.waits = []
        self.signal = False
        self.dma = None
        self.snap = None


class Prog:
    ENGS = ('pe', 'act', 'dve', 'pool', 'sp')

    def __init__(self, nc, n_dma_sems=32):
        self.nc = nc
        self.ops = {e: [] for e in self.ENGS}
        self.track = {}
        self.seen = {e: {f: -1 for f in self.ENGS} for e in self.ENGS}
        self.seen_dma = {e: {} for e in self.ENGS}
        self.n_dma_sems = n_dma_sems
        self.dma_sem_val = [0] * n_dma_sems
        h = n_dma_sems // 2
        self.dma_pool = {'sp': list(range(0, h)), 'act': list(range(0, h)), 'pool': list(range(h, n_dma_sems))}
        self.dma_rr = {'sp': 0, 'act': 0, 'pool': 0}
        self.nops = 0

    def _deps(self, regs_r, regs_w, eng=None):
        deps = []
        for (key, p0, p1, b0, b1) in regs_r:
            lst = self.track.get(key)
            if lst:
                psum = key.startswith('ps')
                for ent in lst:
                    if ent[5] and ent[0] < p1 and p0 < ent[1] and ent[2] < b1 and b0 < ent[3]:
                        deps.append(ent[4])
                    elif psum and (not ent[5]) and ent[4].eng != eng:
                        deps.append(ent[4])
        for (key, p0, p1, b0, b1) in regs_w:
            lst = self.track.get(key)
            if lst:
                for ent in lst:
                    if ent[0] < p1 and p0 < ent[1] and ent[2] < b1 and b0 < ent[3]:
                        deps.append(ent[4])
        return deps

    def _record(self, op, regs_r, regs_w):
        for (key, p0, p1, b0, b1) in regs_w:
            lst = self.track.setdefault(key, [])
            lst[:] = [e for e in lst if not (p0 <= e[0] and e[1] <= p1 and b0 <= e[2] and e[3] <= b1)]
            lst.append([p0, p1, b0, b1, op, True])
        for (key, p0, p1, b0, b1) in regs_r:
            lst = self.track.setdefault(key, [])
            lst[:] = [e for e in lst if not ((not e[5]) and e[4].eng == op.eng and e[4].dma is None
                                             and op.dma is None
                                             and p0 <= e[0] and e[1] <= p1 and b0 <= e[2] and e[3] <= b1)]
            lst.append([p0, p1, b0, b1, op, False])

    def add(self, eng, fn, reads=(), writes=(), dma=False):
        op = Op(eng, fn)
        op.seq = len(self.ops[eng])
        regs_r = [region(a) for a in reads]
        regs_w = [region(a) for a in writes]
        deps = self._deps(regs_r, regs_w, eng)
        seen = self.seen[eng]
        seen_d = self.seen_dma[eng]
        need_e = {}
        need_d = {}
        for d in deps:
            if d.dma is not None:
                s, v = d.dma
                if seen_d.get(s, 0) < v and need_d.get(s, 0) < v:
                    need_d[s] = v
            else:
                if d.eng == 'pe' and eng == 'pe':
                    continue
                if d.seq > seen[d.eng]:
                    if d.eng not in need_e or need_e[d.eng].seq < d.seq:
                        need_e[d.eng] = d
        if dma:
            pool = self.dma_pool[eng]
            s = pool[self.dma_rr[eng] % len(pool)]
            self.dma_rr[eng] += 1
            prev = self.dma_sem_val[s]
            if prev > 0 and seen_d.get(s, 0) < prev and need_d.get(s, 0) < prev:
                need_d[s] = prev
            self.dma_sem_val[s] = prev + 16
            op.dma = (s, prev + 16)
        for f, d in need_e.items():
            d.signal = True
            op.waits.append(('e', f, d))
            seen[f] = max(seen[f], d.seq)
            if d.snap is not None:
                se, sd = d.snap
                for g, v in se.items():
                    if v > seen[g]:
                        seen[g] = v
                for g, v in sd.items():
                    if v > seen_d.get(g, 0):
                        seen_d[g] = v
        for s, v in need_d.items():
            op.waits.append(('d', s, v))
            seen_d[s] = max(seen_d.get(s, 0), v)
        if not dma:
            op.snap = (dict(seen), dict(seen_d))
        self.ops[eng].append(op)
        self._record(op, regs_r, regs_w)
        self.nops += 1
        return op

    def wait_all_dma(self, eng='sp'):
        op = Op(eng, None)
        op.seq = len(self.ops[eng])
        for s in range(self.n_dma_sems):
            if self.dma_sem_val[s] > 0:
                op.waits.append(('d', s, self.dma_sem_val[s]))
        self.ops[eng].append(op)

    def emit(self, stack):
        nc = self.nc
        esem = {e: stack.enter_context(nc.semaphore("s_" + e)) for e in self.ENGS}
        dsem = [stack.enter_context(nc.semaphore("d_%d" % i)) for i in range(self.n_dma_sems)]
        for e in self.ENGS:
            c = 0
            for op in self.ops[e]:
                if op.signal:
                    c += 1
                op.sigcount = c
        block = stack.enter_context(nc.Block())
        engmap = {'pe': block.tensor, 'act': block.scalar, 'dve': block.vector,
                  'pool': block.gpsimd, 'sp': block.sync}

        def make(e):
            ops = self.ops[e]

            def body(eng):
                for op in ops:
                    for w in op.waits:
                        if w[0] == 'e':
                            eng.wait_ge(esem[w[1]], w[2].sigcount)
                        else:
                            eng.wait_ge(dsem[w[1]], w[2])
                    if op.fn is None:
                        continue
                    ins = op.fn(eng)
                    if op.dma is not None:
                        ins.then_inc(dsem[op.dma[0]], 16)
                    elif op.signal:
                        ins.then_inc(esem[e], 1)
            return body

        for e in self.ENGS:
            if self.ops[e]:
                engmap[e](make(e))


R0 = 0
R1 = 32768
R2 = 65536
R3 = 98304
R4 = 122880
O_QT = [R0 + 0, R0 + 4096]
O_KT = [R0 + 8192, R0 + 12288]
O_VA = R0 + 16384
O_QN = R0 + 24576
O_CQB = R2
O_CKVB = R2 + 12288
O_WA = R2 + 20480
O_WQ = R2 + 20480
O_PPT = R3
O_VSH = R2
O_WKV = R3
O_XSQ = R3
O_KROPE = R3 + 8192
O_QR = R3 + 10240
O_GAM = R3 + 14336
O_DEL = R3 + 15360
O_DELN = R3 + 16384
O_PT = [R3 + 17408, R3 + 18432, R3 + 19456]
O_SQT = O_PT
O_KRRAW = R3 + 20480
O_REC = R3 + 22528
O_QS = [R3 + 8192, R3 + 9216]
O_SGC = [R3 + 10240, R3 + 11264]
O_OG = [R3 + 12288, R3 + 13312]
O_SP = [R3 + 14336, R3 + 15360]
O_SPACC = [R3 + 16384, R3 + 17408]
O_A = [R3 + 18432, R3 + 19456]
O_E = [R3 + 20480, R3 + 22528]
O_RING = [R4, R4 + 2048, R4 + 4096]
O_TMP = [R4 + 6144, R4 + 8192]
O_XIN = R4 + 6144
O_RB = [R4 + 10240, R4 + 10240]
O_CONSTB = R4 + 12288
O_IDENTF = O_CONSTB + 1792
O_COS = O_IDENTF + 512
O_SIN = O_COS + 1024
O_GQ2 = O_SIN + 1024
O_GKN2 = O_GQ2 + 768
O_GKR = O_GKN2 + 512
O_STAT = O_GKR + 128
O_RECH = O_STAT + 384
O_RECL = O_RECH + 1024
O_GCOL = O_RECL + 1024
O_PST = O_GCOL + 64
O_POSF = O_PST + 512
O_INV = O_POSF + 128
O_ANG = O_INV + 64
ARENA_BYTES = O_ANG + 3072
N_CONSTB = 7
C_IDENT, C_NEGU, C_NEGONES, C_ONES, C_MASKM, C_MASKS, C_BSEL = range(7)


def build_program(n_layers=4, dbg=None, stage=99, layers=None):
    nc = bass.Bass("TRN2", target_bir_lowering=False)
    dt_in = lambda n, s, d=F32: nc.dram_tensor(n, s, d, kind="ExternalInput").ap()
    x_d = dt_in("x", [S, D])
    p_d = dt_in("p", [4, S, 256])
    pos_d = dt_in("pos", [128, NT], I32)
    cb_d = dt_in("constb", [128, N_CONSTB * 128])
    cf_d = dt_in("constf", [128, 128 + 16])
    mla_g_d = dt_in("mla_gcol", [2, 128, 13])
    mla_wa_d = dt_in("mla_wa", [2, 128, 8, 672])
    mla_wg_d = dt_in("mla_wgate", [2, 8, 128, 8, 128])
    mla_wq_d = dt_in("mla_wq", [2, 128, 3, 1536])
    mla_wkv_d = dt_in("mla_wkv", [2, 128, 2, 2048])
    mla_hg_d = dt_in("mla_hg", [2, 128, 352])
    mla_wo_d = dt_in("mla_wo", [2, 8, 128, 8, 128])
    kv_g_d = dt_in("kv_gcol", [128, 8])
    kv_wk_d = dt_in("kv_wk", [8, 128, 8, 128])
    kv_wv_d = dt_in("kv_wv", [8, 128, 8, 128])
    sb_g_d = dt_in("sb_gcol", [2, 128, 8])
    sb_wq_d = dt_in("sb_wq", [2, 8, 128, 8, 128])
    sb_wgt_d = dt_in("sb_wgate", [2, 8, 128, 8, 128])
    sb_wo_d = dt_in("sb_wo", [2, 8, 128, 1024])
    ple_wg_d = dt_in("ple_wg", [4, 8, 128, 8, 128])
    ple_wp_d = dt_in("ple_wp", [4, 8, 128, 2, 128])
    y_d = nc.dram_tensor("y", [S, D], F32, kind="ExternalOutput").ap()
    dbg_d = None
    if dbg is not None:
        dbg_d = nc.dram_tensor("dbg", list(dbg[1]), F32, kind="ExternalOutput").ap()

    st = ExitStack()
    XTt = st.enter_context(nc.sbuf_tensor("XT", [128, 8 * S], F32))
    AR = st.enter_context(nc.sbuf_tensor("AR", [128, ARENA_BYTES // 2], BF16))
    PS = [st.enter_context(nc.psum_tensor("ps%d" % i, [128, 512], F32)) for i in range(8)]
    PSB = [t.bitcast(BF16) for t in PS]
    P = Prog(nc)

    def av(off, n, dt=BF16):
        v = AR[:, off // 2: off // 2 + (n * esize(dt)) // 2]
        return v if dt == BF16 else v.bitcast(dt)

    XT = XTt[:].rearrange("p (k t) -> p k t", k=8)

    def mm(out, lhsT, rhs, start=True, stop=True, skip=False):
        kw = dict(start=start, stop=stop)
        if skip:
            kw['skip_group_check'] = True
        P.add('pe', lambda q: q.matmul(out, lhsT=lhsT, rhs=rhs, **kw),
              reads=[lhsT, rhs] + ([] if start else [out]), writes=[out])

    def tr(out, in_, ident):
        P.add('pe', lambda q: q.transpose(out, in_, ident), reads=[in_, ident], writes=[out])

    def act(out, in_, func, scale=1.0, bias=0.0, eng='act'):
        rd = [in_]
        if not isinstance(scale, (int, float)):
            rd.append(scale)
        if not isinstance(bias, (int, float)):
            rd.append(bias)
        P.add('act', lambda q: q.activation(out=out, in_=in_, func=func, scale=scale, bias=bias),
              reads=rd, writes=[out])

    def tt(out, in0, in1, op, eng='dve'):
        P.add(eng, lambda q: q.tensor_tensor(out=out, in0=in0, in1=in1, op=op), reads=[in0, in1], writes=[out])

    def ts(out, in0, s1, op0, s2=None, op1=None, eng='dve'):
        rd = [in0]
        if not isinstance(s1, (int, float)):
            rd.append(s1)
        if s2 is not None and not isinstance(s2, (int, float)):
            rd.append(s2)
        if op1 is None:
            P.add(eng, lambda q: q.tensor_scalar(out=out, in0=in0, scalar1=s1, scalar2=None, op0=op0),
                  reads=rd, writes=[out])
        else:
            P.add(eng, lambda q: q.tensor_scalar(out=out, in0=in0, scalar1=s1, scalar2=s2, op0=op0, op1=op1),
                  reads=rd, writes=[out])

    def stt(out, in0, scalar, in1, op0, op1):
        rd = [in0, in1]
        if not isinstance(scalar, (int, float)):
            rd.append(scalar)
        P.add('dve', lambda q: q.scalar_tensor_tensor(out=out, in0=in0, scalar=scalar, in1=in1, op0=op0, op1=op1),
              reads=rd, writes=[out])

    def cp(out, in_, eng='dve'):
        if eng == 'act':
            P.add('act', lambda q: q.copy(out=out, in_=in_), reads=[in_], writes=[out])
        else:
            P.add(eng, lambda q: q.tensor_copy(out=out, in_=in_), reads=[in_], writes=[out])

    def red(out, in_, eng='dve'):
        P.add(eng, lambda q: q.tensor_reduce(out=out, in_=in_, op=ALU.add, axis=AX.X), reads=[in_], writes=[out])

    def recip(out, in_):
        P.add('dve', lambda q: q.reciprocal(out=out, in_=in_), reads=[in_], writes=[out])

    def memset(ap, val, eng='dve'):
        P.add(eng, lambda q: q.memset(ap, val), reads=[], writes=[ap])

    def dma(out, in_, eng='sp'):
        P.add(eng, lambda q: q.dma_start(out=out, in_=in_), reads=[in_], writes=[out], dma=True)

    def rsqrt_to(out, in_, scale, tmp):
        act(tmp, in_, AF.Ln, scale=scale, bias=EPSC)
        act(out, tmp, AF.Exp, scale=-0.5)

    CB = av(O_CONSTB, N_CONSTB * 128).rearrange("p (c n) -> p c n", c=N_CONSTB)
    dma(av(O_CONSTB, N_CONSTB * 128), cb_d, eng='pool')
    IDENT = CB[:, C_IDENT, :]
    NEGU = CB[:, C_NEGU, :]
    NEGONES = CB[:, C_NEGONES, :]
    ONES = CB[:, C_ONES, :]
    MASKM = CB[:, C_MASKM, :]
    MASKS = CB[:, C_MASKS, :]
    BSEL = CB[:, C_BSEL, :]
    IDENTF = av(O_IDENTF, 128, F32)
    dma(IDENTF, cf_d[:, 0:128])
    INV = av(O_INV, 16, F32)
    dma(INV, cf_d[:, 128:144])
    STAT = av(O_STAT, 96, F32)
    EPSC = STAT[:, 95:96]
    memset(EPSC, EPS)
    GCOL = av(O_GCOL, 16, F32)
    psrot = [0]

    def bank(i=None):
        if i is None:
            psrot[0] = (psrot[0] + 1) % 2
            return 6 + psrot[0]
        return i

    TMP = [av(o, 512, F32) for o in O_TMP]
    RBt = [av(o, 512, F32) for o in O_RB]
    RING = [av(o, 1024).rearrange("p (k n) -> p k n", k=8) for o in O_RING]
    ring_i = [0]

    def ring_load(src, shape=None):
        i = ring_i[0] % 3
        ring_i[0] += 1
        k, n = src.shape[1], src.shape[2]
        dst = av(O_RING[i], k * n).rearrange("p (k n) -> p k n", k=k)
        dma(dst, src, eng='pool')
        return dst

    XB = av(R0, 8 * S).rearrange("p (k t) -> p k t", k=8)
    XSQ = av(O_XSQ, 8 * 512).rearrange("p (k t) -> p k t", k=8)

    XIN = av(O_XIN, 1024, F32)
    for t in range(NT):
        dma(XIN, x_d[t * 128:(t + 1) * 128, :])
        for half in range(2):
            b = 4 + half
            for q4 in range(4):
                kc = half * 4 + q4
                tr(PS[b][:, q4 * 128:(q4 + 1) * 128], XIN[:, kc * 128:(kc + 1) * 128], IDENTF)
            dst = XT[:, half * 4:(half + 1) * 4, t * 128:(t + 1) * 128]
            src = PS[b][:, :].rearrange("p (k t) -> p k t", k=4)
            if half == 0:
                cp(dst, src, eng='dve')
            else:
                cp(dst, src, eng='act')

    COS = av(O_COS, 256, F32).rearrange("p (t i) -> p t i", t=NT)
    SIN = av(O_SIN, 256, F32).rearrange("p (t i) -> p t i", t=NT)
    if True:
        POSI = av(O_POSF + 64, 16, I32)
        POSF = av(O_POSF, 16, F32)
        dma(POSI, pos_d)
        cp(POSF, POSI)
        ANG = av(O_ANG, 256, F32)
        A2 = av(O_ANG + 1024, 256, F32)
        A3 = av(O_ANG + 2048, 256, F32)
        KI = av(O_ANG + 2048, 256, I32)
        ANG3 = ANG.rearrange("p (t i) -> p t i", t=NT)
        tt(ANG3, POSF.unsqueeze(2).broadcast_to([128, NT, 16]), INV.unsqueeze(1).broadcast_to([128, NT, 16]), ALU.mult)
        TWO_PI = 2.0 * math.pi

        def reduce_to_pi(dst, src):
            ts(A2, src, 1.0 / TWO_PI, ALU.mult)
            cp(KI, A2)
            cp(A2, KI)
            stt(dst, A2, -TWO_PI, src, ALU.mult, ALU.add)
            ts(A2, dst, math.pi, ALU.is_gt, -TWO_PI, ALU.mult)
            tt(dst, dst, A2, ALU.add)
            ts(A2, dst, -math.pi, ALU.is_lt, TWO_PI, ALU.mult)
            tt(dst, dst, A2, ALU.add)

        SINr = av(O_SIN, 256, F32)
        COSr = av(O_COS, 256, F32)
        reduce_to_pi(SINr, ANG)
        ts(ANG, ANG, math.pi / 2.0, ALU.add)
        reduce_to_pi(COSr, ANG)
        act(SINr, SINr, AF.Sin)
        act(COSr, COSr, AF.Sin)

    def prep_norm(gcols):
        for j in range(NJ):
            cols = slice(j * 512, (j + 1) * 512)
            for kc in range(8):
                act(XSQ[:, kc, :], XT[:, kc, cols], AF.Square)
            b = bank()
            for kc in range(8):
                mm(PS[b][:, :], ONES, XSQ[:, kc, :], start=(kc == 0), stop=(kc == 7))
            rb = RBt[j % 2]
            rsqrt_to(rb, PS[b][:, :], 1.0 / D, TMP[j % 2])
            for kc in range(8):
                stt(XB[:, kc, cols], XT[:, kc, cols], gcols[:, kc:kc + 1], rb, ALU.mult, ALU.mult)

    PPT = av(O_PPT, 2 * S).rearrange("p (k t) -> p k t", k=2)

    def ple(i):
        PST = [av(O_PST, 256), av(O_PST, 256)]
        for t in range(NT):
            pst = PST[t % 2]
            dma(pst, p_d[i, t * 128:(t + 1) * 128, :], eng='pool')
            b = bank()
            for kc in range(2):
                tr(PSB[b][:, kc * 128:(kc + 1) * 128], pst[:, kc * 128:(kc + 1) * 128], IDENT)
            cp(PPT[:, :, t * 128:(t + 1) * 128], PSB[b][:, 0:256].rearrange("p (k t) -> p k t", k=2),
               eng='act' if t % 2 else 'dve')
        for n in range(8):
            wg = ring_load(ple_wg_d[i, n])
            wp = ring_load(ple_wp_d[i, n])
            for j in range(NJ):
                cols = slice(j * 512, (j + 1) * 512)
                bg = bank()
                for kc in range(8):
                    mm(PS[bg][:, :], wg[:, kc, :], XB[:, kc, cols], start=(kc == 0), stop=(kc == 7))
                tmp = TMP[j % 2]
                act(tmp, PS[bg][:, :], AF.Sigmoid)
                bp = bank()
                for kc in range(2):
                    mm(PS[bp][:, :], wp[:, kc, :], PPT[:, kc, cols], start=(kc == 0), stop=(kc == 1))
                tt(tmp, tmp, PS[bp][:, :], ALU.mult)
                tt(XT[:, n, cols], XT[:, n, cols], tmp, ALU.add)

    SG = av(R1, 8 * S).rearrange("p (k t) -> p k t", k=8)
    CQB = av(O_CQB, 3 * S).rearrange("p (k t) -> p k t", k=3)
    CKVB = av(O_CKVB, 2 * S).rearrange("p (k t) -> p k t", k=2)
    KRRAW = av(O_KRRAW, NT * 32, F32).rearrange("p (t i) -> p t i", t=NT)
    KROPE = av(O_KROPE, NT * 32, F32).rearrange("p (t i) -> p t i", t=NT)
    GAM = av(O_GAM, 256, F32)
    DEL = av(O_DEL, 256, F32)
    DELN = av(O_DELN, 256, F32)
    GQ2 = av(O_GQ2, 192, F32)
    GKN2 = av(O_GKN2, 128, F32)
    GKR = av(O_GKR, 32, F32)
    BQ = STAT[:, 0:16]
    BKV = STAT[:, 16:32]
    SSKR = STAT[:, 32:48]
    ST1 = STAT[:, 48:64]
    ST2 = STAT[:, 64:80]

    def mla_layer(li):
        dma(GCOL[:, 0:13], mla_g_d[li])
        if stage == 1:
            return
        prep_norm(GCOL[:, 0:8])
        if stage == 1.5:
            return
        dma(av(O_GQ2, 352, F32), mla_hg_d[li])
        ts(GQ2, GQ2, 96.0 ** -0.5, ALU.mult)
        if stage < 2:
            return
        WA = av(O_WA, 8 * 672).rearrange("p (k n) -> p k n", k=8)
        for kc in range(8):
            dma(WA[:, kc, :], mla_wa_d[li, :, kc, :], eng='pool')
        SQT = [av(o, 512) for o in O_SQT]
        for j in range(NJ):
            cols = slice(j * 512, (j + 1) * 512)
            for (nblk, c0, dstT, gofs, statcol) in ((3, 0, CQB, 8, 0), (2, 384, CKVB, 11, 1)):
                for cb in range(nblk):
                    b = bank()
                    for kc in range(8):
                        mm(PS[b][:, :], WA[:, kc, c0 + cb * 128:c0 + (cb + 1) * 128], XB[:, kc, cols],
                           start=(kc == 0), stop=(kc == 7))
                    ts(dstT[:, cb, cols], PS[b][:, :], GCOL[:, gofs + cb:gofs + cb + 1], ALU.mult)
                    if stage >= 2.2:
                        act(SQT[cb], PS[b][:, :], AF.Square)
                for t4 in range(4):
                    if stage < 2.3:
                        break
                    t = j * 4 + t4
                    for cb in range(nblk):
                        mm(PS[4][:, statcol * 16 + t:statcol * 16 + t + 1], SQT[cb][:, t4 * 128:(t4 + 1) * 128],
                           ONES[:, 0:1], start=(cb == 0), stop=(cb == nblk - 1))
            for t4 in range(4):
                if stage < 2.4:
                    break
                t = j * 4 + t4
                for kc in range(8):
                    mm(PS[5][:, t * 32:(t + 1) * 32], XB[:, kc, t * 128:(t + 1) * 128], WA[:, kc, 640:672],
                       start=(kc == 0), stop=(kc == 7))
        if stage < 2.5:
            return
        cp(KRRAW, PS[5][:, :].rearrange("p (t i) -> p t i", t=NT))
        if stage < 2.6:
            return
        rsqrt_to(BQ, PS[4][:, 0:16], 1.0 / 384, ST1)
        rsqrt_to(BKV, PS[4][:, 16:32], 1.0 / 256, ST2)
        if stage < 3:
            return
        for cb in range(8):
            w = ring_load(mla_wg_d[li, cb])
            for j in range(NJ):
                cols = slice(j * 512, (j + 1) * 512)
                b = bank()
                for kc in range(8):
                    mm(PS[b][:, :], w[:, kc, :], XB[:, kc, cols], start=(kc == 0), stop=(kc == 7))
                act(SG[:, cb, cols], PS[b][:, :], AF.Silu)
        if stage < 4:
            return
        KRSQ = av(O_QR, NT * 32, F32).rearrange("p (t i) -> p t i", t=NT)
        tt(KRSQ, KRRAW, KRRAW, ALU.mult)
        red(SSKR, KRSQ)
        KRG = av(O_QR + 2048, NT * 32, F32).rearrange("p (t i) -> p t i", t=NT)
        tt(KRG, KRRAW, GKR.unsqueeze(1).broadcast_to([128, NT, 32]), ALU.mult)

        def rope(dst1, dst2, x1, x2, cosb, sinb, t1, t2):
            tt(t1, x1, cosb, ALU.mult)
            tt(t2, x2, sinb, ALU.mult)
            tt(dst1, t1, t2, ALU.subtract)
            tt(t1, x2, cosb, ALU.mult)
            tt(t2, x1, sinb, ALU.mult)
            tt(dst2, t1, t2, ALU.add)

        RT1 = av(O_ANG, 256, F32).rearrange("p (t i) -> p t i", t=NT)
        RT2 = av(O_ANG + 1024, 256, F32).rearrange("p (t i) -> p t i", t=NT)
        rope(KROPE[:, :, 0:16], KROPE[:, :, 16:32], KRG[:, :, 0:16], KRG[:, :, 16:32], COS, SIN, RT1, RT2)
        if stage < 5:
            return
        WQ = av(O_WQ, 3 * 1536).rearrange("p (k n) -> p k n", k=3)
        WKV = av(O_WKV, 2 * 2048).rearrange("p (k n) -> p k n", k=2)
        for kc in range(3):
            dma(WQ[:, kc, :], mla_wq_d[li, :, kc, :], eng='pool')
        for kc in range(2):
            dma(WKV[:, kc, :], mla_wkv_d[li, :, kc, :], eng='pool')
        GAM3 = GAM.rearrange("p (t h) -> p t h", t=NT)
        DEL3 = DEL.rearrange("p (t h) -> p t h", t=NT)
        DELN3 = DELN.rearrange("p (t h) -> p t h", t=NT)
        for t in range(NT):
            tcols = slice(t * 128, (t + 1) * 128)
            for qb in range(4):
                b = bank()
                for kc in range(3):
                    mm(PS[b][:, 0:384], CQB[:, kc, tcols], WQ[:, kc, qb * 384:(qb + 1) * 384],
                       start=(kc == 0), stop=(kc == 2))
                tmp = TMP[qb % 2]
                act(tmp[:, 0:384], PS[b][:, 0:384], AF.Square)
                red(GAM3[:, t, qb * 4:(qb + 1) * 4], tmp[:, 0:384].rearrange("p (h d) -> p h d", h=4))
            for kb4 in range(4):
                b = bank()
                for kc in range(2):
                    mm(PS[b][:, :], CKVB[:, kc, tcols], WKV[:, kc, kb4 * 512:(kb4 + 1) * 512],
                       start=(kc == 0), stop=(kc == 1))
                tmp = TMP[kb4 % 2]
                act(tmp[:, 0:256].rearrange("p (h d) -> p h d", h=4),
                    PS[b][:, :].rearrange("p (h d) -> p h d", h=4)[:, :, 0:64], AF.Square)
                red(DEL3[:, t, kb4 * 4:(kb4 + 1) * 4], tmp[:, 0:256].rearrange("p (h d) -> p h d", h=4))
        BQb = BQ.unsqueeze(2).broadcast_to([128, NT, H])
        BKVb = BKV.unsqueeze(2).broadcast_to([128, NT, H])
        T256 = av(O_ANG, 256, F32)
        T256b = av(O_ANG + 1024, 256, F32)
        T3 = T256.rearrange("p (t h) -> p t h", t=NT)
        tt(T3, GAM3, BQb, ALU.mult)
        tt(T3, T3, BQb, ALU.mult)
        rsqrt_to(GAM, T256, 1.0 / 96, T256b)
        tt(GAM3, GAM3, BQb, ALU.mult)
        tt(T3, DEL3, BKVb, ALU.mult)
        tt(T3, T3, BKVb, ALU.mult)
        tt(T3, T3, SSKR.unsqueeze(2).broadcast_to([128, NT, H]), ALU.add)
        rsqrt_to(DEL, T256, 1.0 / 96, T256b)
        tt(DELN3, DEL3, BKVb, ALU.mult)
        if stage < 6:
            return
        QT = [av(o, S) for o in O_QT]
        KT = [av(o, S) for o in O_KT]
        VA = av(O_VA, NT * 256).rearrange("p (t c) -> p t c", t=NT)
        QN = av(O_QN, NT * 192).rearrange("p (t c) -> p t c", t=NT)
        QR = av(O_QR, NT * 64, F32).rearrange("p (t h i) -> p t h i", t=NT, h=2)
        memset(VA[:, :, 64:192], 0.0)
        memset(VA[:, :, 64:65], 1.0)
        memset(VA[:, :, 128:129], 1.0)
        PT = [av(o, 512) for o in O_PT]
        REC = av(O_REC, 512, F32)
        RECH = av(O_RECH, 512)
        RECL = av(O_RECL, 512)
        pti = [0]
        COSb = COS.unsqueeze(2).broadcast_to([128, NT, 2, 16])
        SINb = SIN.unsqueeze(2).broadcast_to([128, NT, 2, 16])
        for c in range(8):
            QN4 = QN.rearrange("p t (h d) -> p t h d", h=2)
            for t in range(NT):
                tcols = slice(t * 128, (t + 1) * 128)
                b = bank()
                for kc in range(3):
                    mm(PS[b][:, 0:192], CQB[:, kc, tcols], WQ[:, kc, c * 192:(c + 1) * 192],
                       start=(kc == 0), stop=(kc == 2))
                tmp = TMP[t % 2]
                t3 = tmp[:, 0:192].rearrange("p (h d) -> p h d", h=2)
                tt(t3, PS[b][:, 0:192].rearrange("p (h d) -> p h d", h=2),
                   GAM3[:, t, 2 * c:2 * c + 2].unsqueeze(2).broadcast_to([128, 2, 96]), ALU.mult)
                g3 = GQ2.rearrange("p (h d) -> p h d", h=2)
                tt(QN4[:, t, :, 0:64], t3[:, :, 0:64], g3[:, :, 0:64], ALU.mult)
                tt(QR[:, t, :, :], t3[:, :, 64:96], g3[:, :, 64:96], ALU.mult)
            RA1 = av(O_ANG, 512, F32).rearrange("p (t h i) -> p t h i", t=NT, h=2)
            RA2 = av(O_KRRAW, 512, F32).rearrange("p (t h i) -> p t h i", t=NT, h=2)
            rope(QN4[:, :, :, 64:80], QN4[:, :, :, 80:96], QR[:, :, :, 0:16], QR[:, :, :, 16:32], COSb, SINb, RA1, RA2)
            for hh in range(2):
                for half in range(2):
                    b = bank()
                    for t8 in range(8):
                        t = half * 8 + t8
                        tr(PSB[b][0:96, t8 * 128:(t8 + 1) * 128], QN[:, t, hh * 96:(hh + 1) * 96], IDENT)
                    cp(QT[hh][0:96, half * 1024:(half + 1) * 1024], PSB[b][0:96, :], eng='act' if half else 'dve')
            for t in range(NT):
                tcols = slice(t * 128, (t + 1) * 128)
                b = bank()
                for kc in range(2):
                    mm(PS[b][:, 0:256], CKVB[:, kc, tcols], WKV[:, kc, c * 256:(c + 1) * 256],
                       start=(kc == 0), stop=(kc == 1))
                p3 = PS[b][:, 0:256].rearrange("p (h d) -> p h d", h=2)
                tmp = TMP[t % 2]
                t3 = tmp[:, 0:128].rearrange("p (h d) -> p h d", h=2)
                tt(t3, p3[:, :, 0:64], DELN3[:, t, 2 * c:2 * c + 2].unsqueeze(2).broadcast_to([128, 2, 64]), ALU.mult)
                tt(QN4[:, t, :, 0:64], t3, GKN2.rearrange("p (h d) -> p h d", h=2), ALU.mult)
                vdst = VA[:, t, :].rearrange("p (a c) -> p a c", a=4)[:, 0:4:3, :]
                act(vdst, p3[:, :, 64:128], AF.Copy, scale=BKV[:, t:t + 1])
            tt(QN4[:, :, :, 64:96], KROPE.unsqueeze(2).broadcast_to([128, NT, 2, 32]),
               DEL3[:, :, 2 * c:2 * c + 2].unsqueeze(3).broadcast_to([128, NT, 2, 32]), ALU.mult)
            for hh in range(2):
                for half in range(2):
                    b = bank()
                    for t8 in range(8):
                        t = half * 8 + t8
                        tr(PSB[b][0:96, t8 * 128:(t8 + 1) * 128], QN[:, t, hh * 96:(hh + 1) * 96], IDENT)
                    cp(KT[hh][0:96, half * 1024:(half + 1) * 1024], PSB[b][0:96, :], eng='act' if half else 'dve')
            PT4 = [[PT[0], PT[1]], [PT[2], av(O_RB[0], 512)]]
            for j in range(NJ):
                nkb = 4 * j + 4
                cols = slice(j * 512, (j + 1) * 512)

                def s_mm(hh, kb):
                    r = kb - 4 * j
                    c0 = max(0, r) * 128
                    zb = 2 * hh + kb % 2
                    qs = slice(j * 512 + c0, (j + 1) * 512)
                    mm(PS[zb][:, c0:512], KT[hh][0:96, kb * 128:(kb + 1) * 128], QT[hh][0:96, qs],
                       start=True, stop=True)
                    if r >= 0:
                        mm(PS[zb][:, c0:c0 + 128], IDENT, MASKM, start=False, stop=True, skip=True)

                for hh in range(2):
                    s_mm(hh, 0)
                for kb in range(nkb):
                    r = kb - 4 * j
                    c0 = max(0, r) * 128
                    if kb + 1 < nkb:
                        for hh in range(2):
                            s_mm(hh, kb + 1)
                    for hh in range(2):
                        act(PT4[hh][kb % 2][:, c0:512], PS[2 * hh + kb % 2][:, c0:512], AF.Exp)
                    for hh in range(2):
                        pt = PT4[hh][kb % 2]
                        if hh == 0:
                            mm(PS[4][0:65, c0:512], VA[:, kb, 0:65], pt[:, c0:512],
                               start=(kb == 0), stop=(kb == nkb - 1), skip=True)
                        else:
                            mm(PS[5][:, c0:512], VA[:, kb, 128:256], pt[:, c0:512],
                               start=(kb == 0), stop=(kb == nkb - 1), skip=True)
                for hh in range(2):
                    ob = 4 + hh
                    if hh == 0:
                        drow, rows = 64, slice(0, 64)
                        lsel = BSEL[64:65, 0:64]
                    else:
                        drow, rows = 0, slice(64, 128)
                        lsel = BSEL[0:1, :]
                    rr = slice(drow, drow + 1)
                    recip(REC[rr, :], PS[ob][rr, :])
                    cp(RECH[rr, :], REC[rr, :])
                    tt(RECL[rr, :], REC[rr, :], RECH[rr, :], ALU.subtract)
                    bb = bank()
                    orow = slice(0, 64) if hh == 0 else slice(0, 128)
                    mm(PS[bb][orow, :], lsel, RECH[rr, :], start=True, stop=False)
                    mm(PS[bb][orow, :], lsel, RECL[rr, :], start=False, stop=True)
                    tmp = TMP[hh]
                    tt(tmp[rows, :], PS[bb][rows, :], SG[rows, c, cols], ALU.mult)
                    tt(SG[rows, c, cols], PS[ob][rows, :], tmp[rows, :], ALU.mult)
        for n in range(8):
            w = ring_load(mla_wo_d[li, n])
            for j in range(NJ):
                cols = slice(j * 512, (j + 1) * 512)
                b = bank()
                for kc in range(8):
                    mm(PS[b][:, :], w[:, kc, :], SG[:, kc, cols], start=(kc == 0), stop=(kc == 7))
                tt(XT[:, n, cols], XT[:, n, cols], PS[b][:, :], ALU.add)
                cp(XB[:, n, cols], XT[:, n, cols], eng='act')

    KSH = av(R1, 8 * S).rearrange("p (k t) -> p k t", k=8)
    VSH = av(O_VSH, NT * 1024).rearrange("p (t c) -> p t c", t=NT)

    def shared_kv():
        dma(GCOL[:, 0:8], kv_g_d)
        prep_norm(GCOL[:, 0:8])
        for n in range(8):
            w = ring_load(kv_wk_d[n])
            for j in range(NJ):
                cols = slice(j * 512, (j + 1) * 512)
                b = bank()
                for kc in range(8):
                    mm(PS[b][:, :], w[:, kc, :], XB[:, kc, cols], start=(kc == 0), stop=(kc == 7))
                cp(KSH[:, n, cols], PS[b][:, :], eng='act' if j % 2 else 'dve')
        for n in range(8):
            w = ring_load(kv_wv_d[n])
            for t in range(NT):
                b = bank()
                for kc in range(8):
                    mm(PS[b][:, 0:128], XB[:, kc, t * 128:(t + 1) * 128], w[:, kc, :], start=(kc == 0), stop=(kc == 7))
                cp(VSH[:, t, n * 128:(n + 1) * 128], PS[b][:, 0:128], eng='act' if t % 2 else 'dve')

    def sb_layer(lj):
        from collections import deque
        dma(GCOL[:, 0:8], sb_g_d[lj])
        prep_norm(GCOL[:, 0:8])
        QSZ = [av(R3 + k * 2048, 1024) for k in range(4)]
        SGC = [av(R3 + 8192 + k * 1024, 512) for k in range(4)]
        OG = [av(O_RECH, 512), av(O_RECL, 512), av(O_COS, 512), av(O_SIN, 512)]
        for j in range(4):
            memset(QSZ[j][64:128, 0:512], 0.0)
            memset(QSZ[j][0:64, 512:1024], 0.0)
        SP = [av(R3 + 12288 + k * 1024, 512) for k in range(4)]
        SPACC = [av(R3 + 16384 + k * 1024, 512) for k in range(4)]
        AT = [av(R3 + 20480 + k * 1024, 512) for k in range(4)]
        E = [av(O_TMP[0], 512, F32), av(O_TMP[1], 512, F32), av(O_RB[0], 512, F32), av(O_ANG, 512, F32)]
        WQ_ = av(O_RING[0], 1024).rearrange("p (k n) -> p k n", k=8)
        WG_ = av(O_RING[1], 1024).rearrange("p (k n) -> p k n", k=8)
        WO_ = av(O_RING[2], 1024)
        fifo = deque()

        def bg(n):
            while n > 0 and fifo:
                fifo.popleft()[1]()
                n -= 1

        def drain(pred):
            while any(pred(k) for k, _ in fifo):
                fifo.popleft()[1]()

        def load_w(c):
            dma(WQ_, sb_wq_d[lj, c], eng='pool')
            dma(WG_, sb_wgt_d[lj, c], eng='pool')

        def load_wo(c):
            dma(WO_, sb_wo_d[lj, c], eng='pool')

        def proj_items(c, j):
            cols = slice(j * 512, (j + 1) * 512)
            key = ('proj', c, j)
            st_ = {}
            items = []

            def mk_mm(w, kc, which):
                def f():
                    if kc == 0:
                        st_[which] = bank()
                    mm(PS[st_[which]][:, :], w[:, kc, :], XB[:, kc, cols], start=(kc == 0), stop=(kc == 7))
                return f
            for kc in range(8):
                items.append((key, mk_mm(WQ_, kc, 'q')))

            def evq():
                b = st_['q']
                ts(QSZ[j][0:64, 0:512], PS[b][0:64, :], 0.125, ALU.mult)
                ts(QSZ[j][64:128, 512:1024], PS[b][64:128, :], 0.125, ALU.mult)
            items.append((key, evq))
            for kc in range(8):
                items.append((key, mk_mm(WG_, kc, 'g')))
            items.append((key, lambda: act(SGC[j], PS[st_['g']][:, :], AF.Silu)))
            return items

        def wo_items(c, j):
            cols = slice(j * 512, (j + 1) * 512)
            items = []

            def mk(n):
                def f():
                    b = bank()
                    mm(PS[b][:, :], WO_[:, n * 128:(n + 1) * 128], OG[j], start=True, stop=True)
                    tt(XT[:, n, cols], XT[:, n, cols], PS[b][:, :], ALU.add)
                return f
            for n in range(8):
                items.append((('wo', c, j), mk(n)))
            return items

        load_w(0)
        load_wo(0)
        for j in (3, 2, 1, 0):
            for it in proj_items(0, j):
                it[1]()
        slots = []
        for k in range(4):
            seq = (3, 0) if k // 2 == 0 else (2, 1)
            blocks = [(c, j, kb) for c in range(8) for j in seq for kb in range(4 * j + 3, -1, -1)]
            slots.append((k % 2, k, 4 + k // 2, blocks))
        for r in range(160):
            cr, rl = r // 20, r % 20
            if rl == 9 and cr + 1 < 8:
                drain(lambda k: k[0] == 'proj' and k[1] == cr)
                load_w(cr + 1)
            if rl == 9 and cr >= 1:
                drain(lambda k: k[0] == 'wo' and k[1] == cr - 1)
                load_wo(cr)
            info = []
            for (hh, zb, ob, blocks) in slots:
                c, j, kb = blocks[r]
                rr = kb - 4 * j
                first = (kb == 4 * j + 3)
                if first:
                    drain(lambda k: k[0] == 'proj' and k[1] == c and k[2] == j)
                info.append((hh, zb, ob, j, kb, rr, max(0, rr) * 128, first, slice(hh * 64, hh * 64 + 64), c))
            for k, (hh, zb, ob, j, kb, rr, c0, first, rows, c) in enumerate(info):
                mm(PS[zb][:, c0:512], KSH[:, c, kb * 128:(kb + 1) * 128],
                   QSZ[j][:, hh * 512 + c0:hh * 512 + 512], start=True, stop=True)
                if rr >= 0:
                    mm(PS[zb][:, c0:c0 + 128], IDENT, MASKS, start=False, stop=True, skip=True)
            bg(3)
            for k, (hh, zb, ob, j, kb, rr, c0, first, rows, c) in enumerate(info):
                act(E[k][:, c0:512], PS[zb][:, c0:512], AF.Exp)
            for k, (hh, zb, ob, j, kb, rr, c0, first, rows, c) in enumerate(info):
                act(SP[k][:, c0:512], E[k][:, c0:512], AF.Ln, bias=ONEC)
            for k, (hh, zb, ob, j, kb, rr, c0, first, rows, c) in enumerate(info):
                mm(PS[zb][:, c0:512], NEGU, SP[k][:, c0:512], start=False, stop=True, skip=True)
                if not first:
                    a0 = c0 + 128 if rr >= 0 else 0
                    mm(PS[zb][:, a0:512], NEGONES, SPACC[k][:, a0:512], start=False, stop=True, skip=True)
            bg(3)
            for k, (hh, zb, ob, j, kb, rr, c0, first, rows, c) in enumerate(info):
                if kb > 0:
                    if first:
                        cp(SPACC[k][:, c0:512], SP[k][:, c0:512])
                    elif rr >= 0:
                        pc0 = c0 + 128
                        cp(SPACC[k][:, c0:pc0], SP[k][:, c0:pc0])
                        tt(SPACC[k][:, pc0:512], SPACC[k][:, pc0:512], SP[k][:, pc0:512], ALU.add)
                    else:
                        tt(SPACC[k][:, :], SPACC[k][:, :], SP[k][:, :], ALU.add)
            for k, (hh, zb, ob, j, kb, rr, c0, first, rows, c) in enumerate(info):
                act(AT[k][:, c0:512], PS[zb][:, c0:512], AF.Exp)
            for k, (hh, zb, ob, j, kb, rr, c0, first, rows, c) in enumerate(info):
                mm(PS[ob][rows, c0:512], VSH[:, kb, (2 * c + hh) * 64:(2 * c + hh + 1) * 64], AT[k][:, c0:512],
                   start=first, stop=(kb == 0), skip=True)
            bg(3)
            for k, (hh, zb, ob, j, kb, rr, c0, first, rows, c) in enumerate(info):
                if kb == 0 and hh == 1:
                    drain(lambda kk: kk[0] == 'wo' and kk[2] == j)
                    tt(OG[j], PS[ob][:, :], SGC[j], ALU.mult)
                    fifo.extend(wo_items(c, j))
                    if c + 1 < 8:
                        fifo.extend(proj_items(c + 1, j))
        bg(100000)
        for n in range(8):
            for j in range(NJ):
                cols = slice(j * 512, (j + 1) * 512)
                cp(XB[:, n, cols], XT[:, n, cols], eng='act')

    ONEC = STAT[:, 94:95]
    memset(ONEC, 1.0)

    for i in (layers if layers is not None else range(n_layers)):
        if stage < 1:
            break
        if i < 2:
            mla_layer(i)
        else:
            sb_layer(i - 2)
        if stage >= 8:
            ple(i)
        if i == 1 and n_layers > 2:
            shared_kv()

    if dbg is not None:
        name = dbg[0]
        if name == 'cos':
            dma(dbg_d[:, 0:256], av(O_COS, 256, F32))
            dma(dbg_d[:, 256:512], av(O_SIN, 256, F32))

    YO = av(O_XIN, 1024, F32)
    for t in range(NT):
        for half in range(2):
            b = 4 + half
            for q4 in range(4):
                kc = half * 4 + q4
                tr(PS[b][:, q4 * 128:(q4 + 1) * 128], XT[:, kc, t * 128:(t + 1) * 128], IDENTF)
            cp(YO[:, half * 512:(half + 1) * 512], PS[b][:, :], eng='act' if half else 'dve')
        dma(y_d[t * 128:(t + 1) * 128, :], YO)
    P.wait_all_dma('sp')
    P.emit(st)
    st.close()
    return nc, P


def _consts():
    j = np.arange(128)[:, None]
    s = np.arange(128)[None, :]
    cb = np.zeros((128, N_CONSTB, 128), np.float32)
    cb[:, C_IDENT] = (j == s)
    cb[:, C_NEGU] = -(j >= s).astype(np.float32)
    cb[:, C_NEGONES] = -1.0
    cb[:, C_ONES] = 1.0
    cb[:, C_MASKM] = np.where(s >= j, 0.0, NEG)
    cb[:, C_MASKS] = np.where(s > j, 0.0, NEG)
    bs = np.zeros((128, 128), np.float32)
    bs[64, 0:64] = 1.0
    bs[0, 64:128] = 1.0
    cb[:, C_BSEL] = bs
    cf = np.zeros((128, 144), np.float32)
    cf[:, 0:128] = np.eye(128, dtype=np.float32)
    half = 16
    inv = (1.0 / (np.float32(10000.0) ** (np.arange(half, dtype=np.float32) / np.float32(half)))).astype(np.float32)
    cf[:, 128:144] = inv[None, :]
    return cb.reshape(128, -1), cf


def _kt(w):
    K, N = w.shape
    return np.ascontiguousarray(w.reshape(K // 128, 128, N).transpose(1, 0, 2))


def _blk(w, nb=128):
    K, N = w.shape
    a = w.reshape(K // 128, 128, N // nb, nb).transpose(2, 1, 0, 3)
    return np.ascontiguousarray(a)


def _gcol(g):
    return np.ascontiguousarray(g.reshape(-1, 128).T)


def make_in_maps(inputs):
    f = lambda k: np.asarray(inputs[k], dtype=np.float32)
    cb, cf = _consts()
    mla_w_in = f("mla_w_in")
    shared = {
        "constb": cb, "constf": cf,
        "mla_gcol": np.stack([np.concatenate([_gcol(f("mla_ln_g")[l]), _gcol(f("mla_q_norm_g")[l]),
                                              _gcol(f("mla_kv_norm_g")[l])], axis=1) for l in range(2)]),
        "mla_wa": np.stack([_kt(mla_w_in[l][:, 0:672]) for l in range(2)]),
        "mla_wgate": np.stack([_blk(mla_w_in[l][:, 672:1696]) for l in range(2)]),
        "mla_wq": np.stack([_kt(f("mla_w_q_up")[l]) for l in range(2)]),
        "mla_wkv": np.stack([_kt(f("mla_w_kv_up")[l]) for l in range(2)]),
        "mla_hg": np.stack([np.ascontiguousarray(np.broadcast_to(np.concatenate(
            [f("mla_q_head_g")[l], f("mla_q_head_g")[l], f("mla_k_head_g")[l][:64], f("mla_k_head_g")[l][:64],
             f("mla_k_head_g")[l][64:]])[None, :], (128, 352))) for l in range(2)]),
        "mla_wo": np.stack([_blk(f("mla_w_out")[l]) for l in range(2)]),
        "kv_gcol": _gcol(f("kv_ln_g")),
        "kv_wk": _blk(f("w_kv_shared")[:, 0:1024]),
        "kv_wv": _blk(f("w_kv_shared")[:, 1024:2048]),
        "sb_gcol": np.stack([_gcol(f("sb_ln_g")[l]) for l in range(2)]),
        "sb_wq": np.stack([_blk(f("sb_w_in")[l][:, 0:1024]) for l in range(2)]),
        "sb_wgate": np.stack([_blk(f("sb_w_in")[l][:, 1024:2048]) for l in range(2)]),
        "sb_wo": np.ascontiguousarray(f("sb_w_out").reshape(2, 8, 128, 1024)),
        "ple_wg": np.stack([_blk(f("ple_w_gate")[l]) for l in range(4)]),
        "ple_wp": np.stack([_blk(f("ple_w_proj")[l]) for l in range(4)]),
    }
    x = f("x")
    p = f("p")
    pos = np.asarray(inputs["positions"]).astype(np.int32)
    maps = []
    for b in range(8):
        m = dict(shared)
        m["x"] = np.ascontiguousarray(x[b])
        m["p"] = np.ascontiguousarray(p[:, b])
        m["pos"] = np.ascontiguousarray(pos[b].reshape(NT, 128).T)
        maps.append(m)
    return maps


_CACHE = {}


def kernel(**inputs):
    maps = make_in_maps(inputs)
    if "nc" not in _CACHE:
        _CACHE["nc"] = build_program(4)[0]
    res = run_bass_kernel_spmd(_CACHE["nc"], maps, core_ids=list(range(8)))
    return np.stack([np.asarray(r["y"], dtype=np.float32) for r in res.results], axis=0)
```

```python
import math
from contextlib import ExitStack

import numpy as np
import concourse.bass as bass
import concourse.mybir as mybir
from concourse.bass_utils import run_bass_kernel_spmd

F32 = mybir.dt.float32
BF16 = mybir.dt.bfloat16
I32 = mybir.dt.int32
AF = mybir.ActivationFunctionType
ALU = mybir.AluOpType
AX = mybir.AxisListType

S = 2048
D = 1024
NT = 16
NJ = 4
H = 16
EPS = 1e-6
NEG = -30000.0

_ESZ = {}


def esize(dt):
    if dt not in _ESZ:
        _ESZ[dt] = mybir.dt.size(dt)
    return _ESZ[dt]


def region(ap):
    t = ap.tensor
    es = esize(ap.dtype)
    dims = ap.ap
    off = ap.offset
    if type(t).__name__.startswith('DRam'):
        lo = hi = off
        for st, cnt in dims:
            d = st * (cnt - 1)
            if d < 0:
                lo += d
            else:
                hi += d
        return (t.name, 0, 1, lo * es, (hi + 1) * es)
    if t.name.startswith('ps'):
        return (t.name, 0, 128, 0, 2048)
    pstep, pcnt = dims[0]
    p0 = off // pstep
    c0 = off - p0 * pstep
    lo = hi = c0
    for st, cnt in dims[1:]:
        d = st * (cnt - 1)
        if d < 0:
            lo += d
        else:
            hi += d
    return (t.name, p0, p0 + pcnt, lo * es, (hi + 1) * es)


class Op:
    __slots__ = ('eng', 'fn', 'seq', 'waits', 'signal', 'dma', 'snap', 'sigcount')

    def __init__(self, eng, fn):
        self.eng = eng
        self.fn = fn
        self.waits = []
        self.signal = False
        self.dma = None
        self.snap = None


class Prog:
    ENGS = ('pe', 'act', 'dve', 'pool', 'sp')

    def __init__(self, nc, n_dma_sems=32):
        self.nc = nc
        self.ops = {e: [] for e in self.ENGS}
        self.track = {}
        self.seen = {e: {f: -1 for f in self.ENGS} for e in self.ENGS}
        self.seen_dma = {e: {} for e in self.ENGS}
        self.n_dma_sems = n_dma_sems
        self.dma_sem_val = [0] * n_dma_sems
        h = n_dma_sems // 2
        self.dma_pool = {'sp': list(range(0, h)), 'act': list(range(0, h)), 'pool': list(range(h, n_dma_sems))}
        self.dma_rr = {'sp': 0, 'act': 0, 'pool': 0}
        self.nops = 0

    def _deps(self, regs_r, regs_w, eng=None):
        deps = []
        for (key, p0, p1, b0, b1) in regs_r:
            lst = self.track.get(key)
            if lst:
                psum = key.startswith('ps')
                for ent in lst:
                    if ent[5] and ent[0] < p1 and p0 < ent[1] and ent[2] < b1 and b0 < ent[3]:
                        deps.append(ent[4])
                    elif psum and (not ent[5]) and ent[4].eng != eng:
                        deps.append(ent[4])
        for (key, p0, p1, b0, b1) in regs_w:
            lst = self.track.get(key)
            if lst:
                for ent in lst:
                    if ent[0] < p1 and p0 < ent[1] and ent[2] < b1 and b0 < ent[3]:
                        deps.append(ent[4])
        return deps

    def _record(self, op, regs_r, regs_w):
        for (key, p0, p1, b0, b1) in regs_w:
            lst = self.track.setdefault(key, [])
            lst[:] = [e for e in lst if not (p0 <= e[0] and e[1] <= p1 and b0 <= e[2] and e[3] <= b1)]
            lst.append([p0, p1, b0, b1, op, True])
        for (key, p0, p1, b0, b1) in regs_r:
            lst = self.track.setdefault(key, [])
            lst[:] = [e for e in lst if not ((not e[5]) and e[4].eng == op.eng and e[4].dma is None
                                             and op.dma is None
                                             and p0 <= e[0] and e[1] <= p1 and b0 <= e[2] and e[3] <= b1)]
            lst.append([p0, p1, b0, b1, op, False])

    def add(self, eng, fn, reads=(), writes=(), dma=False):
        op = Op(eng, fn)
        op.seq = len(self.ops[eng])
        regs_r = [region(a) for a in reads]
        regs_w = [region(a) for a in writes]
        deps = self._deps(regs_r, regs_w, eng)
        seen = self.seen[eng]
        seen_d = self.seen_dma[eng]
        need_e = {}
        need_d = {}
        for d in deps:
            if d.dma is not None:
                s, v = d.dma
                if seen_d.get(s, 0) < v and need_d.get(s, 0) < v:
                    need_d[s] = v
            else:
                if d.eng == 'pe' and eng == 'pe':
                    continue
                if d.seq > seen[d.eng]:
                    if d.eng not in need_e or need_e[d.eng].seq < d.seq:
                        need_e[d.eng] = d
        if dma:
            pool = self.dma_pool[eng]
            s = pool[self.dma_rr[eng] % len(pool)]
            self.dma_rr[eng] += 1
            prev = self.dma_sem_val[s]
            if prev > 0 and seen_d.get(s, 0) < prev and need_d.get(s, 0) < prev:
                need_d[s] = prev
            self.dma_sem_val[s] = prev + 16
            op.dma = (s, prev + 16)
        for f, d in need_e.items():
            d.signal = True
            op.waits.append(('e', f, d))
            seen[f] = max(seen[f], d.seq)
            if d.snap is not None:
                se, sd = d.snap
                for g, v in se.items():
                    if v > seen[g]:
                        seen[g] = v
                for g, v in sd.items():
                    if v > seen_d.get(g, 0):
                        seen_d[g] = v
        for s, v in need_d.items():
            op.waits.append(('d', s, v))
            seen_d[s] = max(seen_d.get(s, 0), v)
        if not dma:
            op.snap = (dict(seen), dict(seen_d))
        self.ops[eng].append(op)
        self._record(op, regs_r, regs_w)
        self.nops += 1
        return op

    def wait_all_dma(self, eng='sp'):
        op = Op(eng, None)
        op.seq = len(self.ops[eng])
        for s in range(self.n_dma_sems):
            if self.dma_sem_val[s] > 0:
                op.waits.append(('d', s, self.dma_sem_val[s]))
        self.ops[eng].append(op)

    def emit(self, stack):
        nc = self.nc
        esem = {e: stack.enter_context(nc.semaphore("s_" + e)) for e in self.ENGS}
        dsem = [stack.enter_context(nc.semaphore("d_%d" % i)) for i in range(self.n_dma_sems)]
        for e in self.ENGS:
            c = 0
            for op in self.ops[e]:
                if op.signal:
                    c += 1
                op.sigcount = c
        block = stack.enter_context(nc.Block())
        engmap = {'pe': block.tensor, 'act': block.scalar, 'dve': block.vector,
                  'pool': block.gpsimd, 'sp': block.sync}

        def make(e):
            ops = self.ops[e]

            def body(eng):
                for op in ops:
                    for w in op.waits:
                        if w[0] == 'e':
                            eng.wait_ge(esem[w[1]], w[2].sigcount)
                        else:
                            eng.wait_ge(dsem[w[1]], w[2])
                    if op.fn is None:
                        continue
                    ins = op.fn(eng)
                    if op.dma is not None:
                        ins.then_inc(dsem[op.dma[0]], 16)
                    elif op.signal:
                        ins.then_inc(esem[e], 1)
            return body

        for e in self.ENGS:
            if self.ops[e]:
                engmap[e](make(e))


R0 = 0
R1 = 32768
R2 = 65536
R3 = 98304
R4 = 122880
O_QT = [R0 + 0, R0 + 4096]
O_KT = [R0 + 8192, R0 + 12288]
O_VA = R0 + 16384
O_QN = R0 + 24576
O_CQB = R2
O_CKVB = R2 + 12288
O_WA = R2 + 20480
O_WQ = R2 + 20480
O_PPT = R3
O_VSH = R2
O_WKV = R3
O_XSQ = R3
O_KROPE = R3 + 8192
O_QR = R3 + 10240
O_GAM = R3 + 14336
O_DEL = R3 + 15360
O_DELN = R3 + 16384
O_PT = [R3 + 17408, R3 + 18432, R3 + 19456]
O_SQT = O_PT
O_KRRAW = R3 + 20480
O_REC = R3 + 22528
O_QS = [R3 + 8192, R3 + 9216]
O_SGC = [R3 + 10240, R3 + 11264]
O_OG = [R3 + 12288, R3 + 13312]
O_SP = [R3 + 14336, R3 + 15360]
O_SPACC = [R3 + 16384, R3 + 17408]
O_A = [R3 + 18432, R3 + 19456]
O_E = [R3 + 20480, R3 + 22528]
O_RING = [R4, R4 + 2048, R4 + 4096]
O_TMP = [R4 + 6144, R4 + 8192]
O_XIN = R4 + 6144
O_RB = [R4 + 10240, R4 + 10240]
O_CONSTB = R4 + 12288
O_IDENTF = O_CONSTB + 1792
O_COS = O_IDENTF + 512
O_SIN = O_COS + 1024
O_GQ2 = O_SIN + 1024
O_GKN2 = O_GQ2 + 768
O_GKR = O_GKN2 + 512
O_STAT = O_GKR + 128
O_RECH = O_STAT + 384
O_RECL = O_RECH + 1024
O_GCOL = O_RECL + 1024
O_PST = O_GCOL + 64
O_POSF = O_PST + 512
O_INV = O_POSF + 128
O_ANG = O_INV + 64
ARENA_BYTES = O_ANG + 3072
N_CONSTB = 7
C_IDENT, C_NEGU, C_NEGONES, C_ONES, C_MASKM, C_MASKS, C_BSEL = range(7)


def build_program(n_layers=4, dbg=None, stage=99, layers=None):
    nc = bass.Bass("TRN2", target_bir_lowering=False)
    dt_in = lambda n, s, d=F32: nc.dram_tensor(n, s, d, kind="ExternalInput").ap()
    x_d = dt_in("x", [S, D])
    p_d = dt_in("p", [4, S, 256])
    pos_d = dt_in("pos", [128, NT], I32)
    cb_d = dt_in("constb", [128, N_CONSTB * 128])
    cf_d = dt_in("constf", [128, 128 + 16])
    mla_g_d = dt_in("mla_gcol", [2, 128, 13])
    mla_wa_d = dt_in("mla_wa", [2, 128, 8, 672])
    mla_wg_d = dt_in("mla_wgate", [2, 8, 128, 8, 128])
    mla_wq_d = dt_in("mla_wq", [2, 128, 3, 1536])
    mla_wkv_d = dt_in("mla_wkv", [2, 128, 2, 2048])
    mla_hg_d = dt_in("mla_hg", [2, 128, 352])
    mla_wo_d = dt_in("mla_wo", [2, 8, 128, 8, 128])
    kv_g_d = dt_in("kv_gcol", [128, 8])
    kv_wk_d = dt_in("kv_wk", [8, 128, 8, 128])
    kv_wv_d = dt_in("kv_wv", [8, 128, 8, 128])
    sb_g_d = dt_in("sb_gcol", [2, 128, 8])
    sb_wq_d = dt_in("sb_wq", [2, 8, 128, 8, 128])
    sb_wgt_d = dt_in("sb_wgate", [2, 8, 128, 8, 128])
    sb_wo_d = dt_in("sb_wo", [2, 8, 128, 1024])
    ple_wg_d = dt_in("ple_wg", [4, 8, 128, 8, 128])
    ple_wp_d = dt_in("ple_wp", [4, 8, 128, 2, 128])
    y_d = nc.dram_tensor("y", [S, D], F32, kind="ExternalOutput").ap()
    dbg_d = None
    if dbg is not None:
        dbg_d = nc.dram_tensor("dbg", list(dbg[1]), F32, kind="ExternalOutput").ap()

    st = ExitStack()
    XTt = st.enter_context(nc.sbuf_tensor("XT", [128, 8 * S], F32))
    AR = st.enter_context(nc.sbuf_tensor("AR", [128, ARENA_BYTES // 2], BF16))
    PS = [st.enter_context(nc.psum_tensor("ps%d" % i, [128, 512], F32)) for i in range(8)]
    PSB = [t.bitcast(BF16) for t in PS]
    P = Prog(nc)

    def av(off, n, dt=BF16):
        v = AR[:, off // 2: off // 2 + (n * esize(dt)) // 2]
        return v if dt == BF16 else v.bitcast(dt)

    XT = XTt[:].rearrange("p (k t) -> p k t", k=8)

    def mm(out, lhsT, rhs, start=True, stop=True, skip=False):
        kw = dict(start=start, stop=stop)
        if skip:
            kw['skip_group_check'] = True
        P.add('pe', lambda q: q.matmul(out, lhsT=lhsT, rhs=rhs, **kw),
              reads=[lhsT, rhs] + ([] if start else [out]), writes=[out])

    def tr(out, in_, ident):
        P.add('pe', lambda q: q.transpose(out, in_, ident), reads=[in_, ident], writes=[out])

    def act(out, in_, func, scale=1.0, bias=0.0, eng='act'):
        rd = [in_]
        if not isinstance(scale, (int, float)):
            rd.append(scale)
        if not isinstance(bias, (int, float)):
            rd.append(bias)
        P.add('act', lambda q: q.activation(out=out, in_=in_, func=func, scale=scale, bias=bias),
              reads=rd, writes=[out])

    def tt(out, in0, in1, op, eng='dve'):
        P.add(eng, lambda q: q.tensor_tensor(out=out, in0=in0, in1=in1, op=op), reads=[in0, in1], writes=[out])

    def ts(out, in0, s1, op0, s2=None, op1=None, eng='dve'):
        rd = [in0]
        if not isinstance(s1, (int, float)):
            rd.append(s1)
        if s2 is not None and not isinstance(s2, (int, float)):
            rd.append(s2)
        if op1 is None:
            P.add(eng, lambda q: q.tensor_scalar(out=out, in0=in0, scalar1=s1, scalar2=None, op0=op0),
                  reads=rd, writes=[out])
        else:
            P.add(eng, lambda q: q.tensor_scalar(out=out, in0=in0, scalar1=s1, scalar2=s2, op0=op0, op1=op1),
                  reads=rd, writes=[out])

    def stt(out, in0, scalar, in1, op0, op1):
        rd = [in0, in1]
        if not isinstance(scalar, (int, float)):
            rd.append(scalar)
        P.add('dve', lambda q: q.scalar_tensor_tensor(out=out, in0=in0, scalar=scalar, in1=in1, op0=op0, op1=op1),
              reads=rd, writes=[out])

    def cp(out, in_, eng='dve'):
        if eng == 'act':
            P.add('act', lambda q: q.copy(out=out, in_=in_), reads=[in_], writes=[out])
        else:
            P.add(eng, lambda q: q.tensor_copy(out=out, in_=in_), reads=[in_], writes=[out])

    def red(out, in_, eng='dve'):
        P.add(eng, lambda q: q.tensor_reduce(out=out, in_=in_, op=ALU.add, axis=AX.X), reads=[in_], writes=[out])

    def recip(out, in_):
        P.add('dve', lambda q: q.reciprocal(out=out, in_=in_), reads=[in_], writes=[out])

    def memset(ap, val, eng='dve'):
        P.add(eng, lambda q: q.memset(ap, val), reads=[], writes=[ap])

    def dma(out, in_, eng='sp'):
        P.add(eng, lambda q: q.dma_start(out=out, in_=in_), reads=[in_], writes=[out], dma=True)

    def rsqrt_to(out, in_, scale, tmp):
        act(tmp, in_, AF.Ln, scale=scale, bias=EPSC)
        act(out, tmp, AF.Exp, scale=-0.5)

    CB = av(O_CONSTB, N_CONSTB * 128).rearrange("p (c n) -> p c n", c=N_CONSTB)
    dma(av(O_CONSTB, N_CONSTB * 128), cb_d, eng='pool')
    IDENT = CB[:, C_IDENT, :]
    NEGU = CB[:, C_NEGU, :]
    NEGONES = CB[:, C_NEGONES, :]
    ONES = CB[:, C_ONES, :]
    MASKM = CB[:, C_MASKM, :]
    MASKS = CB[:, C_MASKS, :]
    BSEL = CB[:, C_BSEL, :]
    IDENTF = av(O_IDENTF, 128, F32)
    dma(IDENTF, cf_d[:, 0:128])
    INV = av(O_INV, 16, F32)
    dma(INV, cf_d[:, 128:144])
    STAT = av(O_STAT, 96, F32)
    EPSC = STAT[:, 95:96]
    memset(EPSC, EPS)
    GCOL = av(O_GCOL, 16, F32)
    psrot = [0]

    def bank(i=None):
        if i is None:
            psrot[0] = (psrot[0] + 1) % 2
            return 6 + psrot[0]
        return i

    TMP = [av(o, 512, F32) for o in O_TMP]
    RBt = [av(o, 512, F32) for o in O_RB]
    RING = [av(o, 1024).rearrange("p (k n) -> p k n", k=8) for o in O_RING]
    ring_i = [0]

    def ring_load(src, shape=None):
        i = ring_i[0] % 3
        ring_i[0] += 1
        k, n = src.shape[1], src.shape[2]
        dst = av(O_RING[i], k * n).rearrange("p (k n) -> p k n", k=k)
        dma(dst, src, eng='pool')
        return dst

    XB = av(R0, 8 * S).rearrange("p (k t) -> p k t", k=8)
    XSQ = av(O_XSQ, 8 * 512).rearrange("p (k t) -> p k t", k=8)

    XIN = av(O_XIN, 1024, F32)
    for t in range(NT):
        dma(XIN, x_d[t * 128:(t + 1) * 128, :])
        for half in range(2):
            b = 4 + half
            for q4 in range(4):
                kc = half * 4 + q4
                tr(PS[b][:, q4 * 128:(q4 + 1) * 128], XIN[:, kc * 128:(kc + 1) * 128], IDENTF)
            dst = XT[:, half * 4:(half + 1) * 4, t * 128:(t + 1) * 128]
            src = PS[b][:, :].rearrange("p (k t) -> p k t", k=4)
            if half == 0:
                cp(dst, src, eng='dve')
            else:
                cp(dst, src, eng='act')

    COS = av(O_COS, 256, F32).rearrange("p (t i) -> p t i", t=NT)
    SIN = av(O_SIN, 256, F32).rearrange("p (t i) -> p t i", t=NT)
    if True:
        POSI = av(O_POSF + 64, 16, I32)
        POSF = av(O_POSF, 16, F32)
        dma(POSI, pos_d)
        cp(POSF, POSI)
        ANG = av(O_ANG, 256, F32)
        A2 = av(O_ANG + 1024, 256, F32)
        A3 = av(O_ANG + 2048, 256, F32)
        KI = av(O_ANG + 2048, 256, I32)
        ANG3 = ANG.rearrange("p (t i) -> p t i", t=NT)
        tt(ANG3, POSF.unsqueeze(2).broadcast_to([128, NT, 16]), INV.unsqueeze(1).broadcast_to([128, NT, 16]), ALU.mult)
        TWO_PI = 2.0 * math.pi

        def reduce_to_pi(dst, src):
            ts(A2, src, 1.0 / TWO_PI, ALU.mult)
            cp(KI, A2)
            cp(A2, KI)
            stt(dst, A2, -TWO_PI, src, ALU.mult, ALU.add)
            ts(A2, dst, math.pi, ALU.is_gt, -TWO_PI, ALU.mult)
            tt(dst, dst, A2, ALU.add)
            ts(A2, dst, -math.pi, ALU.is_lt, TWO_PI, ALU.mult)
            tt(dst, dst, A2, ALU.add)

        SINr = av(O_SIN, 256, F32)
        COSr = av(O_COS, 256, F32)
        reduce_to_pi(SINr, ANG)
        ts(ANG, ANG, math.pi / 2.0, ALU.add)
        reduce_to_pi(COSr, ANG)
        act(SINr, SINr, AF.Sin)
        act(COSr, COSr, AF.Sin)

    def prep_norm(gcols):
        for j in range(NJ):
            cols = slice(j * 512, (j + 1) * 512)
            for kc in range(8):
                act(XSQ[:, kc, :], XT[:, kc, cols], AF.Square)
            b = bank()
            for kc in range(8):
                mm(PS[b][:, :], ONES, XSQ[:, kc, :], start=(kc == 0), stop=(kc == 7))
            rb = RBt[j % 2]
            rsqrt_to(rb, PS[b][:, :], 1.0 / D, TMP[j % 2])
            for kc in range(8):
                stt(XB[:, kc, cols], XT[:, kc, cols], gcols[:, kc:kc + 1], rb, ALU.mult, ALU.mult)

    PPT = av(O_PPT, 2 * S).rearrange("p (k t) -> p k t", k=2)

    def ple(i):
        PST = [av(O_PST, 256), av(O_PST, 256)]
        for t in range(NT):
            pst = PST[t % 2]
            dma(pst, p_d[i, t * 128:(t + 1) * 128, :], eng='pool')
            b = bank()
            for kc in range(2):
                tr(PSB[b][:, kc * 128:(kc + 1) * 128], pst[:, kc * 128:(kc + 1) * 128], IDENT)
            cp(PPT[:, :, t * 128:(t + 1) * 128], PSB[b][:, 0:256].rearrange("p (k t) -> p k t", k=2),
               eng='act' if t % 2 else 'dve')
        for n in range(8):
            wg = ring_load(ple_wg_d[i, n])
            wp = ring_load(ple_wp_d[i, n])
            for j in range(NJ):
                cols = slice(j * 512, (j + 1) * 512)
                bg = bank()
                for kc in range(8):
                    mm(PS[bg][:, :], wg[:, kc, :], XB[:, kc, cols], start=(kc == 0), stop=(kc == 7))
                tmp = TMP[j % 2]
                act(tmp, PS[bg][:, :], AF.Sigmoid)
                bp = bank()
                for kc in range(2):
                    mm(PS[bp][:, :], wp[:, kc, :], PPT[:, kc, cols], start=(kc == 0), stop=(kc == 1))
                tt(tmp, tmp, PS[bp][:, :], ALU.mult)
                tt(XT[:, n, cols], XT[:, n, cols], tmp, ALU.add)

    SG = av(R1, 8 * S).rearrange("p (k t) -> p k t", k=8)
    CQB = av(O_CQB, 3 * S).rearrange("p (k t) -> p k t", k=3)
    CKVB = av(O_CKVB, 2 * S).rearrange("p (k t) -> p k t", k=2)
    KRRAW = av(O_KRRAW, NT * 32, F32).rearrange("p (t i) -> p t i", t=NT)
    KROPE = av(O_KROPE, NT * 32, F32).rearrange("p (t i) -> p t i", t=NT)
    GAM = av(O_GAM, 256, F32)
    DEL = av(O_DEL, 256, F32)
    DELN = av(O_DELN, 256, F32)
    GQ2 = av(O_GQ2, 192, F32)
    GKN2 = av(O_GKN2, 128, F32)
    GKR = av(O_GKR, 32, F32)
    BQ = STAT[:, 0:16]
    BKV = STAT[:, 16:32]
    SSKR = STAT[:, 32:48]
    ST1 = STAT[:, 48:64]
    ST2 = STAT[:, 64:80]

    def mla_layer(li):
        dma(GCOL[:, 0:13], mla_g_d[li])
        if stage == 1:
            return
        prep_norm(GCOL[:, 0:8])
        if stage == 1.5:
            return
        dma(av(O_GQ2, 352, F32), mla_hg_d[li])
        ts(GQ2, GQ2, 96.0 ** -0.5, ALU.mult)
        if stage < 2:
            return
        WA = av(O_WA, 8 * 672).rearrange("p (k n) -> p k n", k=8)
        for kc in range(8):
            dma(WA[:, kc, :], mla_wa_d[li, :, kc, :], eng='pool')
        SQT = [av(o, 512) for o in O_SQT]
        for j in range(NJ):
            cols = slice(j * 512, (j + 1) * 512)
            for (nblk, c0, dstT, gofs, statcol) in ((3, 0, CQB, 8, 0), (2, 384, CKVB, 11, 1)):
                for cb in range(nblk):
                    b = bank()
                    for kc in range(8):
                        mm(PS[b][:, :], WA[:, kc, c0 + cb * 128:c0 + (cb + 1) * 128], XB[:, kc, cols],
                           start=(kc == 0), stop=(kc == 7))
                    ts(dstT[:, cb, cols], PS[b][:, :], GCOL[:, gofs + cb:gofs + cb + 1], ALU.mult)
                    if stage >= 2.2:
                        act(SQT[cb], PS[b][:, :], AF.Square)
                for t4 in range(4):
                    if stage < 2.3:
                        break
                    t = j * 4 + t4
                    for cb in range(nblk):
                        mm(PS[4][:, statcol * 16 + t:statcol * 16 + t + 1], SQT[cb][:, t4 * 128:(t4 + 1) * 128],
                           ONES[:, 0:1], start=(cb == 0), stop=(cb == nblk - 1))
            for t4 in range(4):
                if stage < 2.4:
                    break
                t = j * 4 + t4
                for kc in range(8):
                    mm(PS[5][:, t * 32:(t + 1) * 32], XB[:, kc, t * 128:(t + 1) * 128], WA[:, kc, 640:672],
                       start=(kc == 0), stop=(kc == 7))
        if stage < 2.5:
            return
        cp(KRRAW, PS[5][:, :].rearrange("p (t i) -> p t i", t=NT))
        if stage < 2.6:
            return
        rsqrt_to(BQ, PS[4][:, 0:16], 1.0 / 384, ST1)
        rsqrt_to(BKV, PS[4][:, 16:32], 1.0 / 256, ST2)
        if stage < 3:
            return
        WQ = av(O_WQ, 3 * 1536).rearrange("p (k n) -> p k n", k=3)
        WKV = av(O_WKV, 2 * 2048).rearrange("p (k n) -> p k n", k=2)
        for kc in range(3):
            dma(WQ[:, kc, :], mla_wq_d[li, :, kc, :], eng='pool')
        for kc in range(2):
            dma(WKV[:, kc, :], mla_wkv_d[li, :, kc, :], eng='pool')
        GAM3 = GAM.rearrange("p (t h) -> p t h", t=NT)
        DEL3 = DEL.rearrange("p (t h) -> p t h", t=NT)
        DELN3 = DELN.rearrange("p (t h) -> p t h", t=NT)
        b0_items = []
        b0n = [0]

        def mk_q(t, qb):
            def f():
                tcols = slice(t * 128, (t + 1) * 128)
                b = b0n[0] % 4
                b0n[0] += 1
                for kc in range(3):
                    mm(PS[b][:, 0:384], CQB[:, kc, tcols], WQ[:, kc, qb * 384:(qb + 1) * 384],
                       start=(kc == 0), stop=(kc == 2))
                tmp = TMP[b % 2]
                act(tmp[:, 0:384], PS[b][:, 0:384], AF.Square)
                red(GAM3[:, t, qb * 4:(qb + 1) * 4], tmp[:, 0:384].rearrange("p (h d) -> p h d", h=4))
            return f

        def mk_k(t, kb4):
            def f():
                tcols = slice(t * 128, (t + 1) * 128)
                b = b0n[0] % 4
                b0n[0] += 1
                for kc in range(2):
                    mm(PS[b][:, :], CKVB[:, kc, tcols], WKV[:, kc, kb4 * 512:(kb4 + 1) * 512],
                       start=(kc == 0), stop=(kc == 1))
                tmp = TMP[b % 2]
                act(tmp[:, 0:256].rearrange("p (h d) -> p h d", h=4),
                    PS[b][:, :].rearrange("p (h d) -> p h d", h=4)[:, :, 0:64], AF.Square)
                red(DEL3[:, t, kb4 * 4:(kb4 + 1) * 4], tmp[:, 0:256].rearrange("p (h d) -> p h d", h=4))
            return f
        for t in range(NT):
            for qb in range(4):
                b0_items.append(mk_q(t, qb))
            for kb4 in range(4):
                b0_items.append(mk_k(t, kb4))
        for cb in range(8):
            w = ring_load(mla_wg_d[li, cb])
            for j in range(NJ):
                cols = slice(j * 512, (j + 1) * 512)
                b = bank()
                for kc in range(8):
                    mm(PS[b][:, :], w[:, kc, :], XB[:, kc, cols], start=(kc == 0), stop=(kc == 7))
                act(SG[:, cb, cols], PS[b][:, :], AF.Silu)
                for _ in range(4):
                    if b0_items:
                        b0_items.pop(0)()
        while b0_items:
            b0_items.pop(0)()
        if stage < 4:
            return
        KRSQ = av(O_QR, NT * 32, F32).rearrange("p (t i) -> p t i", t=NT)
        tt(KRSQ, KRRAW, KRRAW, ALU.mult)
        red(SSKR, KRSQ)
        KRG = av(O_QR + 2048, NT * 32, F32).rearrange("p (t i) -> p t i", t=NT)
        tt(KRG, KRRAW, GKR.unsqueeze(1).broadcast_to([128, NT, 32]), ALU.mult)

        def rope(dst1, dst2, x1, x2, cosb, sinb, t1, t2):
            tt(t1, x1, cosb, ALU.mult)
            tt(t2, x2, sinb, ALU.mult)
            tt(dst1, t1, t2, ALU.subtract)
            tt(t1, x2, cosb, ALU.mult)
            tt(t2, x1, sinb, ALU.mult)
            tt(dst2, t1, t2, ALU.add)

        RT1 = av(O_ANG, 256, F32).rearrange("p (t i) -> p t i", t=NT)
        RT2 = av(O_ANG + 1024, 256, F32).rearrange("p (t i) -> p t i", t=NT)
        rope(KROPE[:, :, 0:16], KROPE[:, :, 16:32], KRG[:, :, 0:16], KRG[:, :, 16:32], COS, SIN, RT1, RT2)
        if stage < 5:
            return
        BQb = BQ.unsqueeze(2).broadcast_to([128, NT, H])
        BKVb = BKV.unsqueeze(2).broadcast_to([128, NT, H])
        T256 = av(O_ANG, 256, F32)
        T256b = av(O_ANG + 1024, 256, F32)
        T3 = T256.rearrange("p (t h) -> p t h", t=NT)
        tt(T3, GAM3, BQb, ALU.mult)
        tt(T3, T3, BQb, ALU.mult)
        rsqrt_to(GAM, T256, 1.0 / 96, T256b)
        tt(GAM3, GAM3, BQb, ALU.mult)
        tt(T3, DEL3, BKVb, ALU.mult)
        tt(T3, T3, BKVb, ALU.mult)
        tt(T3, T3, SSKR.unsqueeze(2).broadcast_to([128, NT, H]), ALU.add)
        rsqrt_to(DEL, T256, 1.0 / 96, T256b)
        tt(DELN3, DEL3, BKVb, ALU.mult)
        if stage < 6:
            return
        QT = [av(o, S) for o in O_QT]
        KT = [av(o, S) for o in O_KT]
        VA = av(O_VA, NT * 256).rearrange("p (t c) -> p t c", t=NT)
        QN = av(O_QN, NT * 192).rearrange("p (t c) -> p t c", t=NT)
        QR = av(O_QR, NT * 64, F32).rearrange("p (t h i) -> p t h i", t=NT, h=2)
        memset(VA[:, :, 64:192], 0.0)
        memset(VA[:, :, 64:65], 1.0)
        memset(VA[:, :, 128:129], 1.0)
        PT = [av(o, 512) for o in O_PT]
        REC = av(O_REC, 512, F32)
        RECH = av(O_RECH, 512)
        RECL = av(O_RECL, 512)
        pti = [0]
        COSb = COS.unsqueeze(2).broadcast_to([128, NT, 2, 16])
        SINb = SIN.unsqueeze(2).broadcast_to([128, NT, 2, 16])
        for c in range(8):
            QN4 = QN.rearrange("p t (h d) -> p t h d", h=2)
            for t in range(NT):
                tcols = slice(t * 128, (t + 1) * 128)
                b = bank()
                for kc in range(3):
                    mm(PS[b][:, 0:192], CQB[:, kc, tcols], WQ[:, kc, c * 192:(c + 1) * 192],
                       start=(kc == 0), stop=(kc == 2))
                tmp = TMP[t % 2]
                t3 = tmp[:, 0:192].rearrange("p (h d) -> p h d", h=2)
                tt(t3, PS[b][:, 0:192].rearrange("p (h d) -> p h d", h=2),
                   GAM3[:, t, 2 * c:2 * c + 2].unsqueeze(2).broadcast_to([128, 2, 96]), ALU.mult)
                g3 = GQ2.rearrange("p (h d) -> p h d", h=2)
                tt(QN4[:, t, :, 0:64], t3[:, :, 0:64], g3[:, :, 0:64], ALU.mult)
                tt(QR[:, t, :, :], t3[:, :, 64:96], g3[:, :, 64:96], ALU.mult)
            RA1 = av(O_ANG, 512, F32).rearrange("p (t h i) -> p t h i", t=NT, h=2)
            RA2 = av(O_KRRAW, 512, F32).rearrange("p (t h i) -> p t h i", t=NT, h=2)
            rope(QN4[:, :, :, 64:80], QN4[:, :, :, 80:96], QR[:, :, :, 0:16], QR[:, :, :, 16:32], COSb, SINb, RA1, RA2)
            for hh in range(2):
                for half in range(2):
                    b = bank()
                    for t8 in range(8):
                        t = half * 8 + t8
                        tr(PSB[b][0:96, t8 * 128:(t8 + 1) * 128], QN[:, t, hh * 96:(hh + 1) * 96], IDENT)
                    cp(QT[hh][0:96, half * 1024:(half + 1) * 1024], PSB[b][0:96, :], eng='act' if half else 'dve')
            for t in range(NT):
                tcols = slice(t * 128, (t + 1) * 128)
                b = bank()
                for kc in range(2):
                    mm(PS[b][:, 0:256], CKVB[:, kc, tcols], WKV[:, kc, c * 256:(c + 1) * 256],
                       start=(kc == 0), stop=(kc == 1))
                p3 = PS[b][:, 0:256].rearrange("p (h d) -> p h d", h=2)
                tmp = TMP[t % 2]
                t3 = tmp[:, 0:128].rearrange("p (h d) -> p h d", h=2)
                tt(t3, p3[:, :, 0:64], DELN3[:, t, 2 * c:2 * c + 2].unsqueeze(2).broadcast_to([128, 2, 64]), ALU.mult)
                tt(QN4[:, t, :, 0:64], t3, GKN2.rearrange("p (h d) -> p h d", h=2), ALU.mult)
                vdst = VA[:, t, :].rearrange("p (a c) -> p a c", a=4)[:, 0:4:3, :]
                act(vdst, p3[:, :, 64:128], AF.Copy, scale=BKV[:, t:t + 1])
            tt(QN4[:, :, :, 64:96], KROPE.unsqueeze(2).broadcast_to([128, NT, 2, 32]),
               DEL3[:, :, 2 * c:2 * c + 2].unsqueeze(3).broadcast_to([128, NT, 2, 32]), ALU.mult)
            for hh in range(2):
                for half in range(2):
                    b = bank()
                    for t8 in range(8):
                        t = half * 8 + t8
                        tr(PSB[b][0:96, t8 * 128:(t8 + 1) * 128], QN[:, t, hh * 96:(hh + 1) * 96], IDENT)
                    cp(KT[hh][0:96, half * 1024:(half + 1) * 1024], PSB[b][0:96, :], eng='act' if half else 'dve')
            PT4 = [[PT[0], PT[1]], [PT[2], av(O_RB[0], 512)]]
            for j in range(NJ):
                nkb = 4 * j + 4
                cols = slice(j * 512, (j + 1) * 512)

                def s_mm(hh, kb):
                    r = kb - 4 * j
                    c0 = max(0, r) * 128
                    zb = 2 * hh + kb % 2
                    qs = slice(j * 512 + c0, (j + 1) * 512)
                    mm(PS[zb][:, c0:512], KT[hh][0:96, kb * 128:(kb + 1) * 128], QT[hh][0:96, qs],
                       start=True, stop=True)
                    if r >= 0:
                        mm(PS[zb][:, c0:c0 + 128], IDENT, MASKM, start=False, stop=True, skip=True)

                for hh in range(2):
                    s_mm(hh, 0)
                for kb in range(nkb):
                    r = kb - 4 * j
                    c0 = max(0, r) * 128
                    if kb + 1 < nkb:
                        for hh in range(2):
                            s_mm(hh, kb + 1)
                    for hh in range(2):
                        act(PT4[hh][kb % 2][:, c0:512], PS[2 * hh + kb % 2][:, c0:512], AF.Exp)
                    for hh in range(2):
                        pt = PT4[hh][kb % 2]
                        if hh == 0:
                            mm(PS[4][0:65, c0:512], VA[:, kb, 0:65], pt[:, c0:512],
                               start=(kb == 0), stop=(kb == nkb - 1), skip=True)
                        else:
                            mm(PS[5][:, c0:512], VA[:, kb, 128:256], pt[:, c0:512],
                               start=(kb == 0), stop=(kb == nkb - 1), skip=True)
                for hh in range(2):
                    ob = 4 + hh
                    if hh == 0:
                        drow, rows = 64, slice(0, 64)
                        lsel = BSEL[64:65, 0:64]
                    else:
                        drow, rows = 0, slice(64, 128)
                        lsel = BSEL[0:1, :]
                    rr = slice(drow, drow + 1)
                    recip(REC[rr, :], PS[ob][rr, :])
                    cp(RECH[rr, :], REC[rr, :])
                    tt(RECL[rr, :], REC[rr, :], RECH[rr, :], ALU.subtract)
                    bb = bank()
                    orow = slice(0, 64) if hh == 0 else slice(0, 128)
                    mm(PS[bb][orow, :], lsel, RECH[rr, :], start=True, stop=False)
                    mm(PS[bb][orow, :], lsel, RECL[rr, :], start=False, stop=True)
                    tmp = TMP[hh]
                    tt(tmp[rows, :], PS[bb][rows, :], SG[rows, c, cols], ALU.mult)
                    tt(SG[rows, c, cols], PS[ob][rows, :], tmp[rows, :], ALU.mult)
        for n in range(8):
            w = ring_load(mla_wo_d[li, n])
            for j in range(NJ):
                cols = slice(j * 512, (j + 1) * 512)
                b = bank()
                for kc in range(8):
                    mm(PS[b][:, :], w[:, kc, :], SG[:, kc, cols], start=(kc == 0), stop=(kc == 7))
                tt(XT[:, n, cols], XT[:, n, cols], PS[b][:, :], ALU.add)
                cp(XB[:, n, cols], XT[:, n, cols], eng='act')

    KSH = av(R1, 8 * S).rearrange("p (k t) -> p k t", k=8)
    VSH = av(O_VSH, NT * 1024).rearrange("p (t c) -> p t c", t=NT)

    def shared_kv():
        dma(GCOL[:, 0:8], kv_g_d)
        prep_norm(GCOL[:, 0:8])
        for n in range(8):
            w = ring_load(kv_wk_d[n])
            for j in range(NJ):
                cols = slice(j * 512, (j + 1) * 512)
                b = bank()
                for kc in range(8):
                    mm(PS[b][:, :], w[:, kc, :], XB[:, kc, cols], start=(kc == 0), stop=(kc == 7))
                cp(KSH[:, n, cols], PS[b][:, :], eng='act' if j % 2 else 'dve')
        for n in range(8):
            w = ring_load(kv_wv_d[n])
            for t in range(NT):
                b = bank()
                for kc in range(8):
                    mm(PS[b][:, 0:128], XB[:, kc, t * 128:(t + 1) * 128], w[:, kc, :], start=(kc == 0), stop=(kc == 7))
                cp(VSH[:, t, n * 128:(n + 1) * 128], PS[b][:, 0:128], eng='act' if t % 2 else 'dve')

    def sb_layer(lj):
        from collections import deque
        dma(GCOL[:, 0:8], sb_g_d[lj])
        prep_norm(GCOL[:, 0:8])
        QSZ = [av(R3 + k * 2048, 1024) for k in range(4)]
        SGC = [av(R3 + 8192 + k * 1024, 512) for k in range(4)]
        OG = [av(O_RECH, 512), av(O_RECL, 512), av(O_COS, 512), av(O_SIN, 512)]
        for j in range(4):
            memset(QSZ[j][64:128, 0:512], 0.0)
            memset(QSZ[j][0:64, 512:1024], 0.0)
        SP = [av(R3 + 12288 + k * 1024, 512) for k in range(4)]
        SPACC = [av(R3 + 16384 + k * 1024, 512) for k in range(4)]
        AT = [av(R3 + 20480 + k * 1024, 512) for k in range(4)]
        E = [av(O_TMP[0], 512, F32), av(O_TMP[1], 512, F32), av(O_RB[0], 512, F32), av(O_ANG, 512, F32)]
        WQ_ = av(O_RING[0], 1024).rearrange("p (k n) -> p k n", k=8)
        WG_ = av(O_RING[1], 1024).rearrange("p (k n) -> p k n", k=8)
        WO_ = av(O_RING[2], 1024)
        fifo = deque()

        def bg(n):
            while n > 0 and fifo:
                fifo.popleft()[1]()
                n -= 1

        def drain(pred):
            while any(pred(k) for k, _ in fifo):
                fifo.popleft()[1]()

        def load_w(c):
            dma(WQ_, sb_wq_d[lj, c], eng='pool')
            dma(WG_, sb_wgt_d[lj, c], eng='pool')

        def load_wo(c):
            dma(WO_, sb_wo_d[lj, c], eng='pool')

        def proj_items(c, j):
            cols = slice(j * 512, (j + 1) * 512)
            key = ('proj', c, j)
            st_ = {}
            items = []

            def mk_mm(w, kc, which):
                def f():
                    if kc == 0:
                        st_[which] = bank()
                    mm(PS[st_[which]][:, :], w[:, kc, :], XB[:, kc, cols], start=(kc == 0), stop=(kc == 7))
                return f
            for kc in range(8):
                items.append((key, mk_mm(WQ_, kc, 'q')))

            def evq():
                b = st_['q']
                ts(QSZ[j][0:64, 0:512], PS[b][0:64, :], 0.125, ALU.mult)
                ts(QSZ[j][64:128, 512:1024], PS[b][64:128, :], 0.125, ALU.mult)
            items.append((key, evq))
            for kc in range(8):
                items.append((key, mk_mm(WG_, kc, 'g')))
            items.append((key, lambda: act(SGC[j], PS[st_['g']][:, :], AF.Silu)))
            return items

        def wo_items(c, j):
            cols = slice(j * 512, (j + 1) * 512)
            items = []

            def mk(n):
                def f():
                    b = bank()
                    mm(PS[b][:, :], WO_[:, n * 128:(n + 1) * 128], OG[j], start=True, stop=True)
                    tt(XT[:, n, cols], XT[:, n, cols], PS[b][:, :], ALU.add)
                return f
            for n in range(8):
                items.append((('wo', c, j), mk(n)))
            return items

        load_w(0)
        load_wo(0)
        for j in (3, 2, 1, 0):
            for it in proj_items(0, j):
                it[1]()
        slots = []
        for k in range(4):
            seq = (3, 0) if k // 2 == 0 else (2, 1)
            blocks = [(c, j, kb) for c in range(8) for j in seq for kb in range(4 * j + 3, -1, -1)]
            slots.append((k % 2, k, 4 + k // 2, blocks))
        for r in range(160):
            cr, rl = r // 20, r % 20
            if rl == 9 and cr + 1 < 8:
                drain(lambda k: k[0] == 'proj' and k[1] == cr)
                load_w(cr + 1)
            if rl == 9 and cr >= 1:
                drain(lambda k: k[0] == 'wo' and k[1] == cr - 1)
                load_wo(cr)
            info = []
            for (hh, zb, ob, blocks) in slots:
                c, j, kb = blocks[r]
                rr = kb - 4 * j
                first = (kb == 4 * j + 3)
                if first:
                    drain(lambda k: k[0] == 'proj' and k[1] == c and k[2] == j)
                info.append((hh, zb, ob, j, kb, rr, max(0, rr) * 128, first, slice(hh * 64, hh * 64 + 64), c))
            for k, (hh, zb, ob, j, kb, rr, c0, first, rows, c) in enumerate(info):
                mm(PS[zb][:, c0:512], KSH[:, c, kb * 128:(kb + 1) * 128],
                   QSZ[j][:, hh * 512 + c0:hh * 512 + 512], start=True, stop=True)
                if rr >= 0:
                    mm(PS[zb][:, c0:c0 + 128], IDENT, MASKS, start=False, stop=True, skip=True)
            bg(3)
            for k, (hh, zb, ob, j, kb, rr, c0, first, rows, c) in enumerate(info):
                act(E[k][:, c0:512], PS[zb][:, c0:512], AF.Exp)
            for k, (hh, zb, ob, j, kb, rr, c0, first, rows, c) in enumerate(info):
                act(SP[k][:, c0:512], E[k][:, c0:512], AF.Ln, bias=ONEC)
            for k, (hh, zb, ob, j, kb, rr, c0, first, rows, c) in enumerate(info):
                mm(PS[zb][:, c0:512], NEGU, SP[k][:, c0:512], start=False, stop=True, skip=True)
                if not first:
                    a0 = c0 + 128 if rr >= 0 else 0
                    mm(PS[zb][:, a0:512], NEGONES, SPACC[k][:, a0:512], start=False, stop=True, skip=True)
            bg(3)
            for k, (hh, zb, ob, j, kb, rr, c0, first, rows, c) in enumerate(info):
                if kb > 0:
                    if first:
                        cp(SPACC[k][:, c0:512], SP[k][:, c0:512])
                    elif rr >= 0:
                        pc0 = c0 + 128
                        cp(SPACC[k][:, c0:pc0], SP[k][:, c0:pc0])
                        tt(SPACC[k][:, pc0:512], SPACC[k][:, pc0:512], SP[k][:, pc0:512], ALU.add)
                    else:
                        tt(SPACC[k][:, :], SPACC[k][:, :], SP[k][:, :], ALU.add)
            for k, (hh, zb, ob, j, kb, rr, c0, first, rows, c) in enumerate(info):
                act(AT[k][:, c0:512], PS[zb][:, c0:512], AF.Exp)
            for k, (hh, zb, ob, j, kb, rr, c0, first, rows, c) in enumerate(info):
                mm(PS[ob][rows, c0:512], VSH[:, kb, (2 * c + hh) * 64:(2 * c + hh + 1) * 64], AT[k][:, c0:512],
                   start=first, stop=(kb == 0), skip=True)
            bg(3)
            for k, (hh, zb, ob, j, kb, rr, c0, first, rows, c) in enumerate(info):
                if kb == 0 and hh == 1:
                    drain(lambda kk: kk[0] == 'wo' and kk[2] == j)
                    tt(OG[j], PS[ob][:, :], SGC[j], ALU.mult)
                    fifo.extend(wo_items(c, j))
                    if c + 1 < 8:
                        fifo.extend(proj_items(c + 1, j))
        bg(100000)
        for n in range(8):
            for j in range(NJ):
                cols = slice(j * 512, (j + 1) * 512)
                cp(XB[:, n, cols], XT[:, n, cols], eng='act')

    ONEC = STAT[:, 94:95]
    memset(ONEC, 1.0)

    for i in (layers if layers is not None else range(n_layers)):
        if stage < 1:
            break
        if i < 2:
            mla_layer(i)
        else:
            sb_layer(i - 2)
        if stage >= 8:
            ple(i)
        if i == 1 and n_layers > 2:
            shared_kv()

    if dbg is not None:
        name = dbg[0]
        if name == 'cos':
            dma(dbg_d[:, 0:256], av(O_COS, 256, F32))
            dma(dbg_d[:, 256:512], av(O_SIN, 256, F32))

    YO = av(O_XIN, 1024, F32)
    for t in range(NT):
        for half in range(2):
            b = 4 + half
            for q4 in range(4):
                kc = half * 4 + q4
                tr(PS[b][:, q4 * 128:(q4 + 1) * 128], XT[:, kc, t * 128:(t + 1) * 128], IDENTF)
            cp(YO[:, half * 512:(half + 1) * 512], PS[b][:, :], eng='act' if half else 'dve')
        dma(y_d[t * 128:(t + 1) * 128, :], YO)
    P.wait_all_dma('sp')
    P.emit(st)
    st.close()
    return nc, P


def _consts():
    j = np.arange(128)[:, None]
    s = np.arange(128)[None, :]
    cb = np.zeros((128, N_CONSTB, 128), np.float32)
    cb[:, C_IDENT] = (j == s)
    cb[:, C_NEGU] = -(j >= s).astype(np.float32)
    cb[:, C_NEGONES] = -1.0
    cb[:, C_ONES] = 1.0
    cb[:, C_MASKM] = np.where(s >= j, 0.0, NEG)
    cb[:, C_MASKS] = np.where(s > j, 0.0, NEG)
    bs = np.zeros((128, 128), np.float32)
    bs[64, 0:64] = 1.0
    bs[0, 64:128] = 1.0
    cb[:, C_BSEL] = bs
    cf = np.zeros((128, 144), np.float32)
    cf[:, 0:128] = np.eye(128, dtype=np.float32)
    half = 16
    inv = (1.0 / (np.float32(10000.0) ** (np.arange(half, dtype=np.float32) / np.float32(half)))).astype(np.float32)
    cf[:, 128:144] = inv[None, :]
    return cb.reshape(128, -1), cf


def _kt(w):
    K, N = w.shape
    return np.ascontiguousarray(w.reshape(K // 128, 128, N).transpose(1, 0, 2))


def _blk(w, nb=128):
    K, N = w.shape
    a = w.reshape(K // 128, 128, N // nb, nb).transpose(2, 1, 0, 3)
    return np.ascontiguousarray(a)


def _gcol(g):
    return np.ascontiguousarray(g.reshape(-1, 128).T)


def make_in_maps(inputs):
    f = lambda k: np.asarray(inputs[k], dtype=np.float32)
    cb, cf = _consts()
    mla_w_in = f("mla_w_in")
    shared = {
        "constb": cb, "constf": cf,
        "mla_gcol": np.stack([np.concatenate([_gcol(f("mla_ln_g")[l]), _gcol(f("mla_q_norm_g")[l]),
                                              _gcol(f("mla_kv_norm_g")[l])], axis=1) for l in range(2)]),
        "mla_wa": np.stack([_kt(mla_w_in[l][:, 0:672]) for l in range(2)]),
        "mla_wgate": np.stack([_blk(mla_w_in[l][:, 672:1696]) for l in range(2)]),
        "mla_wq": np.stack([_kt(f("mla_w_q_up")[l]) for l in range(2)]),
        "mla_wkv": np.stack([_kt(f("mla_w_kv_up")[l]) for l in range(2)]),
        "mla_hg": np.stack([np.ascontiguousarray(np.broadcast_to(np.concatenate(
            [f("mla_q_head_g")[l], f("mla_q_head_g")[l], f("mla_k_head_g")[l][:64], f("mla_k_head_g")[l][:64],
             f("mla_k_head_g")[l][64:]])[None, :], (128, 352))) for l in range(2)]),
        "mla_wo": np.stack([_blk(f("mla_w_out")[l]) for l in range(2)]),
        "kv_gcol": _gcol(f("kv_ln_g")),
        "kv_wk": _blk(f("w_kv_shared")[:, 0:1024]),
        "kv_wv": _blk(f("w_kv_shared")[:, 1024:2048]),
        "sb_gcol": np.stack([_gcol(f("sb_ln_g")[l]) for l in range(2)]),
        "sb_wq": np.stack([_blk(f("sb_w_in")[l][:, 0:1024]) for l in range(2)]),
        "sb_wgate": np.stack([_blk(f("sb_w_in")[l][:, 1024:2048]) for l in range(2)]),
        "sb_wo": np.ascontiguousarray(f("sb_w_out").reshape(2, 8, 128, 1024)),
        "ple_wg": np.stack([_blk(f("ple_w_gate")[l]) for l in range(4)]),
        "ple_wp": np.stack([_blk(f("ple_w_proj")[l]) for l in range(4)]),
    }
    x = f("x")
    p = f("p")
    pos = np.asarray(inputs["positions"]).astype(np.int32)
    maps = []
    for b in range(8):
        m = dict(shared)
        m["x"] = np.ascontiguousarray(x[b])
        m["p"] = np.ascontiguousarray(p[:, b])
        m["pos"] = np.ascontiguousarray(pos[b].reshape(NT, 128).T)
        maps.append(m)
    return maps


_CACHE = {}


def kernel(**inputs):
    maps = make_in_maps(inputs)
    if "nc" not in _CACHE:
        _CACHE["nc"] = build_program(4)[0]
    res = run_bass_kernel_spmd(_CACHE["nc"], maps, core_ids=list(range(8)))
    return np.stack([np.asarray(r["y"], dtype=np.float32) for r in res.results], axis=0)
```

```python
import math
from contextlib import ExitStack

import numpy as np
import concourse.bass as bass
import concourse.mybir as mybir
from concourse.bass_utils import run_bass_kernel_spmd

F32 = mybir.dt.float32
BF16 = mybir.dt.bfloat16
I32 = mybir.dt.int32
AF = mybir.ActivationFunctionType
ALU = mybir.AluOpType
AX = mybir.AxisListType

S = 2048
D = 1024
NT = 16
NJ = 4
H = 16
EPS = 1e-6
NEG = -30000.0

_ESZ = {}


def esize(dt):
    if dt not in _ESZ:
        _ESZ[dt] = mybir.dt.size(dt)
    return _ESZ[dt]


def region(ap):
    t = ap.tensor
    es = esize(ap.dtype)
    dims = ap.ap
    off = ap.offset
    if type(t).__name__.startswith('DRam'):
        lo = hi = off
        for st, cnt in dims:
            d = st * (cnt - 1)
            if d < 0:
                lo += d
            else:
                hi += d
        return (t.name, 0, 1, lo * es, (hi + 1) * es)
    if t.name.startswith('ps'):
        return (t.name, 0, 128, 0, 2048)
    pstep, pcnt = dims[0]
    p0 = off // pstep
    c0 = off - p0 * pstep
    lo = hi = c0
    for st, cnt in dims[1:]:
        d = st * (cnt - 1)
        if d < 0:
            lo += d
        else:
            hi += d
    return (t.name, p0, p0 + pcnt, lo * es, (hi + 1) * es)


class Op:
    __slots__ = ('eng', 'fn', 'seq', 'waits', 'signal', 'dma', 'snap', 'sigcount')

    def __init__(self, eng, fn):
        self.eng = eng
        self.fn = fn
        self.waits = []
        self.signal = False
        self.dma = None
        self.snap = None


class Prog:
    ENGS = ('pe', 'act', 'dve', 'pool', 'sp')

    def __init__(self, nc, n_dma_sems=32):
        self.nc = nc
        self.ops = {e: [] for e in self.ENGS}
        self.track = {}
        self.seen = {e: {f: -1 for f in self.ENGS} for e in self.ENGS}
        self.seen_dma = {e: {} for e in self.ENGS}
        self.n_dma_sems = n_dma_sems
        self.dma_sem_val = [0] * n_dma_sems
        h = n_dma_sems // 2
        self.dma_pool = {'sp': list(range(0, h)), 'act': list(range(0, h)), 'pool': list(range(h, n_dma_sems))}
        self.dma_rr = {'sp': 0, 'act': 0, 'pool': 0}
        self.nops = 0

    def _deps(self, regs_r, regs_w, eng=None):
        deps = []
        for (key, p0, p1, b0, b1) in regs_r:
            lst = self.track.get(key)
            if lst:
                psum = key.startswith('ps')
                for ent in lst:
                    if ent[5] and ent[0] < p1 and p0 < ent[1] and ent[2] < b1 and b0 < ent[3]:
                        deps.append(ent[4])
                    elif psum and (not ent[5]) and ent[4].eng != eng:
                        deps.append(ent[4])
        for (key, p0, p1, b0, b1) in regs_w:
            lst = self.track.get(key)
            if lst:
                for ent in lst:
                    if ent[0] < p1 and p0 < ent[1] and ent[2] < b1 and b0 < ent[3]:
                        deps.append(ent[4])
        return deps

    def _record(self, op, regs_r, regs_w):
        for (key, p0, p1, b0, b1) in regs_w:
            lst = self.track.setdefault(key, [])
            lst[:] = [e for e in lst if not (p0 <= e[0] and e[1] <= p1 and b0 <= e[2] and e[3] <= b1)]
            lst.append([p0, p1, b0, b1, op, True])
        for (key, p0, p1, b0, b1) in regs_r:
            lst = self.track.setdefault(key, [])
            lst[:] = [e for e in lst if not ((not e[5]) and e[4].eng == op.eng and e[4].dma is None
                                             and op.dma is None
                                             and p0 <= e[0] and e[1] <= p1 and b0 <= e[2] and e[3] <= b1)]
            lst.append([p0, p1, b0, b1, op, False])

    def add(self, eng, fn, reads=(), writes=(), dma=False):
        op = Op(eng, fn)
        op.seq = len(self.ops[eng])
        regs_r = [region(a) for a in reads]
        regs_w = [region(a) for a in writes]
        deps = self._deps(regs_r, regs_w, eng)
        seen = self.seen[eng]
        seen_d = self.seen_dma[eng]
        need_e = {}
        need_d = {}
        for d in deps:
            if d.dma is not None:
                s, v = d.dma
                if seen_d.get(s, 0) < v and need_d.get(s, 0) < v:
                    need_d[s] = v
            else:
                if d.eng == 'pe' and eng == 'pe':
                    continue
                if d.seq > seen[d.eng]:
                    if d.eng not in need_e or need_e[d.eng].seq < d.seq:
                        need_e[d.eng] = d
        if dma:
            pool = self.dma_pool[eng]
            s = pool[self.dma_rr[eng] % len(pool)]
            self.dma_rr[eng] += 1
            prev = self.dma_sem_val[s]
            if prev > 0 and seen_d.get(s, 0) < prev and need_d.get(s, 0) < prev:
                need_d[s] = prev
            self.dma_sem_val[s] = prev + 16
            op.dma = (s, prev + 16)
        for f, d in need_e.items():
            d.signal = True
            op.waits.append(('e', f, d))
            seen[f] = max(seen[f], d.seq)
            if d.snap is not None:
                se, sd = d.snap
                for g, v in se.items():
                    if v > seen[g]:
                        seen[g] = v
                for g, v in sd.items():
                    if v > seen_d.get(g, 0):
                        seen_d[g] = v
        for s, v in need_d.items():
            op.waits.append(('d', s, v))
            seen_d[s] = max(seen_d.get(s, 0), v)
        if not dma:
            op.snap = (dict(seen), dict(seen_d))
        self.ops[eng].append(op)
        self._record(op, regs_r, regs_w)
        self.nops += 1
        return op

    def wait_all_dma(self, eng='sp'):
        op = Op(eng, None)
        op.seq = len(self.ops[eng])
        for s in range(self.n_dma_sems):
            if self.dma_sem_val[s] > 0:
                op.waits.append(('d', s, self.dma_sem_val[s]))
        self.ops[eng].append(op)

    def emit(self, stack):
        nc = self.nc
        esem = {e: stack.enter_context(nc.semaphore("s_" + e)) for e in self.ENGS}
        dsem = [stack.enter_context(nc.semaphore("d_%d" % i)) for i in range(self.n_dma_sems)]
        for e in self.ENGS:
            c = 0
            for op in self.ops[e]:
                if op.signal:
                    c += 1
                op.sigcount = c
        block = stack.enter_context(nc.Block())
        engmap = {'pe': block.tensor, 'act': block.scalar, 'dve': block.vector,
                  'pool': block.gpsimd, 'sp': block.sync}

        def make(e):
            ops = self.ops[e]

            def body(eng):
                for op in ops:
                    for w in op.waits:
                        if w[0] == 'e':
                            eng.wait_ge(esem[w[1]], w[2].sigcount)
                        else:
                            eng.wait_ge(dsem[w[1]], w[2])
                    if op.fn is None:
                        continue
                    ins = op.fn(eng)
                    if op.dma is not None:
                        ins.then_inc(dsem[op.dma[0]], 16)
                    elif op.signal:
                        ins.then_inc(esem[e], 1)
            return body

        for e in self.ENGS:
            if self.ops[e]:
                engmap[e](make(e))


R0 = 0
R1 = 32768
R2 = 65536
R3 = 98304
R4 = 122880
O_QT = [R0 + 0, R0 + 4096]
O_KT = [R0 + 8192, R0 + 12288]
O_VA = R0 + 16384
O_QN = R0 + 24576
O_CQB = R2
O_CKVB = R2 + 12288
O_WA = R2 + 20480
O_WQ = R2 + 20480
O_PPT = R3
O_VSH = R2
O_WKV = R3
O_XSQ = R3
O_KROPE = R3 + 8192
O_QR = R3 + 10240
O_GAM = R3 + 14336
O_DEL = R3 + 15360
O_DELN = R3 + 16384
O_PT = [R3 + 17408, R3 + 18432, R3 + 19456]
O_SQT = O_PT
O_KRRAW = R3 + 20480
O_REC = R3 + 22528
O_QS = [R3 + 8192, R3 + 9216]
O_SGC = [R3 + 10240, R3 + 11264]
O_OG = [R3 + 12288, R3 + 13312]
O_SP = [R3 + 14336, R3 + 15360]
O_SPACC = [R3 + 16384, R3 + 17408]
O_A = [R3 + 18432, R3 + 19456]
O_E = [R3 + 20480, R3 + 22528]
O_RING = [R4, R4 + 2048, R4 + 4096]
O_TMP = [R4 + 6144, R4 + 8192]
O_XIN = R4 + 6144
O_RB = [R4 + 10240, R4 + 10240]
O_CONSTB = R4 + 12288
O_IDENTF = O_CONSTB + 1792
O_COS = O_IDENTF + 512
O_SIN = O_COS + 1024
O_GQ2 = O_SIN + 1024
O_GKN2 = O_GQ2 + 768
O_GKR = O_GKN2 + 512
O_STAT = O_GKR + 128
O_RECH = O_STAT + 384
O_RECL = O_RECH + 1024
O_GCOL = O_RECL + 1024
O_PST = O_GCOL + 64
O_POSF = O_PST + 512
O_INV = O_POSF + 128
O_ANG = O_INV + 64
ARENA_BYTES = O_ANG + 3072
N_CONSTB = 7
C_IDENT, C_NEGU, C_NEGONES, C_ONES, C_MASKM, C_MASKS, C_BSEL = range(7)


def build_program(n_layers=4, dbg=None, stage=99, layers=None):
    nc = bass.Bass("TRN2", target_bir_lowering=False)
    dt_in = lambda n, s, d=F32: nc.dram_tensor(n, s, d, kind="ExternalInput").ap()
    x_d = dt_in("x", [S, D])
    p_d = dt_in("p", [4, S, 256])
    pos_d = dt_in("pos", [128, NT], I32)
    cb_d = dt_in("constb", [128, N_CONSTB * 128])
    cf_d = dt_in("constf", [128, 128 + 16])
    mla_g_d = dt_in("mla_gcol", [2, 128, 13])
    mla_wa_d = dt_in("mla_wa", [2, 128, 8, 672])
    mla_wg_d = dt_in("mla_wgate", [2, 8, 128, 8, 128])
    mla_wq_d = dt_in("mla_wq", [2, 128, 3, 1536])
    mla_wkv_d = dt_in("mla_wkv", [2, 128, 2, 2048])
    mla_hg_d = dt_in("mla_hg", [2, 128, 352])
    mla_wo_d = dt_in("mla_wo", [2, 8, 128, 8, 128])
    kv_g_d = dt_in("kv_gcol", [128, 8])
    kv_wk_d = dt_in("kv_wk", [8, 128, 8, 128])
    kv_wv_d = dt_in("kv_wv", [8, 128, 8, 128])
    sb_g_d = dt_in("sb_gcol", [2, 128, 8])
    sb_wq_d = dt_in("sb_wq", [2, 8, 128, 8, 128])
    sb_wgt_d = dt_in("sb_wgate", [2, 8, 128, 8, 128])
    sb_wo_d = dt_in("sb_wo", [2, 8, 128, 1024])
    ple_wg_d = dt_in("ple_wg", [4, 8, 128, 8, 128])
    ple_wp_d = dt_in("ple_wp", [4, 8, 128, 2, 128])
    y_d = nc.dram_tensor("y", [S, D], F32, kind="ExternalOutput").ap()
    dbg_d = None
    if dbg is not None:
        dbg_d = nc.dram_tensor("dbg", list(dbg[1]), F32, kind="ExternalOutput").ap()

    st = ExitStack()
    XTt = st.enter_context(nc.sbuf_tensor("XT", [128, 8 * S], F32))
    AR = st.enter_context(nc.sbuf_tensor("AR", [128, ARENA_BYTES // 2], BF16))
    PS = [st.enter_context(nc.psum_tensor("ps%d" % i, [128, 512], F32)) for i in range(8)]
    PSB = [t.bitcast(BF16) for t in PS]
    P = Prog(nc)

    def av(off, n, dt=BF16):
        v = AR[:, off // 2: off // 2 + (n * esize(dt)) // 2]
        return v if dt == BF16 else v.bitcast(dt)

    XT = XTt[:].rearrange("p (k t) -> p k t", k=8)

    def mm(out, lhsT, rhs, start=True, stop=True, skip=False):
        kw = dict(start=start, stop=stop)
        if skip:
            kw['skip_group_check'] = True
        P.add('pe', lambda q: q.matmul(out, lhsT=lhsT, rhs=rhs, **kw),
              reads=[lhsT, rhs] + ([] if start else [out]), writes=[out])

    def tr(out, in_, ident):
        P.add('pe', lambda q: q.transpose(out, in_, ident), reads=[in_, ident], writes=[out])

    def act(out, in_, func, scale=1.0, bias=0.0, eng='act'):
        rd = [in_]
        if not isinstance(scale, (int, float)):
            rd.append(scale)
        if not isinstance(bias, (int, float)):
            rd.append(bias)
        P.add('act', lambda q: q.activation(out=out, in_=in_, func=func, scale=scale, bias=bias),
              reads=rd, writes=[out])

    def tt(out, in0, in1, op, eng='dve'):
        P.add(eng, lambda q: q.tensor_tensor(out=out, in0=in0, in1=in1, op=op), reads=[in0, in1], writes=[out])

    def ts(out, in0, s1, op0, s2=None, op1=None, eng='dve'):
        rd = [in0]
        if not isinstance(s1, (int, float)):
            rd.append(s1)
        if s2 is not None and not isinstance(s2, (int, float)):
            rd.append(s2)
        if op1 is None:
            P.add(eng, lambda q: q.tensor_scalar(out=out, in0=in0, scalar1=s1, scalar2=None, op0=op0),
                  reads=rd, writes=[out])
        else:
            P.add(eng, lambda q: q.tensor_scalar(out=out, in0=in0, scalar1=s1, scalar2=s2, op0=op0, op1=op1),
                  reads=rd, writes=[out])

    def stt(out, in0, scalar, in1, op0, op1):
        rd = [in0, in1]
        if not isinstance(scalar, (int, float)):
            rd.append(scalar)
        P.add('dve', lambda q: q.scalar_tensor_tensor(out=out, in0=in0, scalar=scalar, in1=in1, op0=op0, op1=op1),
              reads=rd, writes=[out])

    def cp(out, in_, eng='dve'):
        if eng == 'act':
            P.add('act', lambda q: q.copy(out=out, in_=in_), reads=[in_], writes=[out])
        else:
            P.add(eng, lambda q: q.tensor_copy(out=out, in_=in_), reads=[in_], writes=[out])

    def red(out, in_, eng='dve'):
        P.add(eng, lambda q: q.tensor_reduce(out=out, in_=in_, op=ALU.add, axis=AX.X), reads=[in_], writes=[out])

    def recip(out, in_):
        P.add('dve', lambda q: q.reciprocal(out=out, in_=in_), reads=[in_], writes=[out])

    def memset(ap, val, eng='dve'):
        P.add(eng, lambda q: q.memset(ap, val), reads=[], writes=[ap])

    def dma(out, in_, eng='sp'):
        P.add(eng, lambda q: q.dma_start(out=out, in_=in_), reads=[in_], writes=[out], dma=True)

    def rsqrt_to(out, in_, scale, tmp):
        act(tmp, in_, AF.Ln, scale=scale, bias=EPSC)
        act(out, tmp, AF.Exp, scale=-0.5)

    CB = av(O_CONSTB, N_CONSTB * 128).rearrange("p (c n) -> p c n", c=N_CONSTB)
    dma(av(O_CONSTB, N_CONSTB * 128), cb_d, eng='pool')
    IDENT = CB[:, C_IDENT, :]
    NEGU = CB[:, C_NEGU, :]
    NEGONES = CB[:, C_NEGONES, :]
    ONES = CB[:, C_ONES, :]
    MASKM = CB[:, C_MASKM, :]
    MASKS = CB[:, C_MASKS, :]
    BSEL = CB[:, C_BSEL, :]
    IDENTF = av(O_IDENTF, 128, F32)
    dma(IDENTF, cf_d[:, 0:128])
    INV = av(O_INV, 16, F32)
    dma(INV, cf_d[:, 128:144])
    STAT = av(O_STAT, 96, F32)
    EPSC = STAT[:, 95:96]
    memset(EPSC, EPS)
    GCOL = av(O_GCOL, 16, F32)
    psrot = [0]

    def bank(i=None):
        if i is None:
            psrot[0] = (psrot[0] + 1) % 2
            return 6 + psrot[0]
        return i

    psrot4 = [0]

    def bank4():
        psrot4[0] = (psrot4[0] + 1) % 4
        return 4 + psrot4[0]

    TMP = [av(o, 512, F32) for o in O_TMP]
    RBt = [av(o, 512, F32) for o in O_RB]
    RING = [av(o, 1024).rearrange("p (k n) -> p k n", k=8) for o in O_RING]
    ring_i = [0]

    def ring_load(src, shape=None):
        i = ring_i[0] % 3
        ring_i[0] += 1
        k, n = src.shape[1], src.shape[2]
        dst = av(O_RING[i], k * n).rearrange("p (k n) -> p k n", k=k)
        dma(dst, src, eng='pool')
        return dst

    XB = av(R0, 8 * S).rearrange("p (k t) -> p k t", k=8)
    XSQ = av(O_XSQ, 8 * 512).rearrange("p (k t) -> p k t", k=8)

    XIN = av(O_XIN, 1024, F32)
    for t in range(NT):
        dma(XIN, x_d[t * 128:(t + 1) * 128, :])
        for half in range(2):
            b = 4 + half
            for q4 in range(4):
                kc = half * 4 + q4
                tr(PS[b][:, q4 * 128:(q4 + 1) * 128], XIN[:, kc * 128:(kc + 1) * 128], IDENTF)
            dst = XT[:, half * 4:(half + 1) * 4, t * 128:(t + 1) * 128]
            src = PS[b][:, :].rearrange("p (k t) -> p k t", k=4)
            if half == 0:
                cp(dst, src, eng='dve')
            else:
                cp(dst, src, eng='act')

    COS = av(O_COS, 256, F32).rearrange("p (t i) -> p t i", t=NT)
    SIN = av(O_SIN, 256, F32).rearrange("p (t i) -> p t i", t=NT)
    if True:
        POSI = av(O_POSF + 64, 16, I32)
        POSF = av(O_POSF, 16, F32)
        dma(POSI, pos_d)
        cp(POSF, POSI)
        ANG = av(O_ANG, 256, F32)
        A2 = av(O_ANG + 1024, 256, F32)
        A3 = av(O_ANG + 2048, 256, F32)
        KI = av(O_ANG + 2048, 256, I32)
        ANG3 = ANG.rearrange("p (t i) -> p t i", t=NT)
        tt(ANG3, POSF.unsqueeze(2).broadcast_to([128, NT, 16]), INV.unsqueeze(1).broadcast_to([128, NT, 16]), ALU.mult)
        TWO_PI = 2.0 * math.pi

        def reduce_to_pi(dst, src):
            ts(A2, src, 1.0 / TWO_PI, ALU.mult)
            cp(KI, A2)
            cp(A2, KI)
            stt(dst, A2, -TWO_PI, src, ALU.mult, ALU.add)
            ts(A2, dst, math.pi, ALU.is_gt, -TWO_PI, ALU.mult)
            tt(dst, dst, A2, ALU.add)
            ts(A2, dst, -math.pi, ALU.is_lt, TWO_PI, ALU.mult)
            tt(dst, dst, A2, ALU.add)

        SINr = av(O_SIN, 256, F32)
        COSr = av(O_COS, 256, F32)
        reduce_to_pi(SINr, ANG)
        ts(ANG, ANG, math.pi / 2.0, ALU.add)
        reduce_to_pi(COSr, ANG)
        act(SINr, SINr, AF.Sin)
        act(COSr, COSr, AF.Sin)

    def prep_norm(gcols):
        for j in range(NJ):
            cols = slice(j * 512, (j + 1) * 512)
            for kc in range(8):
                act(XSQ[:, kc, :], XT[:, kc, cols], AF.Square)
            b = bank()
            for kc in range(8):
                mm(PS[b][:, :], ONES, XSQ[:, kc, :], start=(kc == 0), stop=(kc == 7))
            rb = RBt[j % 2]
            rsqrt_to(rb, PS[b][:, :], 1.0 / D, TMP[j % 2])
            for kc in range(8):
                stt(XB[:, kc, cols], XT[:, kc, cols], gcols[:, kc:kc + 1], rb, ALU.mult, ALU.mult)

    PPT = av(O_PPT, 2 * S).rearrange("p (k t) -> p k t", k=2)

    def ple(i):
        PST = [av(O_PST, 256), av(O_PST, 256)]
        for t in range(NT):
            pst = PST[t % 2]
            dma(pst, p_d[i, t * 128:(t + 1) * 128, :], eng='pool')
            b = bank()
            for kc in range(2):
                tr(PSB[b][:, kc * 128:(kc + 1) * 128], pst[:, kc * 128:(kc + 1) * 128], IDENT)
            cp(PPT[:, :, t * 128:(t + 1) * 128], PSB[b][:, 0:256].rearrange("p (k t) -> p k t", k=2),
               eng='act' if t % 2 else 'dve')
        for n in range(8):
            wg = ring_load(ple_wg_d[i, n])
            wp = ring_load(ple_wp_d[i, n])
            for j in range(NJ):
                cols = slice(j * 512, (j + 1) * 512)
                bg = bank4()
                for kc in range(8):
                    mm(PS[bg][:, :], wg[:, kc, :], XB[:, kc, cols], start=(kc == 0), stop=(kc == 7))
                tmp = TMP[j % 2]
                act(tmp, PS[bg][:, :], AF.Sigmoid)
                bp = bank4()
                for kc in range(2):
                    mm(PS[bp][:, :], wp[:, kc, :], PPT[:, kc, cols], start=(kc == 0), stop=(kc == 1))
                tt(tmp, tmp, PS[bp][:, :], ALU.mult)
                tt(XT[:, n, cols], XT[:, n, cols], tmp, ALU.add)

    SG = av(R1, 8 * S).rearrange("p (k t) -> p k t", k=8)
    CQB = av(O_CQB, 3 * S).rearrange("p (k t) -> p k t", k=3)
    CKVB = av(O_CKVB, 2 * S).rearrange("p (k t) -> p k t", k=2)
    KRRAW = av(O_KRRAW, NT * 32, F32).rearrange("p (t i) -> p t i", t=NT)
    KROPE = av(O_KROPE, NT * 32, F32).rearrange("p (t i) -> p t i", t=NT)
    GAM = av(O_GAM, 256, F32)
    DEL = av(O_DEL, 256, F32)
    DELN = av(O_DELN, 256, F32)
    GQ2 = av(O_GQ2, 192, F32)
    GKN2 = av(O_GKN2, 128, F32)
    GKR = av(O_GKR, 32, F32)
    BQ = STAT[:, 0:16]
    BKV = STAT[:, 16:32]
    SSKR = STAT[:, 32:48]
    ST1 = STAT[:, 48:64]
    ST2 = STAT[:, 64:80]

    def mla_layer(li):
        dma(GCOL[:, 0:13], mla_g_d[li])
        if stage == 1:
            return
        prep_norm(GCOL[:, 0:8])
        if stage == 1.5:
            return
        dma(av(O_GQ2, 352, F32), mla_hg_d[li])
        ts(GQ2, GQ2, 96.0 ** -0.5, ALU.mult)
        if stage < 2:
            return
        WA = av(O_WA, 8 * 672).rearrange("p (k n) -> p k n", k=8)
        for kc in range(8):
            dma(WA[:, kc, :], mla_wa_d[li, :, kc, :], eng='pool')
        SQT = [av(o, 512) for o in O_SQT]
        for j in range(NJ):
            cols = slice(j * 512, (j + 1) * 512)
            for (nblk, c0, dstT, gofs, statcol) in ((3, 0, CQB, 8, 0), (2, 384, CKVB, 11, 1)):
                for cb in range(nblk):
                    b = bank()
                    for kc in range(8):
                        mm(PS[b][:, :], WA[:, kc, c0 + cb * 128:c0 + (cb + 1) * 128], XB[:, kc, cols],
                           start=(kc == 0), stop=(kc == 7))
                    ts(dstT[:, cb, cols], PS[b][:, :], GCOL[:, gofs + cb:gofs + cb + 1], ALU.mult)
                    if stage >= 2.2:
                        act(SQT[cb], PS[b][:, :], AF.Square)
                for t4 in range(4):
                    if stage < 2.3:
                        break
                    t = j * 4 + t4
                    for cb in range(nblk):
                        mm(PS[4][:, statcol * 16 + t:statcol * 16 + t + 1], SQT[cb][:, t4 * 128:(t4 + 1) * 128],
                           ONES[:, 0:1], start=(cb == 0), stop=(cb == nblk - 1))
            for t4 in range(4):
                if stage < 2.4:
                    break
                t = j * 4 + t4
                for kc in range(8):
                    mm(PS[5][:, t * 32:(t + 1) * 32], XB[:, kc, t * 128:(t + 1) * 128], WA[:, kc, 640:672],
                       start=(kc == 0), stop=(kc == 7))
        if stage < 2.5:
            return
        cp(KRRAW, PS[5][:, :].rearrange("p (t i) -> p t i", t=NT))
        if stage < 2.6:
            return
        rsqrt_to(BQ, PS[4][:, 0:16], 1.0 / 384, ST1)
        rsqrt_to(BKV, PS[4][:, 16:32], 1.0 / 256, ST2)
        if stage < 3:
            return
        WQ = av(O_WQ, 3 * 1536).rearrange("p (k n) -> p k n", k=3)
        WKV = av(O_WKV, 2 * 2048).rearrange("p (k n) -> p k n", k=2)
        for kc in range(3):
            dma(WQ[:, kc, :], mla_wq_d[li, :, kc, :], eng='pool')
        for kc in range(2):
            dma(WKV[:, kc, :], mla_wkv_d[li, :, kc, :], eng='pool')
        GAM3 = GAM.rearrange("p (t h) -> p t h", t=NT)
        DEL3 = DEL.rearrange("p (t h) -> p t h", t=NT)
        DELN3 = DELN.rearrange("p (t h) -> p t h", t=NT)
        b0_items = []
        b0n = [0]

        def mk_q(t, qb):
            def f():
                tcols = slice(t * 128, (t + 1) * 128)
                b = b0n[0] % 4
                b0n[0] += 1
                for kc in range(3):
                    mm(PS[b][:, 0:384], CQB[:, kc, tcols], WQ[:, kc, qb * 384:(qb + 1) * 384],
                       start=(kc == 0), stop=(kc == 2))
                tmp = TMP[b % 2]
                act(tmp[:, 0:384], PS[b][:, 0:384], AF.Square)
                red(GAM3[:, t, qb * 4:(qb + 1) * 4], tmp[:, 0:384].rearrange("p (h d) -> p h d", h=4))
            return f

        def mk_k(t, kb4):
            def f():
                tcols = slice(t * 128, (t + 1) * 128)
                b = b0n[0] % 4
                b0n[0] += 1
                for kc in range(2):
                    mm(PS[b][:, :], CKVB[:, kc, tcols], WKV[:, kc, kb4 * 512:(kb4 + 1) * 512],
                       start=(kc == 0), stop=(kc == 1))
                tmp = TMP[b % 2]
                act(tmp[:, 0:256].rearrange("p (h d) -> p h d", h=4),
                    PS[b][:, :].rearrange("p (h d) -> p h d", h=4)[:, :, 0:64], AF.Square)
                red(DEL3[:, t, kb4 * 4:(kb4 + 1) * 4], tmp[:, 0:256].rearrange("p (h d) -> p h d", h=4))
            return f
        for t in range(NT):
            for qb in range(4):
                b0_items.append(mk_q(t, qb))
            for kb4 in range(4):
                b0_items.append(mk_k(t, kb4))
        for cb in range(8):
            w = ring_load(mla_wg_d[li, cb])
            for j in range(NJ):
                cols = slice(j * 512, (j + 1) * 512)
                b = bank4()
                for kc in range(8):
                    mm(PS[b][:, :], w[:, kc, :], XB[:, kc, cols], start=(kc == 0), stop=(kc == 7))
                act(SG[:, cb, cols], PS[b][:, :], AF.Silu)
                for _ in range(4):
                    if b0_items:
                        b0_items.pop(0)()
        while b0_items:
            b0_items.pop(0)()
        if stage < 4:
            return
        KRSQ = av(O_QR, NT * 32, F32).rearrange("p (t i) -> p t i", t=NT)
        tt(KRSQ, KRRAW, KRRAW, ALU.mult)
        red(SSKR, KRSQ)
        KRG = av(O_QR + 2048, NT * 32, F32).rearrange("p (t i) -> p t i", t=NT)
        tt(KRG, KRRAW, GKR.unsqueeze(1).broadcast_to([128, NT, 32]), ALU.mult)

        def rope(dst1, dst2, x1, x2, cosb, sinb, t1, t2):
            tt(t1, x1, cosb, ALU.mult)
            tt(t2, x2, sinb, ALU.mult)
            tt(dst1, t1, t2, ALU.subtract)
            tt(t1, x2, cosb, ALU.mult)
            tt(t2, x1, sinb, ALU.mult)
            tt(dst2, t1, t2, ALU.add)

        RT1 = av(O_ANG, 256, F32).rearrange("p (t i) -> p t i", t=NT)
        RT2 = av(O_ANG + 1024, 256, F32).rearrange("p (t i) -> p t i", t=NT)
        rope(KROPE[:, :, 0:16], KROPE[:, :, 16:32], KRG[:, :, 0:16], KRG[:, :, 16:32], COS, SIN, RT1, RT2)
        if stage < 5:
            return
        BQb = BQ.unsqueeze(2).broadcast_to([128, NT, H])
        BKVb = BKV.unsqueeze(2).broadcast_to([128, NT, H])
        T256 = av(O_ANG, 256, F32)
        T256b = av(O_ANG + 1024, 256, F32)
        T3 = T256.rearrange("p (t h) -> p t h", t=NT)
        tt(T3, GAM3, BQb, ALU.mult)
        tt(T3, T3, BQb, ALU.mult)
        rsqrt_to(GAM, T256, 1.0 / 96, T256b)
        tt(GAM3, GAM3, BQb, ALU.mult)
        tt(T3, DEL3, BKVb, ALU.mult)
        tt(T3, T3, BKVb, ALU.mult)
        tt(T3, T3, SSKR.unsqueeze(2).broadcast_to([128, NT, H]), ALU.add)
        rsqrt_to(DEL, T256, 1.0 / 96, T256b)
        tt(DELN3, DEL3, BKVb, ALU.mult)
        if stage < 6:
            return
        QT = [av(o, S) for o in O_QT]
        KT = [av(o, S) for o in O_KT]
        VA = av(O_VA, NT * 256).rearrange("p (t c) -> p t c", t=NT)
        QN = av(O_QN, NT * 192).rearrange("p (t c) -> p t c", t=NT)
        QR = av(O_QR, NT * 64, F32).rearrange("p (t h i) -> p t h i", t=NT, h=2)
        memset(VA[:, :, 64:192], 0.0)
        memset(VA[:, :, 64:65], 1.0)
        memset(VA[:, :, 128:129], 1.0)
        PT = [av(o, 512) for o in O_PT]
        REC = av(O_REC, 512, F32)
        RECH = av(O_RECH, 512)
        RECL = av(O_RECL, 512)
        pti = [0]
        COSb = COS.unsqueeze(2).broadcast_to([128, NT, 2, 16])
        SINb = SIN.unsqueeze(2).broadcast_to([128, NT, 2, 16])
        for c in range(8):
            QN4 = QN.rearrange("p t (h d) -> p t h d", h=2)
            for t in range(NT):
                tcols = slice(t * 128, (t + 1) * 128)
                b = bank()
                for kc in range(3):
                    mm(PS[b][:, 0:192], CQB[:, kc, tcols], WQ[:, kc, c * 192:(c + 1) * 192],
                       start=(kc == 0), stop=(kc == 2))
                tmp = TMP[t % 2]
                t3 = tmp[:, 0:192].rearrange("p (h d) -> p h d", h=2)
                tt(t3, PS[b][:, 0:192].rearrange("p (h d) -> p h d", h=2),
                   GAM3[:, t, 2 * c:2 * c + 2].unsqueeze(2).broadcast_to([128, 2, 96]), ALU.mult)
                g3 = GQ2.rearrange("p (h d) -> p h d", h=2)
                tt(QN4[:, t, :, 0:64], t3[:, :, 0:64], g3[:, :, 0:64], ALU.mult)
                tt(QR[:, t, :, :], t3[:, :, 64:96], g3[:, :, 64:96], ALU.mult)
            RA1 = av(O_ANG, 512, F32).rearrange("p (t h i) -> p t h i", t=NT, h=2)
            RA2 = av(O_KRRAW, 512, F32).rearrange("p (t h i) -> p t h i", t=NT, h=2)
            rope(QN4[:, :, :, 64:80], QN4[:, :, :, 80:96], QR[:, :, :, 0:16], QR[:, :, :, 16:32], COSb, SINb, RA1, RA2)
            for hh in range(2):
                for half in range(2):
                    b = bank()
                    for t8 in range(8):
                        t = half * 8 + t8
                        tr(PSB[b][0:96, t8 * 128:(t8 + 1) * 128], QN[:, t, hh * 96:(hh + 1) * 96], IDENT)
                    cp(QT[hh][0:96, half * 1024:(half + 1) * 1024], PSB[b][0:96, :], eng='act' if half else 'dve')
            for t in range(NT):
                tcols = slice(t * 128, (t + 1) * 128)
                b = bank()
                for kc in range(2):
                    mm(PS[b][:, 0:256], CKVB[:, kc, tcols], WKV[:, kc, c * 256:(c + 1) * 256],
                       start=(kc == 0), stop=(kc == 1))
                p3 = PS[b][:, 0:256].rearrange("p (h d) -> p h d", h=2)
                tmp = TMP[t % 2]
                t3 = tmp[:, 0:128].rearrange("p (h d) -> p h d", h=2)
                tt(t3, p3[:, :, 0:64], DELN3[:, t, 2 * c:2 * c + 2].unsqueeze(2).broadcast_to([128, 2, 64]), ALU.mult)
                tt(QN4[:, t, :, 0:64], t3, GKN2.rearrange("p (h d) -> p h d", h=2), ALU.mult)
                vdst = VA[:, t, :].rearrange("p (a c) -> p a c", a=4)[:, 0:4:3, :]
                act(vdst, p3[:, :, 64:128], AF.Copy, scale=BKV[:, t:t + 1])
            tt(QN4[:, :, :, 64:96], KROPE.unsqueeze(2).broadcast_to([128, NT, 2, 32]),
               DEL3[:, :, 2 * c:2 * c + 2].unsqueeze(3).broadcast_to([128, NT, 2, 32]), ALU.mult)
            for hh in range(2):
                for half in range(2):
                    b = bank()
                    for t8 in range(8):
                        t = half * 8 + t8
                        tr(PSB[b][0:96, t8 * 128:(t8 + 1) * 128], QN[:, t, hh * 96:(hh + 1) * 96], IDENT)
                    cp(KT[hh][0:96, half * 1024:(half + 1) * 1024], PSB[b][0:96, :], eng='act' if half else 'dve')
            PT4 = [[PT[0], PT[1]], [PT[2], av(O_RB[0], 512)]]
            for j in range(NJ):
                nkb = 4 * j + 4
                cols = slice(j * 512, (j + 1) * 512)

                def s_mm(hh, kb):
                    r = kb - 4 * j
                    c0 = max(0, r) * 128
                    zb = 2 * hh + kb % 2
                    qs = slice(j * 512 + c0, (j + 1) * 512)
                    mm(PS[zb][:, c0:512], KT[hh][0:96, kb * 128:(kb + 1) * 128], QT[hh][0:96, qs],
                       start=True, stop=True)
                    if r >= 0:
                        mm(PS[zb][:, c0:c0 + 128], IDENT, MASKM, start=False, stop=True, skip=True)

                for hh in range(2):
                    s_mm(hh, 0)
                for kb in range(nkb):
                    r = kb - 4 * j
                    c0 = max(0, r) * 128
                    if kb + 1 < nkb:
                        for hh in range(2):
                            s_mm(hh, kb + 1)
                    for hh in range(2):
                        act(PT4[hh][kb % 2][:, c0:512], PS[2 * hh + kb % 2][:, c0:512], AF.Exp)
                    for hh in range(2):
                        pt = PT4[hh][kb % 2]
                        if hh == 0:
                            mm(PS[4][0:65, c0:512], VA[:, kb, 0:65], pt[:, c0:512],
                               start=(kb == 0), stop=(kb == nkb - 1), skip=True)
                        else:
                            mm(PS[5][:, c0:512], VA[:, kb, 128:256], pt[:, c0:512],
                               start=(kb == 0), stop=(kb == nkb - 1), skip=True)
                for hh in range(2):
                    ob = 4 + hh
                    if hh == 0:
                        drow, rows = 64, slice(0, 64)
                        lsel = BSEL[64:65, 0:64]
                    else:
                        drow, rows = 0, slice(64, 128)
                        lsel = BSEL[0:1, :]
                    rr = slice(drow, drow + 1)
                    recip(REC[rr, :], PS[ob][rr, :])
                    cp(RECH[rr, :], REC[rr, :])
                    tt(RECL[rr, :], REC[rr, :], RECH[rr, :], ALU.subtract)
                    bb = bank()
                    orow = slice(0, 64) if hh == 0 else slice(0, 128)
                    mm(PS[bb][orow, :], lsel, RECH[rr, :], start=True, stop=False)
                    mm(PS[bb][orow, :], lsel, RECL[rr, :], start=False, stop=True)
                    tmp = TMP[hh]
                    tt(tmp[rows, :], PS[bb][rows, :], SG[rows, c, cols], ALU.mult)
                    tt(SG[rows, c, cols], PS[ob][rows, :], tmp[rows, :], ALU.mult)
        for n in range(8):
            w = ring_load(mla_wo_d[li, n])
            for j in range(NJ):
                cols = slice(j * 512, (j + 1) * 512)
                b = bank4()
                for kc in range(8):
                    mm(PS[b][:, :], w[:, kc, :], SG[:, kc, cols], start=(kc == 0), stop=(kc == 7))
                tt(XT[:, n, cols], XT[:, n, cols], PS[b][:, :], ALU.add)
                cp(XB[:, n, cols], XT[:, n, cols], eng='act')

    KSH = av(R1, 8 * S).rearrange("p (k t) -> p k t", k=8)
    VSH = av(O_VSH, NT * 1024).rearrange("p (t c) -> p t c", t=NT)

    def shared_kv():
        dma(GCOL[:, 0:8], kv_g_d)
        prep_norm(GCOL[:, 0:8])
        for n in range(8):
            w = ring_load(kv_wk_d[n])
            for j in range(NJ):
                cols = slice(j * 512, (j + 1) * 512)
                b = bank4()
                for kc in range(8):
                    mm(PS[b][:, :], w[:, kc, :], XB[:, kc, cols], start=(kc == 0), stop=(kc == 7))
                cp(KSH[:, n, cols], PS[b][:, :], eng='act' if j % 2 else 'dve')
        for n in range(8):
            w = ring_load(kv_wv_d[n])
            for t in range(NT):
                b = bank4()
                for kc in range(8):
                    mm(PS[b][:, 0:128], XB[:, kc, t * 128:(t + 1) * 128], w[:, kc, :], start=(kc == 0), stop=(kc == 7))
                cp(VSH[:, t, n * 128:(n + 1) * 128], PS[b][:, 0:128], eng='act' if t % 2 else 'dve')

    def sb_layer(lj):
        from collections import deque
        dma(GCOL[:, 0:8], sb_g_d[lj])
        prep_norm(GCOL[:, 0:8])
        QSZ = [av(R3 + k * 2048, 1024) for k in range(4)]
        SGC = [av(R3 + 8192 + k * 1024, 512) for k in range(4)]
        OG = [av(O_RECH, 512), av(O_RECL, 512), av(O_COS, 512), av(O_SIN, 512)]
        for j in range(4):
            memset(QSZ[j][64:128, 0:512], 0.0)
            memset(QSZ[j][0:64, 512:1024], 0.0)
        SP = [av(R3 + 12288 + k * 1024, 512) for k in range(4)]
        SPACC = [av(R3 + 16384 + k * 1024, 512) for k in range(4)]
        AT = [av(R3 + 20480 + k * 1024, 512) for k in range(4)]
        E = [av(O_TMP[0], 512, F32), av(O_TMP[1], 512, F32), av(O_RB[0], 512, F32), av(O_ANG, 512, F32)]
        WQ_ = av(O_RING[0], 1024).rearrange("p (k n) -> p k n", k=8)
        WG_ = av(O_RING[1], 1024).rearrange("p (k n) -> p k n", k=8)
        WO_ = av(O_RING[2], 1024)
        fifo = deque()

        def bg(n):
            while n > 0 and fifo:
                fifo.popleft()[1]()
                n -= 1

        def drain(pred):
            while any(pred(k) for k, _ in fifo):
                fifo.popleft()[1]()

        def load_w(c):
            dma(WQ_, sb_wq_d[lj, c], eng='pool')
            dma(WG_, sb_wgt_d[lj, c], eng='pool')

        def load_wo(c):
            dma(WO_, sb_wo_d[lj, c], eng='pool')

        def proj_items(c, j):
            cols = slice(j * 512, (j + 1) * 512)
            key = ('proj', c, j)
            st_ = {}
            items = []

            def mk_mm(w, kc, which):
                def f():
                    if kc == 0:
                        st_[which] = bank()
                    mm(PS[st_[which]][:, :], w[:, kc, :], XB[:, kc, cols], start=(kc == 0), stop=(kc == 7))
                return f
            for kc in range(8):
                items.append((key, mk_mm(WQ_, kc, 'q')))

            def evq():
                b = st_['q']
                ts(QSZ[j][0:64, 0:512], PS[b][0:64, :], 0.125, ALU.mult)
                ts(QSZ[j][64:128, 512:1024], PS[b][64:128, :], 0.125, ALU.mult)
            items.append((key, evq))
            for kc in range(8):
                items.append((key, mk_mm(WG_, kc, 'g')))
            items.append((key, lambda: act(SGC[j], PS[st_['g']][:, :], AF.Silu)))
            return items

        def wo_items(c, j):
            cols = slice(j * 512, (j + 1) * 512)
            items = []

            def mk(n):
                def f():
                    b = bank()
                    mm(PS[b][:, :], WO_[:, n * 128:(n + 1) * 128], OG[j], start=True, stop=True)
                    tt(XT[:, n, cols], XT[:, n, cols], PS[b][:, :], ALU.add)
                return f
            for n in range(8):
                items.append((('wo', c, j), mk(n)))
            return items

        load_w(0)
        load_wo(0)
        for j in (3, 2, 1, 0):
            for it in proj_items(0, j):
                it[1]()
        slots = []
        for k in range(4):
            seq = (3, 0) if k // 2 == 0 else (2, 1)
            blocks = [(c, j, kb) for c in range(8) for j in seq for kb in range(4 * j + 3, -1, -1)]
            slots.append((k % 2, k, 4 + k // 2, blocks))
        for r in range(160):
            cr, rl = r // 20, r % 20
            if rl == 9 and cr + 1 < 8:
                drain(lambda k: k[0] == 'proj' and k[1] == cr)
                load_w(cr + 1)
            if rl == 9 and cr >= 1:
                drain(lambda k: k[0] == 'wo' and k[1] == cr - 1)
                load_wo(cr)
            info = []
            for (hh, zb, ob, blocks) in slots:
                c, j, kb = blocks[r]
                rr = kb - 4 * j
                first = (kb == 4 * j + 3)
                if first:
                    drain(lambda k: k[0] == 'proj' and k[1] == c and k[2] == j)
                info.append((hh, zb, ob, j, kb, rr, max(0, rr) * 128, first, slice(hh * 64, hh * 64 + 64), c))
            for k, (hh, zb, ob, j, kb, rr, c0, first, rows, c) in enumerate(info):
                mm(PS[zb][:, c0:512], KSH[:, c, kb * 128:(kb + 1) * 128],
                   QSZ[j][:, hh * 512 + c0:hh * 512 + 512], start=True, stop=True)
                if rr >= 0:
                    mm(PS[zb][:, c0:c0 + 128], IDENT, MASKS, start=False, stop=True, skip=True)
            bg(3)
            for k, (hh, zb, ob, j, kb, rr, c0, first, rows, c) in enumerate(info):
                act(E[k][:, c0:512], PS[zb][:, c0:512], AF.Exp)
            for k, (hh, zb, ob, j, kb, rr, c0, first, rows, c) in enumerate(info):
                act(SP[k][:, c0:512], E[k][:, c0:512], AF.Ln, bias=ONEC)
            for k, (hh, zb, ob, j, kb, rr, c0, first, rows, c) in enumerate(info):
                mm(PS[zb][:, c0:512], NEGU, SP[k][:, c0:512], start=False, stop=True, skip=True)
                if not first:
                    a0 = c0 + 128 if rr >= 0 else 0
                    mm(PS[zb][:, a0:512], NEGONES, SPACC[k][:, a0:512], start=False, stop=True, skip=True)
            bg(3)
            for k, (hh, zb, ob, j, kb, rr, c0, first, rows, c) in enumerate(info):
                if kb > 0:
                    if first:
                        cp(SPACC[k][:, c0:512], SP[k][:, c0:512])
                    elif rr >= 0:
                        pc0 = c0 + 128
                        cp(SPACC[k][:, c0:pc0], SP[k][:, c0:pc0])
                        tt(SPACC[k][:, pc0:512], SPACC[k][:, pc0:512], SP[k][:, pc0:512], ALU.add)
                    else:
                        tt(SPACC[k][:, :], SPACC[k][:, :], SP[k][:, :], ALU.add)
            for k, (hh, zb, ob, j, kb, rr, c0, first, rows, c) in enumerate(info):
                act(AT[k][:, c0:512], PS[zb][:, c0:512], AF.Exp)
            for k, (hh, zb, ob, j, kb, rr, c0, first, rows, c) in enumerate(info):
                mm(PS[ob][rows, c0:512], VSH[:, kb, (2 * c + hh) * 64:(2 * c + hh + 1) * 64], AT[k][:, c0:512],
                   start=first, stop=(kb == 0), skip=True)
            bg(3)
            for k, (hh, zb, ob, j, kb, rr, c0, first, rows, c) in enumerate(info):
                if kb == 0 and hh == 1:
                    drain(lambda kk: kk[0] == 'wo' and kk[2] == j)
                    tt(OG[j], PS[ob][:, :], SGC[j], ALU.mult)
                    fifo.extend(wo_items(c, j))
                    if c + 1 < 8:
                        fifo.extend(proj_items(c + 1, j))
        bg(100000)
        for n in range(8):
            for j in range(NJ):
                cols = slice(j * 512, (j + 1) * 512)
                cp(XB[:, n, cols], XT[:, n, cols], eng='act')

    ONEC = STAT[:, 94:95]
    memset(ONEC, 1.0)

    for i in (layers if layers is not None else range(n_layers)):
        if stage < 1:
            break
        if i < 2:
            mla_layer(i)
        else:
            sb_layer(i - 2)
        if stage >= 8:
            ple(i)
        if i == 1 and n_layers > 2:
            shared_kv()

    if dbg is not None:
        name = dbg[0]
        if name == 'cos':
            dma(dbg_d[:, 0:256], av(O_COS, 256, F32))
            dma(dbg_d[:, 256:512], av(O_SIN, 256, F32))

    YO = av(O_XIN, 1024, F32)
    for t in range(NT):
        for half in range(2):
            b = 4 + half
            for q4 in range(4):
                kc = half * 4 + q4
                tr(PS[b][:, q4 * 128:(q4 + 1) * 128], XT[:, kc, t * 128:(t + 1) * 128], IDENTF)
            cp(YO[:, half * 512:(half + 1) * 512], PS[b][:, :], eng='act' if half else 'dve')
        dma(y_d[t * 128:(t + 1) * 128, :], YO)
    P.wait_all_dma('sp')
    P.emit(st)
    st.close()
    return nc, P


def _consts():
    j = np.arange(128)[:, None]
    s = np.arange(128)[None, :]
    cb = np.zeros((128, N_CONSTB, 128), np.float32)
    cb[:, C_IDENT] = (j == s)
    cb[:, C_NEGU] = -(j >= s).astype(np.float32)
    cb[:, C_NEGONES] = -1.0
    cb[:, C_ONES] = 1.0
    cb[:, C_MASKM] = np.where(s >= j, 0.0, NEG)
    cb[:, C_MASKS] = np.where(s > j, 0.0, NEG)
    bs = np.zeros((128, 128), np.float32)
    bs[64, 0:64] = 1.0
    bs[0, 64:128] = 1.0
    cb[:, C_BSEL] = bs
    cf = np.zeros((128, 144), np.float32)
    cf[:, 0:128] = np.eye(128, dtype=np.float32)
    half = 16
    inv = (1.0 / (np.float32(10000.0) ** (np.arange(half, dtype=np.float32) / np.float32(half)))).astype(np.float32)
    cf[:, 128:144] = inv[None, :]
    return cb.reshape(128, -1), cf


def _kt(w):
    K, N = w.shape
    return np.ascontiguousarray(w.reshape(K // 128, 128, N).transpose(1, 0, 2))


def _blk(w, nb=128):
    K, N = w.shape
    a = w.reshape(K // 128, 128, N // nb, nb).transpose(2, 1, 0, 3)
    return np.ascontiguousarray(a)


def _gcol(g):
    return np.ascontiguousarray(g.reshape(-1, 128).T)


def make_in_maps(inputs):
    f = lambda k: np.asarray(inputs[k], dtype=np.float32)
    cb, cf = _consts()
    mla_w_in = f("mla_w_in")
    shared = {
        "constb": cb, "constf": cf,
        "mla_gcol": np.stack([np.concatenate([_gcol(f("mla_ln_g")[l]), _gcol(f("mla_q_norm_g")[l]),
                                              _gcol(f("mla_kv_norm_g")[l])], axis=1) for l in range(2)]),
        "mla_wa": np.stack([_kt(mla_w_in[l][:, 0:672]) for l in range(2)]),
        "mla_wgate": np.stack([_blk(mla_w_in[l][:, 672:1696]) for l in range(2)]),
        "mla_wq": np.stack([_kt(f("mla_w_q_up")[l]) for l in range(2)]),
        "mla_wkv": np.stack([_kt(f("mla_w_kv_up")[l]) for l in range(2)]),
        "mla_hg": np.stack([np.ascontiguousarray(np.broadcast_to(np.concatenate(
            [f("mla_q_head_g")[l], f("mla_q_head_g")[l], f("mla_k_head_g")[l][:64], f("mla_k_head_g")[l][:64],
             f("mla_k_head_g")[l][64:]])[None, :], (128, 352))) for l in range(2)]),
        "mla_wo": np.stack([_blk(f("mla_w_out")[l]) for l in range(2)]),
        "kv_gcol": _gcol(f("kv_ln_g")),
        "kv_wk": _blk(f("w_kv_shared")[:, 0:1024]),
        "kv_wv": _blk(f("w_kv_shared")[:, 1024:2048]),
        "sb_gcol": np.stack([_gcol(f("sb_ln_g")[l]) for l in range(2)]),
        "sb_wq": np.stack([_blk(f("sb_w_in")[l][:, 0:1024]) for l in range(2)]),
        "sb_wgate": np.stack([_blk(f("sb_w_in")[l][:, 1024:2048]) for l in range(2)]),
        "sb_wo": np.ascontiguousarray(f("sb_w_out").reshape(2, 8, 128, 1024)),
        "ple_wg": np.stack([_blk(f("ple_w_gate")[l]) for l in range(4)]),
        "ple_wp": np.stack([_blk(f("ple_w_proj")[l]) for l in range(4)]),
    }
    x = f("x")
    p = f("p")
    pos = np.asarray(inputs["positions"]).astype(np.int32)
    maps = []
    for b in range(8):
        m = dict(shared)
        m["x"] = np.ascontiguousarray(x[b])
        m["p"] = np.ascontiguousarray(p[:, b])
        m["pos"] = np.ascontiguousarray(pos[b].reshape(NT, 128).T)
        maps.append(m)
    return maps


_CACHE = {}


def kernel(**inputs):
    maps = make_in_maps(inputs)
    if "nc" not in _CACHE:
        _CACHE["nc"] = build_program(4)[0]
    res = run_bass_kernel_spmd(_CACHE["nc"], maps, core_ids=list(range(8)))
    return np.stack([np.asarray(r["y"], dtype=np.float32) for r in res.results], axis=0)
```
